# Optimizing a Trainium2 kernel written in Bass

```python
import math
import jax, jax.numpy as jnp
from jax import lax
import numpy as np

D_MODEL = 1024
BATCH = 2
SEQ = 16384
DEPTH = 2

SB_HEADS = 4
SB_HEAD_DIM = 128
SB_WIDTH = SB_HEADS * SB_HEAD_DIM
SB_BLOCK = 128
SSM_D_INNER = 3 * D_MODEL // 2
SSM_HEAD_DIM = 64
SSM_HEADS = SSM_D_INNER // SSM_HEAD_DIM
SSM_GROUPS = 8
SSM_STATE = 128
SSM_CONV = 4
SSM_CHUNK = 128
SSM_CONV_DIM = SSM_D_INNER + 2 * SSM_GROUPS * SSM_STATE
FFN_HIDDEN = 2816
NORM_EPS = 1e-6
DT_MIN = 1e-3
DT_MAX = 1e-1
IN_WIDTH = 3 * SB_WIDTH + SSM_D_INNER + SSM_CONV_DIM + SSM_HEADS + 2 * D_MODEL
IN_SPLITS = (
    SB_WIDTH,
    2 * SB_WIDTH,
    3 * SB_WIDTH,
    3 * SB_WIDTH + SSM_D_INNER,
    3 * SB_WIDTH + SSM_D_INNER + SSM_CONV_DIM,
    3 * SB_WIDTH + SSM_D_INNER + SSM_CONV_DIM + SSM_HEADS,
    3 * SB_WIDTH + SSM_D_INNER + SSM_CONV_DIM + SSM_HEADS + D_MODEL,
)

kernel_name = "hybrid_stickbreaking_mamba2_macaron"


def rms_norm(x, gain):
    x32 = x.astype(jnp.float32)
    y = x32 * lax.rsqrt(jnp.mean(x32 * x32, axis=-1, keepdims=True) + NORM_EPS)
    return (y * gain.astype(jnp.float32)).astype(x.dtype)


def swiglu(h, w_up, w_down):
    gate, up = jnp.split(h @ w_up, 2, axis=-1)
    return (jax.nn.silu(gate) * up) @ w_down


def stick_breaking_attention(q, k, v):
    b, s, h, dh = q.shape
    nb = s // SB_BLOCK
    q32 = q.astype(jnp.float32) * (dh ** -0.5)
    k32 = k.astype(jnp.float32)
    v32 = v.astype(jnp.float32)
    suffix_in_block = jnp.tril(jnp.ones((SB_BLOCK, SB_BLOCK), jnp.float32), k=-1)
    suffix_blocks = jnp.tril(jnp.ones((nb, nb), jnp.float32), k=-1)
    outs = []
    for i in range(nb):
        lo, hi = i * SB_BLOCK, (i + 1) * SB_BLOCK
        q_blk = q32[:, lo:hi]
        k_blk = k32[:, :hi].reshape(b, i + 1, SB_BLOCK, h, dh)
        v_blk = v32[:, :hi].reshape(b, i + 1, SB_BLOCK, h, dh)
        z = jnp.einsum('bqhd,bjkhd->bhqjk', q_blk, k_blk)
        key_pos = jnp.arange(hi).reshape(i + 1, SB_BLOCK)
        q_pos = lo + jnp.arange(SB_BLOCK)
        mask = key_pos[None] < q_pos[:, None, None]
        sp = jax.nn.softplus(z)
        log_keep = jnp.where(mask, -sp, 0.0)
        within = jnp.einsum('bhqjk,kl->bhqjl', log_keep, suffix_in_block)
        cross = jnp.einsum('bhqj,jm->bhqm', log_keep.sum(-1),
                           suffix_blocks[:i + 1, :i + 1])
        w = jnp.where(mask, jnp.exp(z - sp + within + cross[..., None]), 0.0)
        outs.append(jnp.einsum('bhqjk,bjkhd->bqhd', w, v_blk))
    return jnp.concatenate(outs, axis=1).astype(q.dtype)


def causal_depthwise_conv(x, w, bias):
    k = w.shape[0]
    out = lax.conv_general_dilated(
        x, w[:, None, :], window_strides=(1,), padding=[(k - 1, 0)],
        dimension_numbers=('NWC', 'WIO', 'NWC'), feature_group_count=x.shape[-1])
    return out + bias


def ssd_chunked_scan(x, dt, a, bm, cm):
    b, s, h, p = x.shape
    g = SSM_GROUPS
    hg = h // g
    n = bm.shape[-1]
    q = SSM_CHUNK
    nc = s // q

    def chunks(t):
        return jnp.moveaxis(t.reshape((b, nc, q) + t.shape[2:]), 1, 0)

    xc = chunks(x.astype(jnp.float32).reshape(b, s, g, hg, p))
    dtc = chunks(dt.astype(jnp.float32).reshape(b, s, g, hg))
    bc = chunks(bm.astype(jnp.float32))
    cc = chunks(cm.astype(jnp.float32))
    a_g = a.astype(jnp.float32).reshape(g, hg)
    causal = jnp.tril(jnp.ones((q, q), dtype=bool))

    def step(state, inp):
        x_k, dt_k, b_k, c_k = inp
        cum = jnp.cumsum(dt_k * a_g, axis=1)
        cum_t = jnp.moveaxis(cum, 1, -1)
        decay = jnp.exp(jnp.where(causal, cum_t[..., :, None] - cum_t[..., None, :], -jnp.inf))
        xdt = x_k * dt_k[..., None]
        cb = jnp.einsum('bqgn,bkgn->bgqk', c_k, b_k)
        y_intra = jnp.einsum('bgqk,bghqk,bkghp->bqghp', cb, decay, xdt)
        y_inter = jnp.einsum('bqgn,bghpn->bqghp', c_k, state) * jnp.exp(cum)[..., None]
        to_end = jnp.exp(cum[:, -1:] - cum)
        new_state = (state * jnp.exp(cum[:, -1])[..., None, None]
                     + jnp.einsum('bkgn,bkgh,bkghp->bghpn', b_k, to_end, xdt))
        return new_state, y_intra + y_inter

    init = jnp.zeros((b, g, hg, p, n), jnp.float32)
    _, ys = lax.scan(step, init, (xc, dtc, bc, cc))
    return jnp.moveaxis(ys, 0, 1).reshape(b, s, h, p)


def mamba2_mixer(z, xbc, dt_raw, conv_w, conv_b, dt_bias, a_log, d_skip, ssm_norm):
    b, s, _ = xbc.shape
    xbc = jax.nn.silu(causal_depthwise_conv(xbc, conv_w, conv_b))
    xs, bm, cm = jnp.split(xbc, [SSM_D_INNER, SSM_D_INNER + SSM_GROUPS * SSM_STATE], axis=-1)
    xs = xs.reshape(b, s, SSM_HEADS, SSM_HEAD_DIM)
    bm = bm.reshape(b, s, SSM_GROUPS, SSM_STATE)
    cm = cm.reshape(b, s, SSM_GROUPS, SSM_STATE)
    dt = jax.nn.softplus(dt_raw.astype(jnp.float32) + dt_bias.astype(jnp.float32))
    a = -jnp.exp(a_log.astype(jnp.float32))
    y = ssd_chunked_scan(xs, dt, a, bm, cm)
    y = y + d_skip.astype(jnp.float32)[:, None] * xs.astype(jnp.float32)
    y = y.reshape(b, s, SSM_D_INNER) * jax.nn.silu(z.astype(jnp.float32))
    yg = y.reshape(b, s, SSM_GROUPS, SSM_D_INNER // SSM_GROUPS)
    yg = yg * lax.rsqrt(jnp.mean(yg * yg, axis=-1, keepdims=True) + NORM_EPS)
    return (yg.reshape(b, s, SSM_D_INNER) * ssm_norm.astype(jnp.float32)).astype(z.dtype)


def setup_inputs(seed: int = 0) -> dict:
    key = jax.random.key(seed)
    ks = jax.random.split(key, 24)

    def dense(k, shape, fan_in):
        return jax.random.normal(k, shape, jnp.float32) * (fan_in ** -0.5)

    def gain(k, width):
        return 1.0 + 0.02 * jax.random.normal(k, (DEPTH, width), jnp.float32)

    dt = jnp.exp(jax.random.uniform(ks[8], (DEPTH, SSM_HEADS), jnp.float32,
                                    minval=math.log(DT_MIN), maxval=math.log(DT_MAX)))
    dt_bias = dt + jnp.log(-jnp.expm1(-dt))
    a_log = jnp.log(jax.random.uniform(ks[9], (DEPTH, SSM_HEADS), jnp.float32, minval=1.0, maxval=16.0))

    return {
        "x": jax.random.normal(ks[0], (BATCH, SEQ, D_MODEL), jnp.float32),
        "ffn1_norm": gain(ks[1], D_MODEL),
        "ffn1_w_up": dense(ks[2], (DEPTH, D_MODEL, 2 * FFN_HIDDEN), D_MODEL),
        "ffn1_w_down": dense(ks[3], (DEPTH, FFN_HIDDEN, D_MODEL), FFN_HIDDEN),
        "mix_norm": gain(ks[4], D_MODEL),
        "w_in": dense(ks[5], (DEPTH, D_MODEL, IN_WIDTH), D_MODEL),
        "conv_w": dense(ks[6], (DEPTH, SSM_CONV, SSM_CONV_DIM), SSM_CONV),
        "conv_b": 0.02 * jax.random.normal(ks[7], (DEPTH, SSM_CONV_DIM), jnp.float32),
        "dt_bias": dt_bias,
        "a_log": a_log,
        "d_skip": 1.0 + 0.02 * jax.random.normal(ks[10], (DEPTH, SSM_HEADS), jnp.float32),
        "ssm_norm": gain(ks[11], SSM_D_INNER),
        "w_branch_sb": dense(ks[12], (DEPTH, SB_WIDTH, D_MODEL), SB_WIDTH),
        "w_branch_ssm": dense(ks[13], (DEPTH, SSM_D_INNER, D_MODEL), SSM_D_INNER),
        "w_out": dense(ks[14], (DEPTH, D_MODEL, D_MODEL), D_MODEL),
        "ffn2_norm": gain(ks[15], D_MODEL),
        "ffn2_w_up": dense(ks[16], (DEPTH, D_MODEL, 2 * FFN_HIDDEN), D_MODEL),
        "ffn2_w_down": dense(ks[17], (DEPTH, FFN_HIDDEN, D_MODEL), FFN_HIDDEN),
        "final_norm": 1.0 + 0.02 * jax.random.normal(ks[18], (D_MODEL,), jnp.float32),
    }


def reference(x, ffn1_norm, ffn1_w_up, ffn1_w_down, mix_norm, w_in, conv_w, conv_b,
              dt_bias, a_log, d_skip, ssm_norm, w_branch_sb, w_branch_ssm, w_out,
              ffn2_norm, ffn2_w_up, ffn2_w_down, final_norm):
    b, s, _ = x.shape
    for l in range(DEPTH):
        x = x + 0.5 * swiglu(rms_norm(x, ffn1_norm[l]), ffn1_w_up[l], ffn1_w_down[l])
        h = rms_norm(x, mix_norm[l])
        proj = h @ w_in[l]
        q, k, v, z, xbc, dt_raw, gate_sb, gate_ssm = jnp.split(proj, IN_SPLITS, axis=-1)
        heads = (b, s, SB_HEADS, SB_HEAD_DIM)
        y_sb = stick_breaking_attention(q.reshape(heads), k.reshape(heads), v.reshape(heads))
        y_sb = y_sb.reshape(b, s, SB_WIDTH) @ w_branch_sb[l]
        y_ssm = mamba2_mixer(z, xbc, dt_raw, conv_w[l], conv_b[l], dt_bias[l],
                             a_log[l], d_skip[l], ssm_norm[l]) @ w_branch_ssm[l]
        merged = jax.nn.sigmoid(gate_sb) * y_sb + jax.nn.sigmoid(gate_ssm) * y_ssm
        x = x + merged @ w_out[l]
        x = x + 0.5 * swiglu(rms_norm(x, ffn2_norm[l]), ffn2_w_up[l], ffn2_w_down[l])
    return rms_norm(x, final_norm)
```

```python
import numpy as np
import ml_dtypes
from contextlib import ExitStack
import concourse.bass as bass
import concourse.mybir as mybir
from concourse.bass_utils import run_bass_kernel_spmd

F32 = mybir.dt.float32
BF16 = mybir.dt.bfloat16
AF = mybir.ActivationFunctionType
ALU = mybir.AluOpType
AX = mybir.AxisListType
NPBF = ml_dtypes.bfloat16

D = 1024
FH = 2816
SBW = 512
DI = 1536
CD = 3584
NH = 24
NG = 8
INW = 8728
EPS = 1e-6
NCORES = 8

SEM_LIMIT = 30000


class Buf:
    __slots__ = ("name", "writers", "readers", "sems", "dma_n")

    def __init__(self, name):
        self.name = name
        self.writers = []
        self.readers = []
        self.sems = None
        self.dma_n = 0


class Op:
    __slots__ = ("eng", "fn", "is_dma", "deps", "signal", "sig_idx", "sem_buf", "dma_idx")

    def __init__(self, eng, fn, is_dma):
        self.eng = eng
        self.fn = fn
        self.is_dma = is_dma
        self.deps = []
        self.signal = False
        self.sig_idx = None
        self.sem_buf = None
        self.dma_idx = None


class Sched:
    COMPUTE = ("pe", "act", "dve", "pool")

    def __init__(self, nc, same_engine_sync=True):
        self.nc = nc
        self.ops = []
        self.same_engine_sync = same_engine_sync
        self.dma_bufs = []
        self.n_sems = 0

    def _joins(self, b, op):
        return (b.writers and not b.readers and (op.is_dma or op.eng == "pe") and
                all(w.eng == op.eng and w.is_dma == op.is_dma for w in b.writers))

    def _add(self, op, reads, writes):
        deps = []
        for b in reads:
            deps.extend(b.writers)
        jn = [self._joins(b, op) for b in writes]
        for b, j in zip(writes, jn):
            if not j:
                deps.extend(b.writers)
                deps.extend(b.readers)
        seen = set()
        for d in deps:
            if d is op or id(d) in seen:
                continue
            seen.add(id(d))
            if (not d.is_dma) and (not op.is_dma) and d.eng == op.eng:
                if op.eng == "pe" or not self.same_engine_sync:
                    continue
            op.deps.append(d)
            if not d.is_dma:
                d.signal = True
        for b, j in zip(writes, jn):
            if j:
                b.writers.append(op)
            else:
                b.writers = [op]
                b.readers = []
        for b in reads:
            if b not in writes:
                b.readers.append(op)
        self.ops.append(op)
        return op

    def op(self, eng, fn, reads=(), writes=()):
        return self._add(Op(eng, fn, False), list(reads), list(writes))

    def dma(self, q, out, in_, reads, writes, sem_of, **kw):
        def fn(e, out=out, in_=in_, kw=kw):
            return e.dma_start(out=out, in_=in_, **kw)
        o = Op(q, fn, True)
        o.sem_buf = sem_of
        o.dma_idx = sem_of.dma_n
        sem_of.dma_n += 1
        if sem_of.dma_n == 1:
            self.dma_bufs.append(sem_of)
        return self._add(o, list(reads), list(writes))

    def emit(self):
        nc = self.nc
        engs = {"pe": nc.tensor, "act": nc.scalar, "dve": nc.vector, "pool": nc.gpsimd, "sp": nc.sync}
        cnt = {e: 0 for e in self.COMPUTE}
        for o in self.ops:
            if not o.is_dma and o.signal:
                o.sig_idx = cnt[o.eng]
                cnt[o.eng] += 1
        with ExitStack() as es:
            esem = {}
            for e in self.COMPUTE:
                n = (cnt[e] + SEM_LIMIT - 1) // SEM_LIMIT
                esem[e] = [es.enter_context(nc.semaphore(f"s_{e}{i}")) for i in range(max(n, 1))]
            DL = SEM_LIMIT // 16
            for b in self.dma_bufs:
                n = (b.dma_n + DL - 1) // DL
                b.sems = [es.enter_context(nc.semaphore(f"d_{b.name}_{i}")) for i in range(max(n, 1))]
            self.n_sems = sum(len(v) for v in esem.values()) + sum(len(b.sems) for b in self.dma_bufs)
            streams = {e: [] for e in engs}
            for o in self.ops:
                streams[o.eng].append(o)
            waited = {e: {} for e in engs}

            def target(d):
                if d.is_dma:
                    i = d.dma_idx
                    return d.sem_buf.sems[i // DL], (i % DL + 1) * 16
                i = d.sig_idx
                return esem[d.eng][i // SEM_LIMIT], i % SEM_LIMIT + 1

            def emit_stream(ename):
                e = engs[ename]
                w = waited[ename]
                for o in streams[ename]:
                    need = {}
                    for d in o.deps:
                        s, v = target(d)
                        k = id(s)
                        if w.get(k, 0) >= v:
                            continue
                        if k not in need or need[k][1] < v:
                            need[k] = (s, v)
                    for k, (s, v) in need.items():
                        e.wait_ge(s, v)
                        w[k] = v
                    ins = o.fn(e)
                    if o.is_dma:
                        s, _ = target(o)
                        ins.then_inc(s, 16)
                    elif o.signal:
                        s, _ = target(o)
                        ins.then_inc(s, 1)
                if ename == "sp":
                    for b in self.dma_bufs:
                        for gi, s in enumerate(b.sems):
                            n = min(b.dma_n - gi * DL, DL)
                            if n > 0:
                                e.wait_ge(s, n * 16)

            with nc.Block() as block:
                @block.sync
                def _(eng):
                    emit_stream("sp")

                @block.tensor
                def _(eng):
                    emit_stream("pe")

                @block.scalar
                def _(eng):
                    emit_stream("act")

                @block.vector
                def _(eng):
                    emit_stream("dve")

                @block.gpsimd
                def _(eng):
                    emit_stream("pool")


class Ring:
    def __init__(self, items):
        self.items = items
        self.i = 0

    def next(self):
        it = self.items[self.i % len(self.items)]
        self.i += 1
        return it


T = 512
NB = T // 128

C_Q, C_K, C_V, C_Z, C_XBC, C_DT, C_GSB, C_GSSM = 0, 512, 1024, 1536, 3072, 6656, 6680, 7704


class TokProg:
    def __init__(self, stage, NT):
        self.stage = stage
        self.NT = NT
        self.has_mixout = stage in ("CA", "C1")
        self.ffns = {"A0": ["a"], "CA": ["a", "b"], "C1": ["a"]}[stage]
        self.has_proj = stage in ("A0", "CA")
        self.has_final = stage == "C1"
        self.nc = nc = bass.Bass("TRN2", target_bir_lowering=False)
        self.S = Sched(nc)
        self._n = 0
        self.build()

    def sb(self, shape, dt, name=None):
        self._n += 1
        return self.nc.alloc_sbuf_tensor(name or f"sb{self._n}", list(shape), dt).ap()

    def ps(self, shape, dt, name=None):
        self._n += 1
        return self.nc.alloc_psum_tensor(name or f"ps{self._n}", list(shape), dt).ap()

    def din(self, name, shape, dt):
        return self.nc.dram_tensor(name, list(shape), dt, kind="ExternalInput").ap()

    def dout(self, name, shape, dt):
        return self.nc.dram_tensor(name, list(shape), dt, kind="ExternalOutput").ap()

    def dscr(self, name, shape, dt):
        return self.nc.dram_tensor(name, list(shape), dt, kind="Internal").ap()

    def cast_weight(self, name, w, K, c0, ntiles, ncols):
        KC = K // 128
        scr = self.dscr(name + "_bf", [ntiles, 128, KC, ncols], BF16)
        b = Buf(name + "_bf")
        src = w.rearrange("(kc p) n -> kc p n", p=128)
        for kc in range(KC):
            s = src[kc][:, c0:c0 + ntiles * ncols].rearrange("p (nt nn) -> p nt nn", nn=ncols)
            d = scr[:, :, kc, :].rearrange("nt p nn -> p nt nn")
            self.S.dma("pool", d, s, [], [b], b, max_dma_last_dim=4096)
        return scr, b

    def build(self):
        nc, S = self.nc, self.S
        NT = self.NT
        ntiles = NT // T
        x_in = self.din("x", [NT, D], F32)
        ident_d = self.din("ident", [128, 128], F32)
        B_dram_out = Buf("dram_out")
        W = {}
        if self.has_mixout:
            attnT_d = self.din("attnT", [SBW, NT], BF16)
            yssmT_d = self.din("yssmT", [DI, NT], BF16)
            sgT_d = self.din("sgT", [2 * D, NT], BF16)
            w_bsb = self.din("w_bsb", [SBW, D], F32)
            w_bssm = self.din("w_bssm", [DI, D], F32)
            w_out = self.din("w_out", [D, D], F32)
            W["bsb"] = self.cast_weight("w_bsb", w_bsb, SBW, 0, 4, 256)
            W["bssm"] = self.cast_weight("w_bssm", w_bssm, DI, 0, 4, 256)
            W["out"] = self.cast_weight("w_out", w_out, D, 0, 1, 1024)
        for k in self.ffns:
            g = self.din(f"f{k}_gain", [128, 8], F32)
            up = self.din(f"f{k}_up", [D, 2 * FH], F32)
            dn = self.din(f"f{k}_dn", [FH, D], F32)
            W[f"f{k}_gain"] = g
            W[f"f{k}_up"] = self.cast_weight(f"f{k}_up", up, D, 0, 22, 256)
            W[f"f{k}_dn"] = self.cast_weight(f"f{k}_dn", dn, FH, 0, 1, 1024)
        if self.has_proj:
            W["mix_gain"] = self.din("mix_gain", [128, 8], F32)
            w_in = self.din("w_in", [D, INW], F32)
            dt_bias_d = self.din("dt_bias", [1, NH], F32)
            a_log_d = self.din("a_log", [1, NH], F32)
            W["in_qk"] = self.cast_weight("w_in_qk", w_in, D, C_Q, 4, 256)
            W["in_v"] = self.cast_weight("w_in_v", w_in, D, C_V, 1, 512)
            W["in_z"] = self.cast_weight("w_in_z", w_in, D, C_Z, 3, 512)
            W["in_xbc"] = self.cast_weight("w_in_xbc", w_in, D, C_XBC, 14, 256)
            W["in_dt"] = self.cast_weight("w_in_dt", w_in, D, C_DT, 1, NH)
            W["in_g"] = self.cast_weight("w_in_g", w_in, D, C_GSB, 8, 256)
            x_out = self.dout("x_out", [NT, D], F32)
            qT_o = self.dout("qT", [SBW, NT], BF16)
            kT_o = self.dout("kT", [SBW, NT], BF16)
            v_o = self.dout("v", [NT, SBW], BF16)
            sz_o = self.dout("sz", [NT, DI], BF16)
            xbcT_o = self.dout("xbcT", [CD, NT], BF16)
            dtda_o = self.dout("dtda", [NT, 2 * NH], F32)
            sgT_o = self.dout("sgT_out", [2 * D, NT], BF16)
        if self.has_final:
            fin_gain_d = self.din("fin_gain", [1, D], F32)
            y_o = self.dout("y", [NT, D], F32)

        xt = self.sb([128, NB, D], F32, "xt")
        B_x = [Buf(f"x{b}") for b in range(NB)]
        junk = self.sb([128, D], F32, "junk"); B_junk = Buf("junk")
        hn = [self.sb([128, D], BF16, f"hn{i}") for i in range(2)]
        R_hn = Ring([(hn[i], Buf(f"hn{i}")) for i in range(2)])
        hT = self.sb([128, 8, T], BF16, "hT")
        B_hT = [Buf(f"hT{b}") for b in range(NB)]
        gT = self.sb([128, 22, T], BF16, "gT")
        B_gT = [Buf(f"gT{j}") for j in range(22)]
        wdn = self.sb([128, 22, D], BF16, "wdn"); B_wdn = Buf("wdn")
        NSLOT = 4
        R_w = Ring([(self.sb([128, 4096], BF16, f"wslot{i}"), Buf(f"wslot{i}")) for i in range(NSLOT)])
        R_tmp = Ring([(self.sb([128, T], F32, f"tmp{i}"), Buf(f"tmp{i}")) for i in range(4)])
        stat = self.sb([128, 16], F32, "stat"); B_stat = Buf("stat")
        neghalf = self.sb([128, 4], F32, "neghalf"); B_nh = Buf("neghalf")
        ident_f = self.sb([128, 128], F32, "ident_f"); B_idf = Buf("ident_f")
        ident_b = self.sb([128, 128], BF16, "ident_b"); B_idb = Buf("ident_b")
        gains = {}
        ptr = self.ps([128, 1024], BF16, "ptr"); B_ptr = Buf("ptr")
        R_pA = Ring([(self.ps([128, 512], F32, f"pA{i}"), Buf(f"pA{i}")) for i in range(2)])
        R_pB = Ring([(self.ps([128, 512], F32, f"pB{i}"), Buf(f"pB{i}")) for i in range(2)])
        R_pO = Ring([(self.ps([128, 512], F32, f"pO{i}"), Buf(f"pO{i}")) for i in range(2)])
        pS = self.ps([128, 512], F32, "pS"); B_pS = Buf("pS")

        S.dma("sp", ident_f, ident_d, [], [B_idf], B_idf)
        S.op("dve", lambda e: e.tensor_copy(out=ident_b, in_=ident_f), [B_idf], [B_idb])
        S.op("pool", lambda e: e.memset(neghalf, -0.5), [], [B_nh])
        for key in ([f"f{k}_gain" for k in self.ffns] + (["mix_gain"] if self.has_proj else [])):
            gt = self.sb([128, 8], F32, "g_" + key)
            b = Buf("g_" + key)
            S.dma("sp", gt, W[key], [], [b], b)
            gains[key] = (gt, b)
        if self.has_mixout:
            wout_sb = self.sb([128, 8, D], BF16, "wout_sb"); B_wout = Buf("wout_sb")
            S.dma("sp", wout_sb, W["out"][0][0], [W["out"][1]], [B_wout], B_wout)
            attnT_sb = self.sb([128, 4, T], BF16, "attnT_sb"); B_attn = Buf("attnT_sb")
            yssmT_sb = self.sb([128, 12, T], BF16, "yssmT_sb"); B_yssm = Buf("yssmT_sb")
            R_sg = Ring([(self.sb([128, 4, T], BF16, f"sg_sb{i}"), Buf(f"sg_sb{i}")) for i in range(2)])
            mT = self.sb([128, 8, T], BF16, "mT"); B_mT = [Buf(f"mT{n}") for n in range(8)]
        if self.has_proj:
            dtb4 = self.sb([128, NB, NH], F32, "dtb4"); B_dtb = Buf("dtb4")
            nega4 = self.sb([128, NB, NH], F32, "nega4"); B_nega = Buf("nega4")
            for b in range(NB):
                S.dma("sp", dtb4[:, b, :], dt_bias_d.partition_broadcast(128), [], [B_dtb], B_dtb)
                S.dma("sp", nega4[:, b, :], a_log_d.partition_broadcast(128), [], [B_nega], B_nega)
            S.op("act", lambda e: e.activation(out=nega4, in_=nega4, func=AF.Exp), [B_nega], [B_nega])
            S.op("dve", lambda e: e.tensor_scalar(out=nega4, in0=nega4, scalar1=-1.0, scalar2=None, op0=ALU.mult),
                 [B_nega], [B_nega])
            R_ost = Ring([(self.sb([128, 4, T], BF16, f"ost{i}"), Buf(f"ost{i}")) for i in range(3)])
            dts = self.sb([128, 4, NB * NH], F32, "dts"); B_dts = Buf("dts")
            dtda_st = self.sb([128, NB, 2 * NH], F32, "dtda_st"); B_dtda = Buf("dtda_st")
        if self.has_final:
            fing = self.sb([128, D], F32, "fing"); B_fing = Buf("fing")
            S.dma("sp", fing, fin_gain_d.partition_broadcast(128), [], [B_fing], B_fing)
            R_fo = Ring([(self.sb([128, D], F32, f"fo{i}"), Buf(f"fo{i}")) for i in range(2)])

        def load_wtile(key, ti, dst_view, slotbuf):
            scr, b = W[key]
            S.dma("sp", dst_view, scr[ti], [b], [slotbuf], slotbuf)

        def rmsnorm_hT(gain_key):
            gt, gb = gains[gain_key]
            for b in range(NB):
                S.op("act", lambda e, b=b: e.activation(out=junk, in_=xt[:, b, :], func=AF.Square,
                                                       accum_out=stat[:, b:b + 1]),
                     [B_x[b]], [B_junk, B_stat])
            S.op("dve", lambda e: e.tensor_scalar(out=stat[:, 4:8], in0=stat[:, 0:4], scalar1=1.0 / D, scalar2=EPS,
                                                  op0=ALU.mult, op1=ALU.add), [B_stat], [B_stat])
            S.op("pool", lambda e: e.tensor_tensor(out=stat[:, 8:12], in0=stat[:, 4:8], in1=neghalf, op=ALU.pow),
                 [B_stat, B_nh], [B_stat])
            for b in range(NB):
                h, hb = R_hn.next()
                S.op("act", lambda e, b=b, h=h: e.activation(out=h, in_=xt[:, b, :], func=AF.Copy,
                                                             scale=stat[:, 8 + b:9 + b]),
                     [B_x[b], B_stat], [hb])
                for kc in range(8):
                    S.op("pe", lambda e, kc=kc, h=h: e.transpose(ptr[:, kc * 128:(kc + 1) * 128],
                                                                 h[:, kc * 128:(kc + 1) * 128], ident_b),
                         [hb, B_idb], [B_ptr])
                S.op("dve", lambda e, b=b: e.tensor_tensor(
                    out=hT[:, :, b * 128:(b + 1) * 128], in0=ptr.rearrange("p (k t) -> p k t", k=8),
                    in1=gt.unsqueeze(2).to_broadcast([128, 8, 128]), op=ALU.mult),
                    [B_ptr, gb], [B_hT[b]])

        def ffn(k):
            rmsnorm_hT(f"f{k}_gain")
            scr, b = W[f"f{k}_dn"]
            S.dma("sp", wdn[:, 0:11, :], scr[0][:, 0:11, :], [b], [B_wdn], B_wdn)
            S.dma("sp", wdn[:, 11:22, :], scr[0][:, 11:22, :], [b], [B_wdn], B_wdn)
            for i in range(11):
                slot, sbuf = R_w.next()
                sv = slot.rearrange("p (two kc n) -> p two kc n", two=2, kc=8)
                load_wtile(f"f{k}_up", i, sv[:, 0], sbuf)
                load_wtile(f"f{k}_up", 11 + i, sv[:, 1], sbuf)
                for jj in range(2):
                    j = 2 * i + jj
                    pg, pgb = R_pA.next()
                    pu, pub = R_pB.next()
                    for kc in range(8):
                        S.op("pe", lambda e, kc=kc, jj=jj, pg=pg, sv=sv: e.matmul(
                            pg, lhsT=sv[:, 0, kc, jj * 128:(jj + 1) * 128], rhs=hT[:, kc, :],
                            start=(kc == 0), stop=(kc == 7)), [sbuf] + B_hT, [pgb])
                    for kc in range(8):
                        S.op("pe", lambda e, kc=kc, jj=jj, pu=pu, sv=sv: e.matmul(
                            pu, lhsT=sv[:, 1, kc, jj * 128:(jj + 1) * 128], rhs=hT[:, kc, :],
                            start=(kc == 0), stop=(kc == 7)), [sbuf] + B_hT, [pub])
                    tmp, tb = R_tmp.next()
                    S.op("act", lambda e, pg=pg, tmp=tmp: e.activation(out=tmp, in_=pg, func=AF.Silu), [pgb], [tb])
                    S.op("dve", lambda e, j=j, pu=pu, tmp=tmp: e.tensor_tensor(out=gT[:, j, :], in0=pu, in1=tmp,
                                                                               op=ALU.mult), [pub, tb], [B_gT[j]])
            for half in range(2):
                for b in range(NB):
                    po, pob = R_pO.next()
                    for j in range(22):
                        S.op("pe", lambda e, j=j, b=b, half=half, po=po: e.matmul(
                            po, lhsT=gT[:, j, b * 128:(b + 1) * 128], rhs=wdn[:, j, half * 512:(half + 1) * 512],
                            start=(j == 0), stop=(j == 21)), [B_gT[j], B_wdn], [pob])
                    S.op("dve", lambda e, b=b, half=half, po=po: e.scalar_tensor_tensor(
                        out=xt[:, b, half * 512:(half + 1) * 512], in0=po, scalar=0.5,
                        in1=xt[:, b, half * 512:(half + 1) * 512], op0=ALU.mult, op1=ALU.add),
                        [pob, B_x[b]], [B_x[b]])

        def mixout(t0):
            S.dma("sp", attnT_sb, attnT_d.rearrange("(kc p) t -> p kc t", p=128)[:, :, t0:t0 + T],
                  [], [B_attn], B_attn)
            S.dma("sp", yssmT_sb, yssmT_d.rearrange("(kc p) t -> p kc t", p=128)[:, :, t0:t0 + T],
                  [], [B_yssm], B_yssm)
            sgv = sgT_d.rearrange("(kc p) t -> p kc t", p=128)
            for i in range(4):
                sg_sb, B_sg = R_sg.next()
                S.dma("sp", sg_sb[:, 0:2, :], sgv[:, 2 * i:2 * i + 2, t0:t0 + T], [], [B_sg], B_sg)
                S.dma("sp", sg_sb[:, 2:4, :], sgv[:, 8 + 2 * i:8 + 2 * i + 2, t0:t0 + T], [], [B_sg], B_sg)
                slot, sbuf = R_w.next()
                v_sb = slot[:, 0:1024].rearrange("p (kc n) -> p kc n", kc=4)
                v_ss = slot[:, 1024:4096].rearrange("p (kc n) -> p kc n", kc=12)
                load_wtile("bsb", i, v_sb, sbuf)
                load_wtile("bssm", i, v_ss, sbuf)
                for jj in range(2):
                    n = 2 * i + jj
                    p1, p1b = R_pA.next()
                    p2, p2b = R_pB.next()
                    for kc in range(4):
                        S.op("pe", lambda e, kc=kc, jj=jj, p1=p1, v_sb=v_sb: e.matmul(
                            p1, lhsT=v_sb[:, kc, jj * 128:(jj + 1) * 128], rhs=attnT_sb[:, kc, :],
                            start=(kc == 0), stop=(kc == 3)), [sbuf, B_attn], [p1b])
                    for kc in range(12):
                        S.op("pe", lambda e, kc=kc, jj=jj, p2=p2, v_ss=v_ss: e.matmul(
                            p2, lhsT=v_ss[:, kc, jj * 128:(jj + 1) * 128], rhs=yssmT_sb[:, kc, :],
                            start=(kc == 0), stop=(kc == 11)), [sbuf, B_yssm], [p2b])
                    t1, t1b = R_tmp.next()
                    t2, t2b = R_tmp.next()
                    S.op("dve", lambda e, jj=jj, p1=p1, t1=t1, sg_sb=sg_sb: e.tensor_tensor(
                        out=t1, in0=p1, in1=sg_sb[:, jj, :], op=ALU.mult), [p1b, B_sg], [t1b])
                    S.op("dve", lambda e, jj=jj, p2=p2, t2=t2, sg_sb=sg_sb: e.tensor_tensor(
                        out=t2, in0=p2, in1=sg_sb[:, 2 + jj, :], op=ALU.mult), [p2b, B_sg], [t2b])
                    S.op("pool", lambda e, n=n, t1=t1, t2=t2: e.tensor_tensor(out=mT[:, n, :], in0=t1, in1=t2,
                                                                              op=ALU.add), [t1b, t2b], [B_mT[n]])
            for b in range(NB):
                for half in range(2):
                    po, pob = R_pO.next()
                    for kc in range(8):
                        S.op("pe", lambda e, kc=kc, b=b, half=half, po=po: e.matmul(
                            po, lhsT=mT[:, kc, b * 128:(b + 1) * 128], rhs=wout_sb[:, kc, half * 512:(half + 1) * 512],
                            start=(kc == 0), stop=(kc == 7)), [B_mT[kc], B_wout], [pob])
                    S.op("dve", lambda e, b=b, half=half, po=po: e.tensor_tensor(
                        out=xt[:, b, half * 512:(half + 1) * 512], in0=po,
                        in1=xt[:, b, half * 512:(half + 1) * 512], op=ALU.add), [pob, B_x[b]], [B_x[b]])

        def proj(t0):
            for b in range(NB):
                S.dma("pool", x_out[t0 + b * 128:t0 + (b + 1) * 128, :], xt[:, b, :], [B_x[b]], [B_dram_out], B_x[b])
            rmsnorm_hT("mix_gain")
            fm = []
            fm.append(("in_qk", 0, qT_o[0:512, t0:t0 + T], "q"))
            fm.append(("in_qk", 2, kT_o[0:512, t0:t0 + T], "c"))
            for i in range(7):
                fm.append(("in_xbc", 2 * i, xbcT_o[i * 512:(i + 1) * 512, t0:t0 + T], "c"))
            for i in range(4):
                fm.append(("in_g", 2 * i, sgT_o[i * 512:(i + 1) * 512, t0:t0 + T], "g"))
            for (wkey, ti, oap, kind) in fm:
                slot, sbuf = R_w.next()
                sv = slot.rearrange("p (two kc n) -> p two kc n", two=2, kc=8)
                load_wtile(wkey, ti, sv[:, 0], sbuf)
                load_wtile(wkey, ti + 1, sv[:, 1], sbuf)
                ost, ostb = R_ost.next()
                for c in range(4):
                    pa, pab = (R_pA if c % 2 == 0 else R_pB).next()
                    for kc in range(8):
                        S.op("pe", lambda e, kc=kc, c=c, pa=pa, sv=sv: e.matmul(
                            pa, lhsT=sv[:, c // 2, kc, (c % 2) * 128:(c % 2 + 1) * 128], rhs=hT[:, kc, :],
                            start=(kc == 0), stop=(kc == 7)), [sbuf] + B_hT, [pab])
                    if kind == "g":
                        S.op("act", lambda e, c=c, pa=pa, ost=ost: e.activation(out=ost[:, c, :], in_=pa,
                                                                                func=AF.Sigmoid), [pab], [ostb])
                    else:
                        sc = (128.0 ** -0.5) if kind == "q" else 1.0
                        S.op("dve", lambda e, c=c, pa=pa, ost=ost, sc=sc: e.tensor_scalar(
                            out=ost[:, c, :], in0=pa, scalar1=sc, scalar2=None, op0=ALU.mult), [pab], [ostb])
                S.dma("pool", oap.rearrange("(c p) t -> p c t", p=128), ost, [ostb], [B_dram_out], ostb)
            tm = [("in_v", 0, v_o, 0, "c")] + [("in_z", i, sz_o, i * 512, "s") for i in range(3)]
            for (wkey, ti, oten, c0, kind) in tm:
                slot, sbuf = R_w.next()
                sv = slot.rearrange("p (kc n) -> p kc n", kc=8)
                load_wtile(wkey, ti, sv, sbuf)
                ost, ostb = R_ost.next()
                for b in range(NB):
                    po, pob = R_pO.next()
                    for kc in range(8):
                        S.op("pe", lambda e, kc=kc, b=b, po=po, sv=sv: e.matmul(
                            po, lhsT=hT[:, kc, b * 128:(b + 1) * 128], rhs=sv[:, kc, :],
                            start=(kc == 0), stop=(kc == 7)), [sbuf, B_hT[b]], [pob])
                    if kind == "s":
                        S.op("act", lambda e, b=b, po=po, ost=ost: e.activation(out=ost[:, b, :], in_=po,
                                                                                func=AF.Silu), [pob], [ostb])
                    else:
                        S.op("dve", lambda e, b=b, po=po, ost=ost: e.tensor_copy(out=ost[:, b, :], in_=po),
                             [pob], [ostb])
                S.dma("pool", oten[t0:t0 + T, c0:c0 + 512].rearrange("(b p) n -> p b n", p=128), ost,
                      [ostb], [B_dram_out], ostb)
            slot, sbuf = R_w.next()
            sv = slot[:, 0:8 * NH].rearrange("p (kc n) -> p kc n", kc=8)
            load_wtile("in_dt", 0, sv, sbuf)
            for b in range(NB):
                for kc in range(8):
                    S.op("pe", lambda e, kc=kc, b=b, sv=sv: e.matmul(
                        pS[:, b * NH:(b + 1) * NH], lhsT=hT[:, kc, b * 128:(b + 1) * 128], rhs=sv[:, kc, :],
                        start=(kc == 0), stop=(kc == 7)), [sbuf, B_hT[b]], [B_pS])
            NN = NB * NH
            dtb_f = dtb4.rearrange("p b n -> p (b n)")
            nega_f = nega4.rearrange("p b n -> p (b n)")
            S.op("dve", lambda e: e.tensor_tensor(out=dts[:, 0, :], in0=pS[:, 0:NN], in1=dtb_f, op=ALU.add),
                 [B_pS, B_dtb], [B_dts])
            S.op("act", lambda e: e.activation(out=dts[:, 1, :], in_=dts[:, 0, :], func=AF.Abs), [B_dts], [B_dts])
            S.op("act", lambda e: e.activation(out=dts[:, 2, :], in_=dts[:, 1, :], func=AF.Exp, scale=-1.0),
                 [B_dts], [B_dts])
            S.op("act", lambda e: e.activation(out=dts[:, 3, :], in_=dts[:, 2, :], func=AF.Ln, bias=1.0),
                 [B_dts], [B_dts])
            S.op("dve", lambda e: e.scalar_tensor_tensor(
                out=dtda_st[:, :, 0:NH], in0=dts[:, 0, :].rearrange("p (b n) -> p b n", b=NB), scalar=0.0,
                in1=dts[:, 3, :].rearrange("p (b n) -> p b n", b=NB), op0=ALU.max, op1=ALU.add),
                [B_dts], [B_dtda])
            S.op("dve", lambda e: e.tensor_tensor(out=dtda_st[:, :, NH:2 * NH], in0=dtda_st[:, :, 0:NH],
                                                  in1=nega4, op=ALU.mult), [B_dtda, B_nega], [B_dtda])
            S.dma("pool", dtda_o[t0:t0 + T, :].rearrange("(b p) n -> p b n", p=128), dtda_st,
                  [B_dtda], [B_dram_out], B_dtda)

        def final(t0):
            for b in range(NB):
                S.op("act", lambda e, b=b: e.activation(out=junk, in_=xt[:, b, :], func=AF.Square,
                                                       accum_out=stat[:, b:b + 1]), [B_x[b]], [B_junk, B_stat])
            S.op("dve", lambda e: e.tensor_scalar(out=stat[:, 4:8], in0=stat[:, 0:4], scalar1=1.0 / D, scalar2=EPS,
                                                  op0=ALU.mult, op1=ALU.add), [B_stat], [B_stat])
            S.op("pool", lambda e: e.tensor_tensor(out=stat[:, 8:12], in0=stat[:, 4:8], in1=neghalf, op=ALU.pow),
                 [B_stat, B_nh], [B_stat])
            for b in range(NB):
                fo, fob = R_fo.next()
                S.op("dve", lambda e, b=b, fo=fo: e.scalar_tensor_tensor(
                    out=fo, in0=xt[:, b, :], scalar=stat[:, 8 + b:9 + b], in1=fing, op0=ALU.mult, op1=ALU.mult),
                    [B_x[b], B_stat, B_fing], [fob])
                S.dma("pool", y_o[t0 + b * 128:t0 + (b + 1) * 128, :], fo, [fob], [B_dram_out], fob)

        for ti in range(ntiles):
            t0 = ti * T
            for b in range(NB):
                S.dma("sp", xt[:, b, :], x_in[t0 + b * 128:t0 + (b + 1) * 128, :], [], [B_x[b]], B_x[b])
            fi = 0
            if self.has_mixout:
                mixout(t0)
                ffn(self.ffns[fi]); fi += 1
            if fi < len(self.ffns):
                ffn(self.ffns[fi]); fi += 1
            if self.has_proj:
                proj(t0)
            if self.has_final:
                final(t0)
        S.emit()


class MixProg:
    def __init__(self, SEQ, do_attn=True, do_ssd=True):
        self.SEQ = SEQ
        self.do_attn = do_attn
        self.do_ssd = do_ssd
        self.nc = bass.Bass("TRN2", target_bir_lowering=False)
        self.S = Sched(self.nc)
        self._n = 0
        self.build()

    def sb(self, shape, dt, name=None):
        self._n += 1
        return self.nc.alloc_sbuf_tensor(name or f"sb{self._n}", list(shape), dt).ap()

    def din(self, name, shape, dt):
        return self.nc.dram_tensor(name, list(shape), dt, kind="ExternalInput").ap()

    def dout(self, name, shape, dt):
        return self.nc.dram_tensor(name, list(shape), dt, kind="ExternalOutput").ap()

    def build(self):
        nc, S = self.nc, self.S
        SEQ = self.SEQ
        NBLK = SEQ // 128
        NGRP = SEQ // 512
        B_dram_out = Buf("dram_out")
        qT_d = self.din("qT", [128, SEQ], BF16)
        kT_d = self.din("kT", [128, SEQ], BF16)
        v_d = self.din("v", [SEQ, 128], BF16)
        xbc_d = self.din("xbcT", [896, SEQ], BF16)
        convw_d = self.din("conv_w", [128, 7, 4], F32)
        convb_d = self.din("conv_b", [128, 7], F32)
        sz_d = self.din("sz", [SEQ, 384], BF16)
        dtda_d = self.din("dtda", [SEQ, 12], F32)
        dskip_d = self.din("d_skip", [1, 6], F32)
        ssmn_d = self.din("ssm_norm", [1, 384], F32)
        ident_d = self.din("ident", [128, 128], F32)
        ctri_d = self.din("c_tri", [128, 4, 128], F32)
        cmb_d = self.din("c_mb", [128, 4, 512], BF16)
        attnT_o = self.dout("attnT", [128, SEQ], BF16)
        yssmT_o = self.dout("yssmT", [384, SEQ], BF16)

        ident_f = self.sb([128, 128], F32, "ident_f"); B_idf = Buf("ident_f")
        ident_b = self.sb([128, 128], BF16, "ident_b"); B_idb = Buf("ident_b")
        ctri = self.sb([128, 4, 128], F32, "ctri"); B_ctri = Buf("ctri")
        ctri_b = self.sb([128, 2, 128], BF16, "ctri_b"); B_ctrib = Buf("ctri_b")
        ones_f = self.sb([128, 128], F32, "ones_f"); B_ones = Buf("ones_f")
        S.dma("sp", ident_f, ident_d, [], [B_idf], B_idf)
        S.dma("sp", ctri, ctri_d, [], [B_ctri], B_ctri)
        S.op("dve", lambda e: e.tensor_copy(out=ident_b, in_=ident_f), [B_idf], [B_idb])
        S.op("dve", lambda e: e.tensor_copy(out=ctri_b, in_=ctri[:, 0:2, :]), [B_ctri], [B_ctrib])
        S.op("pool", lambda e: e.memset(ones_f, 1.0), [], [B_ones])
        NTi, NU = ctri_b[:, 0, :], ctri_b[:, 1, :]
        SL, UI = ctri[:, 2, :], ctri[:, 3, :]
        banks = [self.nc.alloc_psum_tensor(f"bank{i}", [128, 512], F32).ap() for i in range(8)]
        B_bank = [Buf(f"bank{i}") for i in range(8)]

        if self.do_attn:
            mb_b = self.sb([128, 4, 512], BF16, "mb_b"); B_mbb = Buf("mb_b")
            S.dma("sp", mb_b, cmb_d, [], [B_mbb], B_mbb)
            kT = self.sb([128, SEQ], BF16, "kT_sb"); B_kT = Buf("kT_sb")
            qT = self.sb([128, SEQ], BF16, "qT_sb"); B_qT = Buf("qT_sb")
            vs = self.sb([128, NBLK, 128], BF16, "v_sb"); B_vs = Buf("v_sb")
            nch = max(1, SEQ // 4096)
            cw = SEQ // nch
            for i in range(nch):
                S.dma("sp", kT[:, i * cw:(i + 1) * cw], kT_d[:, i * cw:(i + 1) * cw], [], [B_kT], B_kT)
                S.dma("sp", qT[:, i * cw:(i + 1) * cw], qT_d[:, i * cw:(i + 1) * cw], [], [B_qT], B_qT)
                bw = NBLK // nch
                S.dma("sp", vs[:, i * bw:(i + 1) * bw, :],
                      v_d.rearrange("(j p) d -> p j d", p=128)[:, i * bw:(i + 1) * bw, :], [], [B_vs], B_vs)
            NS = 2
            zb = [[(banks[4 * c + i], B_bank[4 * c + i]) for i in range(2)] for c in range(NS)]
            Pb = [(banks[4 * c + 2], B_bank[4 * c + 2]) for c in range(NS)]
            Ob = [(banks[4 * c + 3], B_bank[4 * c + 3]) for c in range(NS)]
            eb = [[(self.sb([128, 512], F32, f"e{c}{i}"), Buf(f"e{c}{i}")) for i in range(2)] for c in range(NS)]
            spb = [[(self.sb([128, 512], BF16, f"sp{c}{i}"), Buf(f"sp{c}{i}")) for i in range(2)] for c in range(NS)]
            E2b = [(self.sb([128, 512], F32, f"E2{c}"), Buf(f"E2{c}")) for c in range(NS)]
            Wb = [[(self.sb([128, 512], BF16, f"W{c}{i}"), Buf(f"W{c}{i}")) for i in range(2)] for c in range(NS)]
            R_ao = Ring([(self.sb([128, 512], BF16, f"ao{i}"), Buf(f"ao{i}")) for i in range(2)])
            groups = list(range(NGRP - 1, -1, -1))
            steps = [[], []]
            for idx, G in enumerate(groups):
                c = idx % NS
                for j in range(4 * G + 3, -1, -1):
                    steps[c].append((G, j, j == 4 * G + 3, j == 0))
            n_iter = max(len(s) for s in steps)

            def S1(c, t):
                G, j, first, last = steps[c][t]
                z, zbuf = zb[c][t % 2]
                diag = j >= 4 * G
                S.op("pe", lambda e: e.matmul(z, lhsT=kT[:, j * 128:(j + 1) * 128], rhs=qT[:, G * 512:(G + 1) * 512],
                                              start=True, stop=not diag), [B_kT, B_qT], [zbuf])
                if diag:
                    r = j - 4 * G
                    S.op("pe", lambda e: e.matmul(z, lhsT=ident_b, rhs=mb_b[:, r, :], start=False, stop=True),
                         [B_idb, B_mbb], [zbuf])

            def S23(c, t):
                z, zbuf = zb[c][t % 2]
                ee, ebuf = eb[c][t % 2]
                sp, spbuf = spb[c][t % 2]
                S.op("act", lambda e: e.activation(out=ee, in_=z, func=AF.Exp), [zbuf], [ebuf])
                S.op("act", lambda e: e.activation(out=sp, in_=ee, func=AF.Ln, bias=1.0), [ebuf], [spbuf])

            def S4(c, t):
                G, j, first, last = steps[c][t]
                P, Pbuf = Pb[c]
                sp, spbuf = spb[c][t % 2]
                if first:
                    S.op("pe", lambda e: e.matmul(P, lhsT=NTi, rhs=sp, start=True, stop=True),
                         [B_ctrib, spbuf], [Pbuf])
                else:
                    spp, sppbuf = spb[c][(t - 1) % 2]
                    S.op("pe", lambda e: e.matmul(P, lhsT=NU, rhs=spp, start=False, stop=False),
                         [B_ctrib, sppbuf], [Pbuf])
                    S.op("pe", lambda e: e.matmul(P, lhsT=NTi, rhs=sp, start=False, stop=True),
                         [B_ctrib, spbuf], [Pbuf])

            def S5(c, t):
                P, Pbuf = Pb[c]
                E2, E2buf = E2b[c]
                S.op("act", lambda e: e.activation(out=E2, in_=P, func=AF.Exp), [Pbuf], [E2buf])

            def S6(c, t):
                ee, ebuf = eb[c][t % 2]
                E2, E2buf = E2b[c]
                W, Wbuf = Wb[c][t % 2]
                S.op("dve", lambda e: e.tensor_tensor(out=W, in0=ee, in1=E2, op=ALU.mult), [ebuf, E2buf], [Wbuf])

            def S7(c, t):
                G, j, first, last = steps[c][t]
                W, Wbuf = Wb[c][t % 2]
                O, Obuf = Ob[c]
                S.op("pe", lambda e: e.matmul(O, lhsT=vs[:, j, :], rhs=W, start=first, stop=last),
                     [B_vs, Wbuf], [Obuf])
                if last:
                    ao, aobuf = R_ao.next()
                    S.op("dve", lambda e: e.tensor_copy(out=ao, in_=O), [Obuf], [aobuf])
                    S.dma("pool", attnT_o[:, G * 512:(G + 1) * 512], ao, [aobuf], [B_dram_out], aobuf)

            act = lambda c, t: 0 <= t < len(steps[c])
            for c in range(NS):
                if act(c, 0):
                    S1(c, 0)
            for t in range(n_iter + 1):
                for c in range(NS):
                    if act(c, t + 1):
                        S1(c, t + 1)
                for c in range(NS):
                    if act(c, t):
                        S23(c, t)
                for c in range(NS):
                    if act(c, t - 1):
                        S7(c, t - 1)
                for c in range(NS):
                    if act(c, t):
                        S4(c, t)
                for c in range(NS):
                    if act(c, t):
                        S5(c, t)
                for c in range(NS):
                    if act(c, t):
                        S6(c, t)

        if self.do_ssd:
            NTL = SEQ // 512
            cw_f = self.sb([128, 7, 4], F32, "cw_f"); B_cw = Buf("cw_f")
            cb_f = self.sb([128, 7], F32, "cb_f"); B_cb = Buf("cb_f")
            S.dma("sp", cw_f, convw_d, [], [B_cw], B_cw)
            S.dma("sp", cb_f, convb_d, [], [B_cb], B_cb)
            dg = self.sb([128, 7, 4, 128], BF16, "dg"); B_dg = Buf("dg")
            for c in range(7):
                for k in range(4):
                    S.op("dve", lambda e, c=c, k=k: e.tensor_scalar(out=dg[:, c, k, :], in0=ident_f,
                                                                    scalar1=cw_f[:, c, k:k + 1], scalar2=None,
                                                                    op0=ALU.mult), [B_idf, B_cw], [B_dg])
            dsk6 = self.sb([128, 6], F32, "dsk6"); B_dsk6 = Buf("dsk6")
            dsk = self.sb([128, 6, 64], F32, "dsk"); B_dsk = Buf("dsk")
            ssmn = self.sb([128, 384], F32, "ssmn"); B_ssmn = Buf("ssmn")
            S.dma("sp", dsk6, dskip_d.partition_broadcast(128), [], [B_dsk6], B_dsk6)
            S.dma("sp", ssmn, ssmn_d.partition_broadcast(128), [], [B_ssmn], B_ssmn)
            S.op("dve", lambda e: e.tensor_copy(out=dsk, in_=dsk6.unsqueeze(2).to_broadcast([128, 6, 64])),
                 [B_dsk6], [B_dsk])
            neghalf = self.sb([128, 2], F32, "neghalf"); B_nh = Buf("neghalf")
            S.op("pool", lambda e: e.memset(neghalf, -0.5), [], [B_nh])
            R_xin = Ring([(self.sb([128, 7, 515], BF16, f"xin{i}"), Buf(f"xin{i}")) for i in range(2)])
            R_cT = Ring([(self.sb([128, 7, 512], BF16, f"cT{i}"), Buf(f"cT{i}")) for i in range(2)])
            R_sz = Ring([(self.sb([128, 4, 384], BF16, f"szt{i}"), Buf(f"szt{i}")) for i in range(2)])
            R_dd = Ring([(self.sb([128, 4, 12], F32, f"ddt{i}"), Buf(f"ddt{i}")) for i in range(2)])
            R_yT = Ring([(self.sb([128, 3, 512], BF16, f"yTst{i}"), Buf(f"yTst{i}")) for i in range(2)])
            Sf = self.sb([128, 6, 64], F32, "Sf"); B_Sf = Buf("Sf")
            Sb = self.sb([128, 6, 64], BF16, "Sb"); B_Sb = Buf("Sb")
            S.op("pool", lambda e: e.memset(Sf, 0.0), [], [B_Sf])
            S.op("pool", lambda e: e.memset(Sb, 0.0), [], [B_Sb])
            mk = lambda shape, dt, nm: (self.sb(shape, dt, nm), Buf(nm))
            R_tok = Ring([mk([128, 640], BF16, f"tok{i}") for i in range(2)])
            R_E18 = Ring([mk([128, 18], F32, f"E18{i}") for i in range(2)])
            R_rhsD = Ring([mk([128, 6, 128], F32, f"rhsD{i}") for i in range(1)])
            R_L = Ring([mk([128, 6, 128], F32, f"L{i}") for i in range(1)])
            R_CBm = Ring([mk([128, 2, 128], F32, f"CBm{i}") for i in range(2)])
            R_M = Ring([mk([128, 6, 128], BF16, f"M{i}") for i in range(2)])
            R_xdt = Ring([mk([128, 6, 64], BF16, f"xdt{i}") for i in range(2)])
            R_xw = Ring([mk([128, 6, 64], BF16, f"xw{i}") for i in range(2)])
            R_y = Ring([mk([128, 384], F32, f"yy{i}") for i in range(6)])
            R_st = Ring([mk([128, 8], F32, f"rst{i}") for i in range(2)])
            R_yn = Ring([mk([128, 384], BF16, f"yn{i}") for i in range(2)])
            junk = self.sb([128, 192], F32, "junk_s"); B_junk = Buf("junk_s")
            pc, B_pc = banks[0], B_bank[0]
            ptr, B_ptr = banks[1].bitcast(BF16), B_bank[1]
            pm, B_pm = banks[2], B_bank[2]
            pD = [(banks[3], B_bank[3]), (banks[4], B_bank[4])]
            pY, B_pY = banks[5], B_bank[5]
            pI, B_pI = banks[6], B_bank[6]
            pSt, B_pSt = banks[7], B_bank[7]

            for tt in range(NTL):
                t0 = tt * 512
                xin, B_xin = R_xin.next()
                if tt == 0:
                    S.op("pool", lambda e, xin=xin: e.memset(xin[:, :, 0:3], 0.0), [], [B_xin])
                    S.dma("sp", xin[:, :, 3:515], xbc_d.rearrange("(c p) t -> p c t", p=128)[:, :, 0:512],
                          [], [B_xin], B_xin)
                else:
                    S.dma("sp", xin, xbc_d.rearrange("(c p) t -> p c t", p=128)[:, :, t0 - 3:t0 + 512],
                          [], [B_xin], B_xin)
                szt, B_szt = R_sz.next()
                ddt, B_ddt = R_dd.next()
                S.dma("sp", szt, sz_d[t0:t0 + 512, :].rearrange("(b p) n -> p b n", p=128), [], [B_szt], B_szt)
                S.dma("sp", ddt, dtda_d[t0:t0 + 512, :].rearrange("(b p) n -> p b n", p=128), [], [B_ddt], B_ddt)
                cT, B_cT = R_cT.next()
                for c in range(7):
                    for k in range(4):
                        S.op("pe", lambda e, c=c, k=k, xin=xin: e.matmul(pc, lhsT=dg[:, c, k, :], rhs=xin[:, c, k:k + 512],
                                                                         start=(k == 0), stop=(k == 3)),
                             [B_dg, B_xin], [B_pc])
                    S.op("act", lambda e, c=c, cT=cT: e.activation(out=cT[:, c, :], in_=pc, func=AF.Silu,
                                                                   bias=cb_f[:, c:c + 1]), [B_pc, B_cb], [B_cT])
                yT, B_yT = R_yT.next()
                for ch in range(4):
                    cs = slice(ch * 128, (ch + 1) * 128)
                    dt6 = ddt[:, ch, 0:6]
                    dA6 = ddt[:, ch, 6:12]
                    for i in range(5):
                        S.op("pe", lambda e, i=i, cT=cT, cs=cs: e.transpose(ptr[:, i * 128:(i + 1) * 128], cT[:, i, cs], ident_b),
                             [B_cT, B_idb], [B_ptr])
                    tok, B_tok = R_tok.next()
                    S.op("dve", lambda e, tok=tok: e.tensor_copy(out=tok, in_=ptr[:, 0:640]), [B_ptr], [B_tok])
                    S.op("pe", lambda e, dA6=dA6: e.matmul(pm[:, 256:262], lhsT=UI, rhs=dA6, start=True, stop=True),
                         [B_ctri, B_ddt], [B_pm])
                    S.op("pe", lambda e, dA6=dA6: e.matmul(pm[:, 262:268], lhsT=ones_f, rhs=dA6, start=True, stop=True),
                         [B_ones, B_ddt], [B_pm])
                    S.op("pe", lambda e, dA6=dA6: e.matmul(pm[:, 268:274], lhsT=SL, rhs=dA6, start=True, stop=True),
                         [B_ctri, B_ddt], [B_pm])
                    for g in range(2):
                        S.op("pe", lambda e, g=g, cT=cT, cs=cs: e.matmul(pm[:, g * 128:(g + 1) * 128], lhsT=cT[:, 3 + g, cs],
                                                                         rhs=cT[:, 5 + g, cs], start=True, stop=True),
                             [B_cT], [B_pm])
                    E18, B_E18 = R_E18.next()
                    S.op("act", lambda e, E18=E18: e.activation(out=E18, in_=pm[:, 256:274], func=AF.Exp), [B_pm], [B_E18])
                    CBm, B_CBm = R_CBm.next()
                    S.op("dve", lambda e, CBm=CBm: e.tensor_tensor(
                        out=CBm, in0=pm[:, 0:256].rearrange("p (g q) -> p g q", g=2),
                        in1=UI.unsqueeze(1).to_broadcast([128, 2, 128]), op=ALU.mult), [B_pm, B_ctri], [B_CBm])
                    rhsD, B_rhsD = R_rhsD.next()
                    S.op("dve", lambda e, rhsD=rhsD, dA6=dA6: e.tensor_tensor(
                        out=rhsD, in0=UI.unsqueeze(1).to_broadcast([128, 6, 128]),
                        in1=dA6.unsqueeze(2).to_broadcast([128, 6, 128]), op=ALU.mult), [B_ctri, B_ddt], [B_rhsD])
                    L, B_L = R_L.next()
                    for g in range(2):
                        pDg, B_pDg = pD[g]
                        S.op("pe", lambda e, g=g, pDg=pDg, rhsD=rhsD: e.matmul(
                            pDg[:, 0:384], lhsT=SL, rhs=rhsD[:, 3 * g:3 * g + 3, :].rearrange("p h q -> p (h q)"),
                            start=True, stop=True), [B_ctri, B_rhsD], [B_pDg])
                        S.op("act", lambda e, g=g, pDg=pDg, L=L: e.activation(
                            out=L[:, 3 * g:3 * g + 3, :].rearrange("p h q -> p (h q)"), in_=pDg[:, 0:384], func=AF.Exp),
                            [B_pDg], [B_L])
                    M, B_M = R_M.next()
                    for g in range(2):
                        S.op("dve", lambda e, g=g, M=M, L=L, CBm=CBm: e.tensor_tensor(
                            out=M[:, 3 * g:3 * g + 3, :], in0=L[:, 3 * g:3 * g + 3, :],
                            in1=CBm[:, g, :].unsqueeze(1).to_broadcast([128, 3, 128]), op=ALU.mult),
                            [B_L, B_CBm], [B_M])
                    xdt, B_xdt = R_xdt.next()
                    S.op("dve", lambda e, xdt=xdt, tok=tok, dt6=dt6: e.tensor_tensor(
                        out=xdt, in0=tok[:, 0:384].rearrange("p (h d) -> p h d", h=6),
                        in1=dt6.unsqueeze(2).to_broadcast([128, 6, 64]), op=ALU.mult), [B_tok, B_ddt], [B_xdt])
                    for hh in range(6):
                        S.op("pe", lambda e, hh=hh, M=M, xdt=xdt: e.matmul(pY[:, hh * 64:(hh + 1) * 64], lhsT=M[:, hh, :],
                                                                           rhs=xdt[:, hh, :], start=True, stop=True),
                             [B_M, B_xdt], [B_pY])
                    for g in range(2):
                        S.op("pe", lambda e, g=g, cT=cT, cs=cs: e.matmul(
                            pI[:, g * 192:(g + 1) * 192], lhsT=cT[:, 5 + g, cs],
                            rhs=Sb[:, 3 * g:3 * g + 3, :].rearrange("p h d -> p (h d)"), start=True, stop=True),
                            [B_cT, B_Sb], [B_pI])
                    y1, B_y1 = R_y.next()
                    S.op("dve", lambda e, y1=y1, E18=E18: e.tensor_tensor(
                        out=y1.rearrange("p (h d) -> p h d", h=6), in0=pI[:, 0:384].rearrange("p (h d) -> p h d", h=6),
                        in1=E18[:, 0:6].unsqueeze(2).to_broadcast([128, 6, 64]), op=ALU.mult), [B_pI, B_E18], [B_y1])
                    y2, B_y2 = R_y.next()
                    S.op("dve", lambda e, y1=y1, y2=y2: e.tensor_tensor(out=y2, in0=pY[:, 0:384], in1=y1, op=ALU.add),
                         [B_pY, B_y1], [B_y2])
                    t3, B_t3 = R_y.next()
                    S.op("pool", lambda e, t3=t3, tok=tok: e.tensor_tensor(
                        out=t3, in0=tok[:, 0:384], in1=dsk.rearrange("p h d -> p (h d)"), op=ALU.mult),
                        [B_tok, B_dsk], [B_t3])
                    y3, B_y3 = R_y.next()
                    S.op("pool", lambda e, y3=y3, y2=y2, t3=t3: e.tensor_tensor(out=y3, in0=y2, in1=t3, op=ALU.add),
                         [B_y2, B_t3], [B_y3])
                    y4, B_y4 = R_y.next()
                    S.op("dve", lambda e, y4=y4, y3=y3, szt=szt, ch=ch: e.tensor_tensor(out=y4, in0=y3, in1=szt[:, ch, :],
                                                                                      op=ALU.mult), [B_y3, B_szt], [B_y4])
                    rst, B_rst = R_st.next()
                    for g in range(2):
                        S.op("act", lambda e, g=g, y4=y4, rst=rst: e.activation(
                            out=junk, in_=y4[:, g * 192:(g + 1) * 192], func=AF.Square, accum_out=rst[:, g:g + 1]),
                            [B_y4], [B_junk, B_rst])
                    S.op("dve", lambda e, rst=rst: e.tensor_scalar(out=rst[:, 2:4], in0=rst[:, 0:2], scalar1=1.0 / 192,
                                                                   scalar2=EPS, op0=ALU.mult, op1=ALU.add),
                         [B_rst], [B_rst])
                    S.op("pool", lambda e, rst=rst: e.tensor_tensor(out=rst[:, 4:6], in0=rst[:, 2:4], in1=neghalf,
                                                                    op=ALU.pow), [B_rst, B_nh], [B_rst])
                    yn, B_yn = R_yn.next()
                    for g in range(2):
                        S.op("dve", lambda e, g=g, yn=yn, y4=y4, rst=rst: e.scalar_tensor_tensor(
                            out=yn[:, g * 192:(g + 1) * 192], in0=y4[:, g * 192:(g + 1) * 192],
                            scalar=rst[:, 4 + g:5 + g], in1=ssmn[:, g * 192:(g + 1) * 192], op0=ALU.mult, op1=ALU.mult),
                            [B_y4, B_rst, B_ssmn], [B_yn])
                    for i in range(3):
                        S.op("pe", lambda e, i=i, yn=yn: e.transpose(ptr[:, 640 + i * 128:640 + (i + 1) * 128],
                                                                     yn[:, i * 128:(i + 1) * 128], ident_b),
                             [B_yn, B_idb], [B_ptr])
                    S.op("dve", lambda e, yT=yT, cs=cs: e.tensor_copy(
                        out=yT[:, :, cs], in_=ptr[:, 640:1024].rearrange("p (c t) -> p c t", c=3)), [B_ptr], [B_yT])
                    xw, B_xw = R_xw.next()
                    S.op("dve", lambda e, xw=xw, xdt=xdt, E18=E18: e.tensor_tensor(
                        out=xw, in0=xdt, in1=E18[:, 12:18].unsqueeze(2).to_broadcast([128, 6, 64]), op=ALU.mult),
                        [B_xdt, B_E18], [B_xw])
                    for g in range(2):
                        S.op("pe", lambda e, g=g, tok=tok, xw=xw: e.matmul(
                            pSt[:, g * 192:(g + 1) * 192], lhsT=tok[:, 384 + g * 128:384 + (g + 1) * 128],
                            rhs=xw[:, 3 * g:3 * g + 3, :].rearrange("p h d -> p (h d)"), start=True, stop=True),
                            [B_tok, B_xw], [B_pSt])
                    S.op("dve", lambda e, E18=E18: e.tensor_tensor(
                        out=Sf, in0=Sf, in1=E18[:, 6:12].unsqueeze(2).to_broadcast([128, 6, 64]), op=ALU.mult),
                        [B_Sf, B_E18], [B_Sf])
                    S.op("dve", lambda e: e.tensor_tensor(out=Sf.rearrange("p h d -> p (h d)"),
                                                          in0=Sf.rearrange("p h d -> p (h d)"), in1=pSt[:, 0:384],
                                                          op=ALU.add), [B_Sf, B_pSt], [B_Sf])
                    S.op("act", lambda e: e.copy(out=Sb, in_=Sf), [B_Sf], [B_Sb])
                S.dma("pool", yssmT_o.rearrange("(c p) t -> p c t", p=128)[:, :, t0:t0 + 512], yT, [B_yT],
                      [B_dram_out], B_yT)
        S.emit()


_PROGS = {}


def _get_prog(kind, *args):
    key = (kind,) + args
    if key not in _PROGS:
        if kind == "tok":
            _PROGS[key] = TokProg(*args)
        else:
            _PROGS[key] = MixProg(*args)
    return _PROGS[key]


def _gain_t(g):
    return np.ascontiguousarray(np.asarray(g, np.float32).reshape(8, 128).T)


def _c(a, dt=np.float32):
    return np.ascontiguousarray(np.asarray(a, dt))


def run_tok(stage, NT, per_core, shared):
    prog = _get_prog("tok", stage, NT)
    ident = np.eye(128, dtype=np.float32)
    in_maps = []
    for c in range(NCORES):
        m = dict(shared)
        m.update(per_core[c])
        m["ident"] = ident
        in_maps.append(m)
    res = run_bass_kernel_spmd(prog.nc, in_maps, core_ids=list(range(NCORES)))
    return res.results


def _mix_consts():
    k = np.arange(128)[:, None]
    l = np.arange(128)[None, :]
    ctri = np.zeros((128, 4, 128), np.float32)
    ctri[:, 0, :] = -1.0 * (k >= l)
    ctri[:, 1, :] = -1.0 * (k < l)
    ctri[:, 2, :] = (k > l)
    ctri[:, 3, :] = (k <= l)
    q = np.arange(512)[None, :]
    cmb = np.zeros((128, 4, 512), np.float32)
    for r in range(4):
        cmb[:, r, :] = np.where(r * 128 + k < q, 0.0, -30000.0)
    return ctri, cmb


def run_mix(SEQ, per_core, do_attn=True, do_ssd=True):
    prog = _get_prog("mix", SEQ, do_attn, do_ssd)
    ctri, cmb = _mix_consts()
    ident = np.eye(128, dtype=np.float32)
    in_maps = []
    for c in range(NCORES):
        m = dict(per_core[c])
        m["ident"] = ident
        m["c_tri"] = ctri
        m["c_mb"] = cmb.astype(NPBF)
        in_maps.append(m)
    res = run_bass_kernel_spmd(prog.nc, in_maps, core_ids=list(range(NCORES)))
    return res.results


def _tok_shared_ffn(tag, p, name, l):
    return {f"f{tag}_gain": _gain_t(p[f"{name}_norm"][l]), f"f{tag}_up": _c(p[f"{name}_w_up"][l]),
            f"f{tag}_dn": _c(p[f"{name}_w_down"][l])}


def _tok_shared_proj(p, l):
    return {"mix_gain": _gain_t(p["mix_norm"][l]), "w_in": _c(p["w_in"][l]),
            "dt_bias": _c(p["dt_bias"][l].reshape(1, NH)), "a_log": _c(p["a_log"][l].reshape(1, NH))}


def _tok_shared_mixout(p, l):
    return {"w_bsb": _c(p["w_branch_sb"][l]), "w_bssm": _c(p["w_branch_ssm"][l]), "w_out": _c(p["w_out"][l])}


def _mix_inputs(tok_res, p, l, B, SEQ):
    cpb = NCORES // B
    per_core = []
    for b in range(B):
        cs = range(b * cpb, (b + 1) * cpb)
        qT = np.concatenate([tok_res[c]["qT"] for c in cs], axis=1)
        kT = np.concatenate([tok_res[c]["kT"] for c in cs], axis=1)
        v = np.concatenate([tok_res[c]["v"] for c in cs], axis=0)
        sz = np.concatenate([tok_res[c]["sz"] for c in cs], axis=0)
        xbcT = np.concatenate([tok_res[c]["xbcT"] for c in cs], axis=1)
        dtda = np.concatenate([tok_res[c]["dtda"] for c in cs], axis=0)
        for r in range(4):
            chs = np.concatenate([np.arange(384 * r, 384 * r + 384), DI + np.arange(256 * r, 256 * r + 256),
                                  DI + 1024 + np.arange(256 * r, 256 * r + 256)])
            cw = np.asarray(p["conv_w"][l], np.float32)[:, chs]
            cb = np.asarray(p["conv_b"][l], np.float32)[chs]
            m = {
                "qT": np.ascontiguousarray(qT[128 * r:128 * (r + 1)]),
                "kT": np.ascontiguousarray(kT[128 * r:128 * (r + 1)]),
                "v": np.ascontiguousarray(v[:, 128 * r:128 * (r + 1)]),
                "xbcT": np.ascontiguousarray(xbcT[chs]),
                "conv_w": _c(cw.T.reshape(7, 128, 4).transpose(1, 0, 2)),
                "conv_b": _c(cb.reshape(7, 128).T),
                "sz": np.ascontiguousarray(sz[:, 384 * r:384 * (r + 1)]),
                "dtda": np.ascontiguousarray(np.concatenate([dtda[:, 6 * r:6 * r + 6],
                                                             dtda[:, NH + 6 * r:NH + 6 * r + 6]], axis=1)),
                "d_skip": _c(p["d_skip"][l][6 * r:6 * r + 6].reshape(1, 6)),
                "ssm_norm": _c(p["ssm_norm"][l][384 * r:384 * r + 384].reshape(1, 384)),
            }
            per_core.append(m)
    return per_core


def _mixout_inputs(mix_res, tok_res, B, NT):
    cpb = NCORES // B
    per_core = []
    for b in range(B):
        attnT = np.concatenate([mix_res[b * 4 + r]["attnT"] for r in range(4)], axis=0)
        yssmT = np.concatenate([mix_res[b * 4 + r]["yssmT"] for r in range(4)], axis=0)
        for i in range(cpb):
            c = b * cpb + i
            per_core.append({
                "x": tok_res[c]["x_out"],
                "attnT": np.ascontiguousarray(attnT[:, i * NT:(i + 1) * NT]),
                "yssmT": np.ascontiguousarray(yssmT[:, i * NT:(i + 1) * NT]),
                "sgT": tok_res[c]["sgT_out"],
            })
    return per_core


def forward(x, p):
    x = np.asarray(x, np.float32)
    B, SEQ, _ = x.shape
    NT = B * SEQ // NCORES
    flat = x.reshape(B * SEQ, D)
    per_core = [{"x": _c(flat[c * NT:(c + 1) * NT])} for c in range(NCORES)]
    shared = {}
    shared.update(_tok_shared_ffn("a", p, "ffn1", 0))
    shared.update(_tok_shared_proj(p, 0))
    tok = run_tok("A0", NT, per_core, shared)
    mix = run_mix(SEQ, _mix_inputs(tok, p, 0, B, SEQ))
    per_core = _mixout_inputs(mix, tok, B, NT)
    shared = {}
    shared.update(_tok_shared_mixout(p, 0))
    shared.update(_tok_shared_ffn("a", p, "ffn2", 0))
    shared.update(_tok_shared_ffn("b", p, "ffn1", 1))
    shared.update(_tok_shared_proj(p, 1))
    tok = run_tok("CA", NT, per_core, shared)
    mix = run_mix(SEQ, _mix_inputs(tok, p, 1, B, SEQ))
    per_core = _mixout_inputs(mix, tok, B, NT)
    shared = {}
    shared.update(_tok_shared_mixout(p, 1))
    shared.update(_tok_shared_ffn("a", p, "ffn2", 1))
    shared["fin_gain"] = _c(np.asarray(p["final_norm"]).reshape(1, D))
    fin = run_tok("C1", NT, per_core, shared)
    y = np.concatenate([np.asarray(fin[c]["y"], np.float32) for c in range(NCORES)], axis=0)
    return y.reshape(B, SEQ, D)


def kernel(x, ffn1_norm, ffn1_w_up, ffn1_w_down, mix_norm, w_in, conv_w, conv_b, dt_bias, a_log, d_skip,
           ssm_norm, w_branch_sb, w_branch_ssm, w_out, ffn2_norm, ffn2_w_up, ffn2_w_down, final_norm):
    p = dict(ffn1_norm=ffn1_norm, ffn1_w_up=ffn1_w_up, ffn1_w_down=ffn1_w_down, mix_norm=mix_norm, w_in=w_in,
             conv_w=conv_w, conv_b=conv_b, dt_bias=dt_bias, a_log=a_log, d_skip=d_skip, ssm_norm=ssm_norm,
             w_branch_sb=w_branch_sb, w_branch_ssm=w_branch_ssm, w_out=w_out, ffn2_norm=ffn2_norm,
             ffn2_w_up=ffn2_w_up, ffn2_w_down=ffn2_w_down, final_norm=final_norm)
    p = {k: np.asarray(v, np.float32) for k, v in p.items()}
    return forward(x, p)
```

```python
import numpy as np
import ml_dtypes
from contextlib import ExitStack
import concourse.bass as bass
import concourse.mybir as mybir
from concourse.bass_utils import run_bass_kernel_spmd

F32 = mybir.dt.float32
BF16 = mybir.dt.bfloat16
AF = mybir.ActivationFunctionType
ALU = mybir.AluOpType
AX = mybir.AxisListType
NPBF = ml_dtypes.bfloat16

D = 1024
FH = 2816
SBW = 512
DI = 1536
CD = 3584
NH = 24
NG = 8
INW = 8728
EPS = 1e-6
NCORES = 8

SEM_LIMIT = 30000
FUSED = True


class Buf:
    __slots__ = ("name", "writers", "readers", "sem", "total")

    def __init__(self, name):
        self.name = name
        self.writers = []
        self.readers = []
        self.sem = None
        self.total = 0


class Op:
    __slots__ = ("eng", "fn", "is_dma", "deps", "signal", "sig_idx", "sem_buf", "sem_val", "inc")

    def __init__(self, eng, fn, is_dma):
        self.eng = eng
        self.fn = fn
        self.is_dma = is_dma
        self.deps = []
        self.signal = False
        self.sig_idx = None
        self.sem_buf = None
        self.sem_val = None
        self.inc = 16


class Sched:
    COMPUTE = ("pe", "act", "dve", "pool")

    def __init__(self, nc, same_engine_sync=True):
        self.nc = nc
        self.ops = []
        self.same_engine_sync = same_engine_sync
        self.dma_bufs = []
        self.n_sems = 0

    def _joins(self, b, op):
        return (b.writers and not b.readers and (op.is_dma or op.eng == "pe") and
                all(w.eng == op.eng and w.is_dma == op.is_dma for w in b.writers))

    def _add(self, op, reads, writes):
        deps = []
        for b in reads:
            deps.extend(b.writers)
        jn = [self._joins(b, op) for b in writes]
        for b, j in zip(writes, jn):
            if not j:
                deps.extend(b.writers)
                deps.extend(b.readers)
        seen = set()
        for d in deps:
            if d is op or id(d) in seen:
                continue
            seen.add(id(d))
            if (not d.is_dma) and (not op.is_dma) and d.eng == op.eng:
                if op.eng == "pe" or not self.same_engine_sync:
                    continue
            op.deps.append(d)
            if not d.is_dma:
                d.signal = True
        for b, j in zip(writes, jn):
            if j:
                b.writers.append(op)
            else:
                b.writers = [op]
                b.readers = []
        for b in reads:
            if b not in writes:
                b.readers.append(op)
        self.ops.append(op)
        return op

    def op(self, eng, fn, reads=(), writes=()):
        return self._add(Op(eng, fn, False), list(reads), list(writes))

    def dma(self, q, out, in_, reads, writes, sem_of, **kw):
        def fn(e, out=out, in_=in_, kw=kw):
            return e.dma_start(out=out, in_=in_, **kw)
        return self.dma_fn(q, fn, reads, writes, sem_of)

    def dma_fn(self, q, fn, reads, writes, sem_of, inc=16):
        o = Op(q, fn, True)
        o.sem_buf = sem_of
        o.inc = inc
        if sem_of.total == 0 and sem_of not in self.dma_bufs:
            self.dma_bufs.append(sem_of)
        sem_of.total += inc
        o.sem_val = sem_of.total
        return self._add(o, list(reads), list(writes))

    def coll(self, ins_ap, outs_ap, groups, reads, writes, sem_of):
        def fn(e):
            return e.collective_compute("AllGather", ALU.bypass, replica_groups=groups, ins=[ins_ap], outs=[outs_ap])
        o = self.dma_fn("pool", fn, reads, writes, sem_of, inc=1)
        prev = getattr(self, "_last_coll", None)
        if prev is not None and prev not in o.deps:
            o.deps.append(prev)
        self._last_coll = o
        return o

    def emit(self, barrier=False):
        nc = self.nc
        engs = {"pe": nc.tensor, "act": nc.scalar, "dve": nc.vector, "pool": nc.gpsimd, "sp": nc.sync}
        cnt = {e: 0 for e in self.COMPUTE}
        for o in self.ops:
            if not o.is_dma and o.signal:
                o.sig_idx = cnt[o.eng]
                cnt[o.eng] += 1
        esem = {}
        for e in self.COMPUTE:
            n = (cnt[e] + SEM_LIMIT - 1) // SEM_LIMIT
            esem[e] = [nc.alloc_semaphore(f"s_{e}{i}_{nc.next_id()}") for i in range(max(n, 1))]
        for b in self.dma_bufs:
            b.sem = nc.alloc_semaphore(f"d_{b.name}_{nc.next_id()}")
        self.n_sems = sum(len(v) for v in esem.values()) + len(self.dma_bufs)
        streams = {e: [] for e in engs}
        for o in self.ops:
            streams[o.eng].append(o)
        waited = {e: {} for e in engs}

        def target(d):
            if d.is_dma:
                return d.sem_buf.sem, d.sem_val
            i = d.sig_idx
            return esem[d.eng][i // SEM_LIMIT], i % SEM_LIMIT + 1

        def emit_stream(ename):
            e = engs[ename]
            w = waited[ename]
            for o in streams[ename]:
                need = {}
                for d in o.deps:
                    sm, v = target(d)
                    k = id(sm)
                    if w.get(k, 0) >= v:
                        continue
                    if k not in need or need[k][1] < v:
                        need[k] = (sm, v)
                for k, (sm, v) in need.items():
                    e.wait_ge(sm, v)
                    w[k] = v
                ins = o.fn(e)
                if o.is_dma:
                    ins.then_inc(o.sem_buf.sem, o.inc)
                elif o.signal:
                    sm, _ = target(o)
                    ins.then_inc(sm, 1)
            if ename == "sp" or barrier:
                for b in self.dma_bufs:
                    e.wait_ge(b.sem, b.total)
            if barrier:
                for en in self.COMPUTE:
                    if cnt[en] > 0:
                        i = cnt[en] - 1
                        e.wait_ge(esem[en][i // SEM_LIMIT], i % SEM_LIMIT + 1)
                e.drain()

        with nc.Block() as block:
            @block.sync
            def _(eng):
                emit_stream("sp")

            @block.tensor
            def _(eng):
                emit_stream("pe")

            @block.scalar
            def _(eng):
                emit_stream("act")

            @block.vector
            def _(eng):
                emit_stream("dve")

            @block.gpsimd
            def _(eng):
                emit_stream("pool")
        if barrier:
            nc.all_engine_barrier()


class Ring:
    def __init__(self, items):
        self.items = items
        self.i = 0

    def next(self):
        it = self.items[self.i % len(self.items)]
        self.i += 1
        return it


T = 512
NB = T // 128

C_Q, C_K, C_V, C_Z, C_XBC, C_DT, C_GSB, C_GSSM = 0, 512, 1024, 1536, 3072, 6656, 6680, 7704


class TokProg:
    def __init__(self, stage, NT, nc=None, fz=None, pfx=""):
        self.stage = stage
        self.NT = NT
        self.fz = fz
        self.pfx = pfx
        self.has_mixout = stage in ("CA", "C1")
        self.ffns = {"A0": ["a"], "CA": ["a", "b"], "C1": ["a"]}[stage]
        self.has_proj = stage in ("A0", "CA")
        self.has_final = stage == "C1"
        self.nc = nc = nc if nc is not None else bass.Bass("TRN2", target_bir_lowering=False)
        self.S = Sched(nc)
        self._n = 0
        self.build()

    def sb(self, shape, dt, name=None):
        self._n += 1
        return self.nc.alloc_sbuf_tensor(f"{self.pfx}{name or 'sb'}_{self._n}", list(shape), dt).ap()

    def ps(self, shape, dt, name=None):
        self._n += 1
        return self.nc.alloc_psum_tensor(f"{self.pfx}{name or 'ps'}_{self._n}", list(shape), dt).ap()

    def din(self, name, shape, dt):
        return self.nc.dram_tensor(self.pfx + name, list(shape), dt, kind="ExternalInput").ap()

    def dout(self, name, shape, dt):
        return self.nc.dram_tensor(self.pfx + name, list(shape), dt, kind="ExternalOutput").ap()

    def dscr(self, name, shape, dt):
        return self.nc.dram_tensor(self.pfx + name, list(shape), dt, kind="Internal").ap()

    def cast_weight(self, name, w, K, c0, ntiles, ncols):
        KC = K // 128
        scr = self.dscr(name + "_bf", [ntiles, 128, KC, ncols], BF16)
        b = Buf(name + "_bf")
        src = w.rearrange("(kc p) n -> kc p n", p=128)
        for kc in range(KC):
            s = src[kc][:, c0:c0 + ntiles * ncols].rearrange("p (nt nn) -> p nt nn", nn=ncols)
            d = scr[:, :, kc, :].rearrange("nt p nn -> p nt nn")
            self.S.dma("pool", d, s, [], [b], b, max_dma_last_dim=4096)
        return scr, b

    def build(self):
        nc, S = self.nc, self.S
        NT = self.NT
        ntiles = NT // T
        fz = self.fz
        x_in = fz["x_src"] if fz and fz.get("x_src") is not None else self.din("x", [NT, D], F32)
        ident_d = fz["ident"] if fz else self.din("ident", [128, 128], F32)
        B_dram_out = Buf("dram_out")
        W = {}
        if self.has_mixout:
            if fz:
                sgT_d = fz["sg_src"]
            else:
                attnT_d = self.din("attnT", [SBW, NT], BF16)
                yssmT_d = self.din("yssmT", [DI, NT], BF16)
                sgT_d = self.din("sgT", [2 * D, NT], BF16)
            w_bsb = self.din("w_bsb", [SBW, D], F32)
            w_bssm = self.din("w_bssm", [DI, D], F32)
            w_out = self.din("w_out", [D, D], F32)
            W["bsb"] = self.cast_weight("w_bsb", w_bsb, SBW, 0, 4, 256)
            W["bssm"] = self.cast_weight("w_bssm", w_bssm, DI, 0, 4, 256)
            W["out"] = self.cast_weight("w_out", w_out, D, 0, 1, 1024)
        for k in self.ffns:
            g = self.din(f"f{k}_gain", [128, 8], F32)
            up = self.din(f"f{k}_up", [D, 2 * FH], F32)
            dn = self.din(f"f{k}_dn", [FH, D], F32)
            W[f"f{k}_gain"] = g
            W[f"f{k}_up"] = self.cast_weight(f"f{k}_up", up, D, 0, 22, 256)
            W[f"f{k}_dn"] = self.cast_weight(f"f{k}_dn", dn, FH, 0, 1, 1024)
        if self.has_proj:
            W["mix_gain"] = self.din("mix_gain", [128, 8], F32)
            if fz:
                w_in_g = self.din("w_in_g", [D, 2 * D], F32)
                W["in_g"] = self.cast_weight("w_in_g", w_in_g, D, 0, 8, 256)
                x_out = fz["x_dst"]
                sgT_o = fz["sg_dst"]
            else:
                w_in = self.din("w_in", [D, INW], F32)
                dt_bias_d = self.din("dt_bias", [1, NH], F32)
                a_log_d = self.din("a_log", [1, NH], F32)
                W["in_g"] = self.cast_weight("w_in_g", w_in, D, C_GSB, 8, 256)
                W["in_qk"] = self.cast_weight("w_in_qk", w_in, D, C_Q, 4, 256)
                W["in_v"] = self.cast_weight("w_in_v", w_in, D, C_V, 1, 512)
                W["in_z"] = self.cast_weight("w_in_z", w_in, D, C_Z, 3, 512)
                W["in_xbc"] = self.cast_weight("w_in_xbc", w_in, D, C_XBC, 14, 256)
                W["in_dt"] = self.cast_weight("w_in_dt", w_in, D, C_DT, 1, NH)
                x_out = self.dout("x_out", [NT, D], F32)
                qT_o = self.dout("qT", [SBW, NT], BF16)
                kT_o = self.dout("kT", [SBW, NT], BF16)
                v_o = self.dout("v", [NT, SBW], BF16)
                sz_o = self.dout("sz", [NT, DI], BF16)
                xbcT_o = self.dout("xbcT", [CD, NT], BF16)
                dtda_o = self.dout("dtda", [NT, 2 * NH], F32)
                sgT_o = self.dout("sgT_out", [2 * D, NT], BF16)
        if self.has_final:
            fin_gain_d = self.din("fin_gain", [1, D], F32)
            y_o = self.dout("y", [NT, D], F32)

        xt = self.sb([128, NB, D], F32, "xt")
        B_x = [Buf(f"x{b}") for b in range(NB)]
        junk = self.sb([128, D], F32, "junk"); B_junk = Buf("junk")
        hn = [self.sb([128, D], BF16, f"hn{i}") for i in range(2)]
        R_hn = Ring([(hn[i], Buf(f"hn{i}")) for i in range(2)])
        hT = self.sb([128, 8, T], BF16, "hT")
        B_hT = [Buf(f"hT{b}") for b in range(NB)]
        gT = self.sb([128, 22, T], BF16, "gT")
        B_gT = [Buf(f"gT{j}") for j in range(22)]
        wdn = self.sb([128, 22, D], BF16, "wdn"); B_wdn = Buf("wdn")
        NSLOT = 4
        R_w = Ring([(self.sb([128, 4096], BF16, f"wslot{i}"), Buf(f"wslot{i}")) for i in range(NSLOT)])
        R_tmp = Ring([(self.sb([128, T], F32, f"tmp{i}"), Buf(f"tmp{i}")) for i in range(4)])
        stat = self.sb([128, 16], F32, "stat"); B_stat = Buf("stat")
        neghalf = self.sb([128, 4], F32, "neghalf"); B_nh = Buf("neghalf")
        ident_f = self.sb([128, 128], F32, "ident_f"); B_idf = Buf("ident_f")
        ident_b = self.sb([128, 128], BF16, "ident_b"); B_idb = Buf("ident_b")
        gains = {}
        ptr = self.ps([128, 1024], BF16, "ptr"); B_ptr = Buf("ptr")
        R_pA = Ring([(self.ps([128, 512], F32, f"pA{i}"), Buf(f"pA{i}")) for i in range(2)])
        R_pB = Ring([(self.ps([128, 512], F32, f"pB{i}"), Buf(f"pB{i}")) for i in range(2)])
        R_pO = Ring([(self.ps([128, 512], F32, f"pO{i}"), Buf(f"pO{i}")) for i in range(2)])
        pS = self.ps([128, 512], F32, "pS"); B_pS = Buf("pS")

        S.dma("sp", ident_f, ident_d, [], [B_idf], B_idf)
        S.op("dve", lambda e: e.tensor_copy(out=ident_b, in_=ident_f), [B_idf], [B_idb])
        S.op("pool", lambda e: e.memset(neghalf, -0.5), [], [B_nh])
        for key in ([f"f{k}_gain" for k in self.ffns] + (["mix_gain"] if self.has_proj else [])):
            gt = self.sb([128, 8], F32, "g_" + key)
            b = Buf("g_" + key)
            S.dma("sp", gt, W[key], [], [b], b)
            gains[key] = (gt, b)
        if self.has_mixout:
            wout_sb = self.sb([128, 8, D], BF16, "wout_sb"); B_wout = Buf("wout_sb")
            S.dma("sp", wout_sb, W["out"][0][0], [W["out"][1]], [B_wout], B_wout)
            attnT_sb = self.sb([128, 4, T], BF16, "attnT_sb"); B_attn = Buf("attnT_sb")
            yssmT_sb = self.sb([128, 12, T], BF16, "yssmT_sb"); B_yssm = Buf("yssmT_sb")
            R_sg = Ring([(self.sb([128, 4, T], BF16, f"sg_sb{i}"), Buf(f"sg_sb{i}")) for i in range(2)])
            mT = self.sb([128, 8, T], BF16, "mT"); B_mT = [Buf(f"mT{n}") for n in range(8)]
        if self.has_proj:
            R_ost = Ring([(self.sb([128, 4, T], BF16, f"ost{i}"), Buf(f"ost{i}")) for i in range(3)])
        if self.has_proj and fz:
            B_hsend = Buf("hsend")
            B_hrecv = Buf("hrecv")
        if self.has_proj and not fz:
            dtb4 = self.sb([128, NB, NH], F32, "dtb4"); B_dtb = Buf("dtb4")
            nega4 = self.sb([128, NB, NH], F32, "nega4"); B_nega = Buf("nega4")
            for b in range(NB):
                S.dma("sp", dtb4[:, b, :], dt_bias_d.partition_broadcast(128), [], [B_dtb], B_dtb)
                S.dma("sp", nega4[:, b, :], a_log_d.partition_broadcast(128), [], [B_nega], B_nega)
            S.op("act", lambda e: e.activation(out=nega4, in_=nega4, func=AF.Exp), [B_nega], [B_nega])
            S.op("dve", lambda e: e.tensor_scalar(out=nega4, in0=nega4, scalar1=-1.0, scalar2=None, op0=ALU.mult),
                 [B_nega], [B_nega])
            dts = self.sb([128, 4, NB * NH], F32, "dts"); B_dts = Buf("dts")
            dtda_st = self.sb([128, NB, 2 * NH], F32, "dtda_st"); B_dtda = Buf("dtda_st")
        if self.has_final:
            fing = self.sb([128, D], F32, "fing"); B_fing = Buf("fing")
            S.dma("sp", fing, fin_gain_d.partition_broadcast(128), [], [B_fing], B_fing)
            R_fo = Ring([(self.sb([128, D], F32, f"fo{i}"), Buf(f"fo{i}")) for i in range(2)])

        def load_wtile(key, ti, dst_view, slotbuf):
            scr, b = W[key]
            S.dma("sp", dst_view, scr[ti], [b], [slotbuf], slotbuf)

        def rmsnorm_hT(gain_key):
            gt, gb = gains[gain_key]
            for b in range(NB):
                S.op("act", lambda e, b=b: e.activation(out=junk, in_=xt[:, b, :], func=AF.Square,
                                                       accum_out=stat[:, b:b + 1]),
                     [B_x[b]], [B_junk, B_stat])
            S.op("dve", lambda e: e.tensor_scalar(out=stat[:, 4:8], in0=stat[:, 0:4], scalar1=1.0 / D, scalar2=EPS,
                                                  op0=ALU.mult, op1=ALU.add), [B_stat], [B_stat])
            S.op("pool", lambda e: e.tensor_tensor(out=stat[:, 8:12], in0=stat[:, 4:8], in1=neghalf, op=ALU.pow),
                 [B_stat, B_nh], [B_stat])
            for b in range(NB):
                h, hb = R_hn.next()
                S.op("act", lambda e, b=b, h=h: e.activation(out=h, in_=xt[:, b, :], func=AF.Copy,
                                                             scale=stat[:, 8 + b:9 + b]),
                     [B_x[b], B_stat], [hb])
                for kc in range(8):
                    S.op("pe", lambda e, kc=kc, h=h: e.transpose(ptr[:, kc * 128:(kc + 1) * 128],
                                                                 h[:, kc * 128:(kc + 1) * 128], ident_b),
                         [hb, B_idb], [B_ptr])
                S.op("dve", lambda e, b=b: e.tensor_tensor(
                    out=hT[:, :, b * 128:(b + 1) * 128], in0=ptr.rearrange("p (k t) -> p k t", k=8),
                    in1=gt.unsqueeze(2).to_broadcast([128, 8, 128]), op=ALU.mult),
                    [B_ptr, gb], [B_hT[b]])

        def ffn(k):
            rmsnorm_hT(f"f{k}_gain")
            scr, b = W[f"f{k}_dn"]
            S.dma("sp", wdn[:, 0:11, :], scr[0][:, 0:11, :], [b], [B_wdn], B_wdn)
            S.dma("sp", wdn[:, 11:22, :], scr[0][:, 11:22, :], [b], [B_wdn], B_wdn)
            for i in range(11):
                slot, sbuf = R_w.next()
                sv = slot.rearrange("p (two kc n) -> p two kc n", two=2, kc=8)
                load_wtile(f"f{k}_up", i, sv[:, 0], sbuf)
                load_wtile(f"f{k}_up", 11 + i, sv[:, 1], sbuf)
                for jj in range(2):
                    j = 2 * i + jj
                    pg, pgb = R_pA.next()
                    pu, pub = R_pB.next()
                    for kc in range(8):
                        S.op("pe", lambda e, kc=kc, jj=jj, pg=pg, sv=sv: e.matmul(
                            pg, lhsT=sv[:, 0, kc, jj * 128:(jj + 1) * 128], rhs=hT[:, kc, :],
                            start=(kc == 0), stop=(kc == 7)), [sbuf] + B_hT, [pgb])
                    for kc in range(8):
                        S.op("pe", lambda e, kc=kc, jj=jj, pu=pu, sv=sv: e.matmul(
                            pu, lhsT=sv[:, 1, kc, jj * 128:(jj + 1) * 128], rhs=hT[:, kc, :],
                            start=(kc == 0), stop=(kc == 7)), [sbuf] + B_hT, [pub])
                    tmp, tb = R_tmp.next()
                    S.op("act", lambda e, pg=pg, tmp=tmp: e.activation(out=tmp, in_=pg, func=AF.Silu), [pgb], [tb])
                    S.op("dve", lambda e, j=j, pu=pu, tmp=tmp: e.tensor_tensor(out=gT[:, j, :], in0=pu, in1=tmp,
                                                                               op=ALU.mult), [pub, tb], [B_gT[j]])
            for half in range(2):
                for b in range(NB):
                    po, pob = R_pO.next()
                    for j in range(22):
                        S.op("pe", lambda e, j=j, b=b, half=half, po=po: e.matmul(
                            po, lhsT=gT[:, j, b * 128:(b + 1) * 128], rhs=wdn[:, j, half * 512:(half + 1) * 512],
                            start=(j == 0), stop=(j == 21)), [B_gT[j], B_wdn], [pob])
                    S.op("dve", lambda e, b=b, half=half, po=po: e.scalar_tensor_tensor(
                        out=xt[:, b, half * 512:(half + 1) * 512], in0=po, scalar=0.5,
                        in1=xt[:, b, half * 512:(half + 1) * 512], op0=ALU.mult, op1=ALU.add),
                        [pob, B_x[b]], [B_x[b]])

        def mixout(t0):
            if fz:
                ti_ = t0 // T
                if ti_ == 0:
                    arecv, yrecv = fz["arecv"], fz["yrecv"]
                    self.amine = self.dscr("amine", [ntiles * 4, 128 * 512], BF16)
                    self.ymine = self.dscr("ymine", [ntiles * 4, 384 * 512], BF16)
                    self.B_amine, self.B_ymine = Buf("amine"), Buf("ymine")

                    def cp_a(e):
                        self._rank = e.partition_id() % 4
                        src = arecv.rearrange("(r i) h p t -> r (i h) (p t)", i=ntiles)[bass.ds(self._rank, 1)]
                        return e.dma_start(out=self.amine, in_=src.rearrange("o q n -> (o q) n"))

                    def cp_y(e):
                        src = yrecv.rearrange("(r i) h j p t -> r (i h j) (p t)", i=ntiles // 2)[bass.ds(self._rank, 1)]
                        return e.dma_start(out=self.ymine, in_=src.rearrange("o q n -> (o q) n"))
                    S.dma_fn("sp", cp_a, [], [self.B_amine], self.B_amine)
                    S.dma_fn("sp", cp_y, [], [self.B_ymine], self.B_ymine)
                S.dma("sp", attnT_sb, self.amine.rearrange("(i h) (p t) -> i p h t", h=4, t=512)[ti_],
                      [self.B_amine], [B_attn], B_attn)
                yv = self.ymine.rearrange("(i h j) (c p t) -> i j h p c t", h=4, j=2, c=3, t=512)[ti_ // 2, ti_ % 2]
                for h in range(4):
                    S.dma("sp", yssmT_sb[:, 3 * h:3 * h + 3, :], yv[h], [self.B_ymine], [B_yssm], B_yssm)
            else:
                S.dma("sp", attnT_sb, attnT_d.rearrange("(kc p) t -> p kc t", p=128)[:, :, t0:t0 + T],
                      [], [B_attn], B_attn)
                S.dma("sp", yssmT_sb, yssmT_d.rearrange("(kc p) t -> p kc t", p=128)[:, :, t0:t0 + T],
                      [], [B_yssm], B_yssm)
            sgv = sgT_d.rearrange("(kc p) t -> p kc t", p=128)
            for i in range(4):
                sg_sb, B_sg = R_sg.next()
                S.dma("sp", sg_sb[:, 0:2, :], sgv[:, 2 * i:2 * i + 2, t0:t0 + T], [], [B_sg], B_sg)
                S.dma("sp", sg_sb[:, 2:4, :], sgv[:, 8 + 2 * i:8 + 2 * i + 2, t0:t0 + T], [], [B_sg], B_sg)
                slot, sbuf = R_w.next()
                v_sb = slot[:, 0:1024].rearrange("p (kc n) -> p kc n", kc=4)
                v_ss = slot[:, 1024:4096].rearrange("p (kc n) -> p kc n", kc=12)
                load_wtile("bsb", i, v_sb, sbuf)
                load_wtile("bssm", i, v_ss, sbuf)
                for jj in range(2):
                    n = 2 * i + jj
                    p1, p1b = R_pA.next()
                    p2, p2b = R_pB.next()
                    for kc in range(4):
                        S.op("pe", lambda e, kc=kc, jj=jj, p1=p1, v_sb=v_sb: e.matmul(
                            p1, lhsT=v_sb[:, kc, jj * 128:(jj + 1) * 128], rhs=attnT_sb[:, kc, :],
                            start=(kc == 0), stop=(kc == 3)), [sbuf, B_attn], [p1b])
                    for kc in range(12):
                        S.op("pe", lambda e, kc=kc, jj=jj, p2=p2, v_ss=v_ss: e.matmul(
                            p2, lhsT=v_ss[:, kc, jj * 128:(jj + 1) * 128], rhs=yssmT_sb[:, kc, :],
                            start=(kc == 0), stop=(kc == 11)), [sbuf, B_yssm], [p2b])
                    t1, t1b = R_tmp.next()
                    t2, t2b = R_tmp.next()
                    S.op("dve", lambda e, jj=jj, p1=p1, t1=t1, sg_sb=sg_sb: e.tensor_tensor(
                        out=t1, in0=p1, in1=sg_sb[:, jj, :], op=ALU.mult), [p1b, B_sg], [t1b])
                    S.op("dve", lambda e, jj=jj, p2=p2, t2=t2, sg_sb=sg_sb: e.tensor_tensor(
                        out=t2, in0=p2, in1=sg_sb[:, 2 + jj, :], op=ALU.mult), [p2b, B_sg], [t2b])
                    S.op("pool", lambda e, n=n, t1=t1, t2=t2: e.tensor_tensor(out=mT[:, n, :], in0=t1, in1=t2,
                                                                              op=ALU.add), [t1b, t2b], [B_mT[n]])
            for b in range(NB):
                for half in range(2):
                    po, pob = R_pO.next()
                    for kc in range(8):
                        S.op("pe", lambda e, kc=kc, b=b, half=half, po=po: e.matmul(
                            po, lhsT=mT[:, kc, b * 128:(b + 1) * 128], rhs=wout_sb[:, kc, half * 512:(half + 1) * 512],
                            start=(kc == 0), stop=(kc == 7)), [B_mT[kc], B_wout], [pob])
                    S.op("dve", lambda e, b=b, half=half, po=po: e.tensor_tensor(
                        out=xt[:, b, half * 512:(half + 1) * 512], in0=po,
                        in1=xt[:, b, half * 512:(half + 1) * 512], op=ALU.add), [pob, B_x[b]], [B_x[b]])

        def proj(t0):
            for b in range(NB):
                S.dma("pool", x_out[t0 + b * 128:t0 + (b + 1) * 128, :], xt[:, b, :], [B_x[b]], [B_dram_out], B_x[b])
            rmsnorm_hT("mix_gain")
            fm = []
            if fz:
                ti_ = t0 // T
                hs = fz["hsend"][ti_]
                S.dma("pool", hs, hT, B_hT, [B_hsend], B_hsend)
                S.coll(hs.rearrange("p k t -> p (k t)"), fz["hrecv"][ti_].rearrange("r p k t -> (r p) (k t)"),
                       fz["groups"], [B_hsend], [B_hrecv], B_hrecv)
            else:
                fm.append(("in_qk", 0, qT_o[0:512, t0:t0 + T], "q"))
                fm.append(("in_qk", 2, kT_o[0:512, t0:t0 + T], "c"))
                for i in range(7):
                    fm.append(("in_xbc", 2 * i, xbcT_o[i * 512:(i + 1) * 512, t0:t0 + T], "c"))
            for i in range(4):
                fm.append(("in_g", 2 * i, sgT_o[i * 512:(i + 1) * 512, t0:t0 + T], "g"))
            for (wkey, ti, oap, kind) in fm:
                slot, sbuf = R_w.next()
                sv = slot.rearrange("p (two kc n) -> p two kc n", two=2, kc=8)
                load_wtile(wkey, ti, sv[:, 0], sbuf)
                load_wtile(wkey, ti + 1, sv[:, 1], sbuf)
                ost, ostb = R_ost.next()
                for c in range(4):
                    pa, pab = (R_pA if c % 2 == 0 else R_pB).next()
                    for kc in range(8):
                        S.op("pe", lambda e, kc=kc, c=c, pa=pa, sv=sv: e.matmul(
                            pa, lhsT=sv[:, c // 2, kc, (c % 2) * 128:(c % 2 + 1) * 128], rhs=hT[:, kc, :],
                            start=(kc == 0), stop=(kc == 7)), [sbuf] + B_hT, [pab])
                    if kind == "g":
                        S.op("act", lambda e, c=c, pa=pa, ost=ost: e.activation(out=ost[:, c, :], in_=pa,
                                                                                func=AF.Sigmoid), [pab], [ostb])
                    else:
                        sc = (128.0 ** -0.5) if kind == "q" else 1.0
                        S.op("dve", lambda e, c=c, pa=pa, ost=ost, sc=sc: e.tensor_scalar(
                            out=ost[:, c, :], in0=pa, scalar1=sc, scalar2=None, op0=ALU.mult), [pab], [ostb])
                S.dma("pool", oap.rearrange("(c p) t -> p c t", p=128), ost, [ostb], [B_dram_out], ostb)
            if fz:
                return
            tm = [("in_v", 0, v_o, 0, "c")] + [("in_z", i, sz_o, i * 512, "s") for i in range(3)]
            for (wkey, ti, oten, c0, kind) in tm:
                slot, sbuf = R_w.next()
                sv = slot.rearrange("p (kc n) -> p kc n", kc=8)
                load_wtile(wkey, ti, sv, sbuf)
                ost, ostb = R_ost.next()
                for b in range(NB):
                    po, pob = R_pO.next()
                    for kc in range(8):
                        S.op("pe", lambda e, kc=kc, b=b, po=po, sv=sv: e.matmul(
                            po, lhsT=hT[:, kc, b * 128:(b + 1) * 128], rhs=sv[:, kc, :],
                            start=(kc == 0), stop=(kc == 7)), [sbuf, B_hT[b]], [pob])
                    if kind == "s":
                        S.op("act", lambda e, b=b, po=po, ost=ost: e.activation(out=ost[:, b, :], in_=po,
                                                                                func=AF.Silu), [pob], [ostb])
                    else:
                        S.op("dve", lambda e, b=b, po=po, ost=ost: e.tensor_copy(out=ost[:, b, :], in_=po),
                             [pob], [ostb])
                S.dma("pool", oten[t0:t0 + T, c0:c0 + 512].rearrange("(b p) n -> p b n", p=128), ost,
                      [ostb], [B_dram_out], ostb)
            slot, sbuf = R_w.next()
            sv = slot[:, 0:8 * NH].rearrange("p (kc n) -> p kc n", kc=8)
            load_wtile("in_dt", 0, sv, sbuf)
            for b in range(NB):
                for kc in range(8):
                    S.op("pe", lambda e, kc=kc, b=b, sv=sv: e.matmul(
                        pS[:, b * NH:(b + 1) * NH], lhsT=hT[:, kc, b * 128:(b + 1) * 128], rhs=sv[:, kc, :],
                        start=(kc == 0), stop=(kc == 7)), [sbuf, B_hT[b]], [B_pS])
            NN = NB * NH
            dtb_f = dtb4.rearrange("p b n -> p (b n)")
            nega_f = nega4.rearrange("p b n -> p (b n)")
            S.op("dve", lambda e: e.tensor_tensor(out=dts[:, 0, :], in0=pS[:, 0:NN], in1=dtb_f, op=ALU.add),
                 [B_pS, B_dtb], [B_dts])
            S.op("act", lambda e: e.activation(out=dts[:, 1, :], in_=dts[:, 0, :], func=AF.Abs), [B_dts], [B_dts])
            S.op("act", lambda e: e.activation(out=dts[:, 2, :], in_=dts[:, 1, :], func=AF.Exp, scale=-1.0),
                 [B_dts], [B_dts])
            S.op("act", lambda e: e.activation(out=dts[:, 3, :], in_=dts[:, 2, :], func=AF.Ln, bias=1.0),
                 [B_dts], [B_dts])
            S.op("dve", lambda e: e.scalar_tensor_tensor(
                out=dtda_st[:, :, 0:NH], in0=dts[:, 0, :].rearrange("p (b n) -> p b n", b=NB), scalar=0.0,
                in1=dts[:, 3, :].rearrange("p (b n) -> p b n", b=NB), op0=ALU.max, op1=ALU.add),
                [B_dts], [B_dtda])
            S.op("dve", lambda e: e.tensor_tensor(out=dtda_st[:, :, NH:2 * NH], in0=dtda_st[:, :, 0:NH],
                                                  in1=nega4, op=ALU.mult), [B_dtda, B_nega], [B_dtda])
            S.dma("pool", dtda_o[t0:t0 + T, :].rearrange("(b p) n -> p b n", p=128), dtda_st,
                  [B_dtda], [B_dram_out], B_dtda)

        def final(t0):
            for b in range(NB):
                S.op("act", lambda e, b=b: e.activation(out=junk, in_=xt[:, b, :], func=AF.Square,
                                                       accum_out=stat[:, b:b + 1]), [B_x[b]], [B_junk, B_stat])
            S.op("dve", lambda e: e.tensor_scalar(out=stat[:, 4:8], in0=stat[:, 0:4], scalar1=1.0 / D, scalar2=EPS,
                                                  op0=ALU.mult, op1=ALU.add), [B_stat], [B_stat])
            S.op("pool", lambda e: e.tensor_tensor(out=stat[:, 8:12], in0=stat[:, 4:8], in1=neghalf, op=ALU.pow),
                 [B_stat, B_nh], [B_stat])
            for b in range(NB):
                fo, fob = R_fo.next()
                S.op("dve", lambda e, b=b, fo=fo: e.scalar_tensor_tensor(
                    out=fo, in0=xt[:, b, :], scalar=stat[:, 8 + b:9 + b], in1=fing, op0=ALU.mult, op1=ALU.mult),
                    [B_x[b], B_stat, B_fing], [fob])
                S.dma("pool", y_o[t0 + b * 128:t0 + (b + 1) * 128, :], fo, [fob], [B_dram_out], fob)

        for ti in range(ntiles):
            t0 = ti * T
            for b in range(NB):
                S.dma("sp", xt[:, b, :], x_in[t0 + b * 128:t0 + (b + 1) * 128, :], [], [B_x[b]], B_x[b])
            fi = 0
            if self.has_mixout:
                mixout(t0)
                ffn(self.ffns[fi]); fi += 1
            if fi < len(self.ffns):
                ffn(self.ffns[fi]); fi += 1
            if self.has_proj:
                proj(t0)
            if self.has_final:
                final(t0)
        S.emit(barrier=bool(fz))


class MixProg:
    def __init__(self, SEQ, do_attn=True, do_ssd=True, nc=None, fz=None, pfx=""):
        self.SEQ = SEQ
        self.do_attn = do_attn
        self.do_ssd = do_ssd
        self.fz = fz
        self.pfx = pfx
        self.nc = nc if nc is not None else bass.Bass("TRN2", target_bir_lowering=False)
        self.S = Sched(self.nc)
        self._n = 0
        self.build()

    def sb(self, shape, dt, name=None):
        self._n += 1
        return self.nc.alloc_sbuf_tensor(f"{self.pfx}{name or 'sb'}_{self._n}", list(shape), dt).ap()

    def din(self, name, shape, dt):
        return self.nc.dram_tensor(self.pfx + name, list(shape), dt, kind="ExternalInput").ap()

    def dout(self, name, shape, dt):
        return self.nc.dram_tensor(self.pfx + name, list(shape), dt, kind="ExternalOutput").ap()

    def dscr(self, name, shape, dt):
        return self.nc.dram_tensor(self.pfx + name, list(shape), dt, kind="Internal").ap()

    def build(self):
        nc, S = self.nc, self.S
        SEQ = self.SEQ
        NBLK = SEQ // 128
        NGRP = SEQ // 512
        B_dram_out = Buf("dram_out")
        fz = self.fz
        if self.do_ssd:
            convw_d = self.din("conv_w", [128, 7, 4], F32)
            convb_d = self.din("conv_b", [128, 7], F32)
            dskip_d = self.din("d_skip", [1, 6], F32)
            ssmn_d = self.din("ssm_norm", [1, 384], F32)
        NTLG = SEQ // 512
        B_scr = {k: [Buf(f"{k}{g}") for g in range(NTLG)] for k in ("xbc", "sz", "dtda")}
        if fz:
            if self.do_attn:
                xbc_d = self.dscr("xbc_scr", [896, SEQ], BF16)
                sz_d = self.dscr("sz_scr", [SEQ, 384], BF16)
                dtda_d = self.dscr("dtda_scr", [SEQ, 12], F32)
                fz["scr"] = (xbc_d, sz_d, dtda_d)
                winr_d = self.din("w_in_r", [D, 1670], F32)
                dtbr_d = self.din("dt_bias_r", [1, 6], F32)
                alogr_d = self.din("a_log_r", [1, 6], F32)
            else:
                xbc_d, sz_d, dtda_d = fz["scr"]
            ident_d, ctri_d, cmb_d = fz["ident"], fz["c_tri"], fz["c_mb"]
        else:
            qT_d = self.din("qT", [128, SEQ], BF16)
            kT_d = self.din("kT", [128, SEQ], BF16)
            v_d = self.din("v", [SEQ, 128], BF16)
            xbc_d = self.din("xbcT", [896, SEQ], BF16)
            sz_d = self.din("sz", [SEQ, 384], BF16)
            dtda_d = self.din("dtda", [SEQ, 12], F32)
            ident_d = self.din("ident", [128, 128], F32)
            ctri_d = self.din("c_tri", [128, 4, 128], F32)
            cmb_d = self.din("c_mb", [128, 4, 512], BF16)
            attnT_o = self.dout("attnT", [128, SEQ], BF16)
            yssmT_o = self.dout("yssmT", [384, SEQ], BF16)

        ident_f = self.sb([128, 128], F32, "ident_f"); B_idf = Buf("ident_f")
        ident_b = self.sb([128, 128], BF16, "ident_b"); B_idb = Buf("ident_b")
        ctri = self.sb([128, 4, 128], F32, "ctri"); B_ctri = Buf("ctri")
        ctri_b = self.sb([128, 2, 128], BF16, "ctri_b"); B_ctrib = Buf("ctri_b")
        ones_f = self.sb([128, 128], F32, "ones_f"); B_ones = Buf("ones_f")
        S.dma("sp", ident_f, ident_d, [], [B_idf], B_idf)
        S.dma("sp", ctri, ctri_d, [], [B_ctri], B_ctri)
        S.op("dve", lambda e: e.tensor_copy(out=ident_b, in_=ident_f), [B_idf], [B_idb])
        S.op("dve", lambda e: e.tensor_copy(out=ctri_b, in_=ctri[:, 0:2, :]), [B_ctri], [B_ctrib])
        S.op("pool", lambda e: e.memset(ones_f, 1.0), [], [B_ones])
        NTi, NU = ctri_b[:, 0, :], ctri_b[:, 1, :]
        SL, UI = ctri[:, 2, :], ctri[:, 3, :]
        banks = [self.nc.alloc_psum_tensor(f"{self.pfx}bank{i}", [128, 512], F32).ap() for i in range(8)]
        B_bank = [Buf(f"bank{i}") for i in range(8)]

        if self.do_attn:
            mb_b = self.sb([128, 4, 512], BF16, "mb_b"); B_mbb = Buf("mb_b")
            S.dma("sp", mb_b, cmb_d, [], [B_mbb], B_mbb)
            kT = self.sb([128, SEQ], BF16, "kT_sb"); B_kT = Buf("kT_sb")
            qT = self.sb([128, SEQ], BF16, "qT_sb"); B_qT = Buf("qT_sb")
            vs = self.sb([128, NBLK, 128], BF16, "v_sb"); B_vs = Buf("v_sb")
            nch = max(1, SEQ // 4096)
            cw = SEQ // nch
            for i in range(nch if not fz else 0):
                S.dma("sp", kT[:, i * cw:(i + 1) * cw], kT_d[:, i * cw:(i + 1) * cw], [], [B_kT], B_kT)
                S.dma("sp", qT[:, i * cw:(i + 1) * cw], qT_d[:, i * cw:(i + 1) * cw], [], [B_qT], B_qT)
                bw = NBLK // nch
                S.dma("sp", vs[:, i * bw:(i + 1) * bw, :],
                      v_d.rearrange("(j p) d -> p j d", p=128)[:, i * bw:(i + 1) * bw, :], [], [B_vs], B_vs)
            if fz:
                self.prephase(locals())
            NS = 2
            zb = [[(banks[4 * c + i], B_bank[4 * c + i]) for i in range(2)] for c in range(NS)]
            Pb = [(banks[4 * c + 2], B_bank[4 * c + 2]) for c in range(NS)]
            Ob = [(banks[4 * c + 3], B_bank[4 * c + 3]) for c in range(NS)]
            eb = [[(self.sb([128, 512], F32, f"e{c}{i}"), Buf(f"e{c}{i}")) for i in range(2)] for c in range(NS)]
            spb = [[(self.sb([128, 512], BF16, f"sp{c}{i}"), Buf(f"sp{c}{i}")) for i in range(2)] for c in range(NS)]
            E2b = [(self.sb([128, 512], F32, f"E2{c}"), Buf(f"E2{c}")) for c in range(NS)]
            Wb = [[(self.sb([128, 512], BF16, f"W{c}{i}"), Buf(f"W{c}{i}")) for i in range(2)] for c in range(NS)]
            R_ao = Ring([(self.sb([128, 512], BF16, f"ao{i}"), Buf(f"ao{i}")) for i in range(2)])
            groups = list(range(NGRP - 1, -1, -1))
            steps = [[], []]
            for idx, G in enumerate(groups):
                c = idx % NS
                for j in range(4 * G + 3, -1, -1):
                    steps[c].append((G, j, j == 4 * G + 3, j == 0))
            n_iter = max(len(s) for s in steps)

            def S1(c, t):
                G, j, first, last = steps[c][t]
                z, zbuf = zb[c][t % 2]
                diag = j >= 4 * G
                S.op("pe", lambda e: e.matmul(z, lhsT=kT[:, j * 128:(j + 1) * 128], rhs=qT[:, G * 512:(G + 1) * 512],
                                              start=True, stop=not diag), [B_kT, B_qT], [zbuf])
                if diag:
                    r = j - 4 * G
                    S.op("pe", lambda e: e.matmul(z, lhsT=ident_b, rhs=mb_b[:, r, :], start=False, stop=True),
                         [B_idb, B_mbb], [zbuf])

            def S2(c, t):
                z, zbuf = zb[c][t % 2]
                ee, ebuf = eb[c][t % 2]
                S.op("act", lambda e: e.activation(out=ee, in_=z, func=AF.Exp), [zbuf], [ebuf])

            def S3(c, t):
                ee, ebuf = eb[c][t % 2]
                sp, spbuf = spb[c][t % 2]
                S.op("act", lambda e: e.activation(out=sp, in_=ee, func=AF.Ln, bias=1.0), [ebuf], [spbuf])

            def S4(c, t):
                G, j, first, last = steps[c][t]
                P, Pbuf = Pb[c]
                sp, spbuf = spb[c][t % 2]
                if first:
                    S.op("pe", lambda e: e.matmul(P, lhsT=NTi, rhs=sp, start=True, stop=True),
                         [B_ctrib, spbuf], [Pbuf])
                else:
                    spp, sppbuf = spb[c][(t - 1) % 2]
                    S.op("pe", lambda e: e.matmul(P, lhsT=NU, rhs=spp, start=False, stop=False),
                         [B_ctrib, sppbuf], [Pbuf])
                    S.op("pe", lambda e: e.matmul(P, lhsT=NTi, rhs=sp, start=False, stop=True),
                         [B_ctrib, spbuf], [Pbuf])

            def S5(c, t):
                P, Pbuf = Pb[c]
                E2, E2buf = E2b[c]
                S.op("act", lambda e: e.activation(out=E2, in_=P, func=AF.Exp), [Pbuf], [E2buf])

            def S6(c, t):
                ee, ebuf = eb[c][t % 2]
                E2, E2buf = E2b[c]
                W, Wbuf = Wb[c][t % 2]
                S.op("dve", lambda e: e.tensor_tensor(out=W, in0=ee, in1=E2, op=ALU.mult), [ebuf, E2buf], [Wbuf])

            def S7(c, t):
                G, j, first, last = steps[c][t]
                W, Wbuf = Wb[c][t % 2]
                O, Obuf = Ob[c]
                S.op("pe", lambda e: e.matmul(O, lhsT=vs[:, j, :], rhs=W, start=first, stop=last),
                     [B_vs, Wbuf], [Obuf])
                if last:
                    ao, aobuf = R_ao.next()
                    S.op("dve", lambda e: e.tensor_copy(out=ao, in_=O), [Obuf], [aobuf])
                    if fz:
                        B_as, B_ar = fz["B_asend"], fz["B_arecv"]
                        S.dma("pool", fz["asend"][G], ao, [aobuf], [B_as], aobuf)
                        S.coll(fz["asend"][G], fz["arecv"][G].rearrange("r p t -> (r p) t"), fz["groups"],
                               [B_as], [B_ar], B_ar)
                    else:
                        S.dma("pool", attnT_o[:, G * 512:(G + 1) * 512], ao, [aobuf], [B_dram_out], aobuf)

            act = lambda c, t: 0 <= t < len(steps[c])
            for c in range(NS):
                if act(c, 0):
                    S1(c, 0)
            for t in range(n_iter + 1):
                for c in range(NS):
                    if act(c, t + 1):
                        S1(c, t + 1)
                for c in range(NS):
                    if act(c, t):
                        S2(c, t)
                for c in range(NS):
                    if act(c, t):
                        S3(c, t)
                for c in range(NS):
                    if act(c, t - 1):
                        S7(c, t - 1)
                for c in range(NS):
                    if act(c, t):
                        S4(c, t)
                for c in range(NS):
                    if act(c, t):
                        S5(c, t)
                for c in range(NS):
                    if act(c, t):
                        S6(c, t)

        if self.do_ssd:
            NTL = SEQ // 512
            cw_f = self.sb([128, 7, 4], F32, "cw_f"); B_cw = Buf("cw_f")
            cb_f = self.sb([128, 7], F32, "cb_f"); B_cb = Buf("cb_f")
            S.dma("sp", cw_f, convw_d, [], [B_cw], B_cw)
            S.dma("sp", cb_f, convb_d, [], [B_cb], B_cb)
            dg = self.sb([128, 7, 4, 128], BF16, "dg"); B_dg = Buf("dg")
            for c in range(7):
                for k in range(4):
                    S.op("dve", lambda e, c=c, k=k: e.tensor_scalar(out=dg[:, c, k, :], in0=ident_f,
                                                                    scalar1=cw_f[:, c, k:k + 1], scalar2=None,
                                                                    op0=ALU.mult), [B_idf, B_cw], [B_dg])
            dsk6 = self.sb([128, 6], F32, "dsk6"); B_dsk6 = Buf("dsk6")
            dsk = self.sb([128, 6, 64], F32, "dsk"); B_dsk = Buf("dsk")
            ssmn = self.sb([128, 384], F32, "ssmn"); B_ssmn = Buf("ssmn")
            S.dma("sp", dsk6, dskip_d.partition_broadcast(128), [], [B_dsk6], B_dsk6)
            S.dma("sp", ssmn, ssmn_d.partition_broadcast(128), [], [B_ssmn], B_ssmn)
            S.op("dve", lambda e: e.tensor_copy(out=dsk, in_=dsk6.unsqueeze(2).to_broadcast([128, 6, 64])),
                 [B_dsk6], [B_dsk])
            neghalf = self.sb([128, 2], F32, "neghalf"); B_nh = Buf("neghalf")
            S.op("pool", lambda e: e.memset(neghalf, -0.5), [], [B_nh])
            R_xin = Ring([(self.sb([128, 7, 515], BF16, f"xin{i}"), Buf(f"xin{i}")) for i in range(2)])
            R_cT = Ring([(self.sb([128, 7, 512], BF16, f"cT{i}"), Buf(f"cT{i}")) for i in range(2)])
            R_sz = Ring([(self.sb([128, 4, 384], BF16, f"szt{i}"), Buf(f"szt{i}")) for i in range(2)])
            R_dd = Ring([(self.sb([128, 4, 12], F32, f"ddt{i}"), Buf(f"ddt{i}")) for i in range(2)])
            R_yT = Ring([(self.sb([128, 3, 512], BF16, f"yTst{i}"), Buf(f"yTst{i}")) for i in range(2)])
            Sf = self.sb([128, 6, 64], F32, "Sf"); B_Sf = Buf("Sf")
            Sb = self.sb([128, 6, 64], BF16, "Sb"); B_Sb = Buf("Sb")
            S.op("pool", lambda e: e.memset(Sf, 0.0), [], [B_Sf])
            S.op("pool", lambda e: e.memset(Sb, 0.0), [], [B_Sb])
            mk = lambda shape, dt, nm: (self.sb(shape, dt, nm), Buf(nm))
            R_tok = Ring([mk([128, 640], BF16, f"tok{i}") for i in range(2)])
            R_E18 = Ring([mk([128, 18], F32, f"E18{i}") for i in range(2)])
            R_rhsD = Ring([mk([128, 6, 128], F32, f"rhsD{i}") for i in range(1)])
            R_L = Ring([mk([128, 6, 128], F32, f"L{i}") for i in range(1)])
            R_CBm = Ring([mk([128, 2, 128], F32, f"CBm{i}") for i in range(2)])
            R_M = Ring([mk([128, 6, 128], BF16, f"M{i}") for i in range(2)])
            R_xdt = Ring([mk([128, 6, 64], BF16, f"xdt{i}") for i in range(2)])
            R_xw = Ring([mk([128, 6, 64], BF16, f"xw{i}") for i in range(2)])
            R_y = Ring([mk([128, 384], F32, f"yy{i}") for i in range(6)])
            R_st = Ring([mk([128, 8], F32, f"rst{i}") for i in range(2)])
            R_yn = Ring([mk([128, 384], BF16, f"yn{i}") for i in range(2)])
            junk = self.sb([128, 192], F32, "junk_s"); B_junk = Buf("junk_s")
            pc, B_pc = banks[0], B_bank[0]
            ptr, B_ptr = banks[1].bitcast(BF16), B_bank[1]
            pm, B_pm = banks[2], B_bank[2]
            pD = [(banks[3], B_bank[3]), (banks[4], B_bank[4])]
            pY, B_pY = banks[5], B_bank[5]
            pI, B_pI = banks[6], B_bank[6]
            pSt, B_pSt = banks[7], B_bank[7]

            for tt in range(NTL):
                t0 = tt * 512
                xin, B_xin = R_xin.next()
                if tt == 0:
                    S.op("pool", lambda e, xin=xin: e.memset(xin[:, :, 0:3], 0.0), [], [B_xin])
                    S.dma("sp", xin[:, :, 3:515], xbc_d.rearrange("(c p) t -> p c t", p=128)[:, :, 0:512],
                          [B_scr["xbc"][0]], [B_xin], B_xin)
                else:
                    S.dma("sp", xin, xbc_d.rearrange("(c p) t -> p c t", p=128)[:, :, t0 - 3:t0 + 512],
                          [B_scr["xbc"][tt - 1], B_scr["xbc"][tt]], [B_xin], B_xin)
                szt, B_szt = R_sz.next()
                ddt, B_ddt = R_dd.next()
                S.dma("sp", szt, sz_d[t0:t0 + 512, :].rearrange("(b p) n -> p b n", p=128), [B_scr["sz"][tt]],
                      [B_szt], B_szt)
                S.dma("sp", ddt, dtda_d[t0:t0 + 512, :].rearrange("(b p) n -> p b n", p=128), [B_scr["dtda"][tt]],
                      [B_ddt], B_ddt)
                cT, B_cT = R_cT.next()
                for c in range(7):
                    for k in range(4):
                        S.op("pe", lambda e, c=c, k=k, xin=xin: e.matmul(pc, lhsT=dg[:, c, k, :], rhs=xin[:, c, k:k + 512],
                                                                         start=(k == 0), stop=(k == 3)),
                             [B_dg, B_xin], [B_pc])
                    S.op("act", lambda e, c=c, cT=cT: e.activation(out=cT[:, c, :], in_=pc, func=AF.Silu,
                                                                   bias=cb_f[:, c:c + 1]), [B_pc, B_cb], [B_cT])
                yT, B_yT = R_yT.next()
                for ch in range(4):
                    cs = slice(ch * 128, (ch + 1) * 128)
                    dt6 = ddt[:, ch, 0:6]
                    dA6 = ddt[:, ch, 6:12]
                    for i in range(5):
                        S.op("pe", lambda e, i=i, cT=cT, cs=cs: e.transpose(ptr[:, i * 128:(i + 1) * 128], cT[:, i, cs], ident_b),
                             [B_cT, B_idb], [B_ptr])
                    tok, B_tok = R_tok.next()
                    S.op("dve", lambda e, tok=tok: e.tensor_copy(out=tok, in_=ptr[:, 0:640]), [B_ptr], [B_tok])
                    S.op("pe", lambda e, dA6=dA6: e.matmul(pm[:, 256:262], lhsT=UI, rhs=dA6, start=True, stop=True),
                         [B_ctri, B_ddt], [B_pm])
                    S.op("pe", lambda e, dA6=dA6: e.matmul(pm[:, 262:268], lhsT=ones_f, rhs=dA6, start=True, stop=True),
                         [B_ones, B_ddt], [B_pm])
                    S.op("pe", lambda e, dA6=dA6: e.matmul(pm[:, 268:274], lhsT=SL, rhs=dA6, start=True, stop=True),
                         [B_ctri, B_ddt], [B_pm])
                    for g in range(2):
                        S.op("pe", lambda e, g=g, cT=cT, cs=cs: e.matmul(pm[:, g * 128:(g + 1) * 128], lhsT=cT[:, 3 + g, cs],
                                                                         rhs=cT[:, 5 + g, cs], start=True, stop=True),
                             [B_cT], [B_pm])
                    E18, B_E18 = R_E18.next()
                    S.op("act", lambda e, E18=E18: e.activation(out=E18, in_=pm[:, 256:274], func=AF.Exp), [B_pm], [B_E18])
                    CBm, B_CBm = R_CBm.next()
                    S.op("dve", lambda e, CBm=CBm: e.tensor_tensor(
                        out=CBm, in0=pm[:, 0:256].rearrange("p (g q) -> p g q", g=2),
                        in1=UI.unsqueeze(1).to_broadcast([128, 2, 128]), op=ALU.mult), [B_pm, B_ctri], [B_CBm])
                    rhsD, B_rhsD = R_rhsD.next()
                    S.op("dve", lambda e, rhsD=rhsD, dA6=dA6: e.tensor_tensor(
                        out=rhsD, in0=UI.unsqueeze(1).to_broadcast([128, 6, 128]),
                        in1=dA6.unsqueeze(2).to_broadcast([128, 6, 128]), op=ALU.mult), [B_ctri, B_ddt], [B_rhsD])
                    L, B_L = R_L.next()
                    for g in range(2):
                        pDg, B_pDg = pD[g]
                        S.op("pe", lambda e, g=g, pDg=pDg, rhsD=rhsD: e.matmul(
                            pDg[:, 0:384], lhsT=SL, rhs=rhsD[:, 3 * g:3 * g + 3, :].rearrange("p h q -> p (h q)"),
                            start=True, stop=True), [B_ctri, B_rhsD], [B_pDg])
                        S.op("act", lambda e, g=g, pDg=pDg, L=L: e.activation(
                            out=L[:, 3 * g:3 * g + 3, :].rearrange("p h q -> p (h q)"), in_=pDg[:, 0:384], func=AF.Exp),
                            [B_pDg], [B_L])
                    M, B_M = R_M.next()
                    for g in range(2):
                        S.op("dve", lambda e, g=g, M=M, L=L, CBm=CBm: e.tensor_tensor(
                            out=M[:, 3 * g:3 * g + 3, :], in0=L[:, 3 * g:3 * g + 3, :],
                            in1=CBm[:, g, :].unsqueeze(1).to_broadcast([128, 3, 128]), op=ALU.mult),
                            [B_L, B_CBm], [B_M])
                    xdt, B_xdt = R_xdt.next()
                    S.op("dve", lambda e, xdt=xdt, tok=tok, dt6=dt6: e.tensor_tensor(
                        out=xdt, in0=tok[:, 0:384].rearrange("p (h d) -> p h d", h=6),
                        in1=dt6.unsqueeze(2).to_broadcast([128, 6, 64]), op=ALU.mult), [B_tok, B_ddt], [B_xdt])
                    for hh in range(6):
                        S.op("pe", lambda e, hh=hh, M=M, xdt=xdt: e.matmul(pY[:, hh * 64:(hh + 1) * 64], lhsT=M[:, hh, :],
                                                                           rhs=xdt[:, hh, :], start=True, stop=True),
                             [B_M, B_xdt], [B_pY])
                    for g in range(2):
                        S.op("pe", lambda e, g=g, cT=cT, cs=cs: e.matmul(
                            pI[:, g * 192:(g + 1) * 192], lhsT=cT[:, 5 + g, cs],
                            rhs=Sb[:, 3 * g:3 * g + 3, :].rearrange("p h d -> p (h d)"), start=True, stop=True),
                            [B_cT, B_Sb], [B_pI])
                    xw, B_xw = R_xw.next()
                    S.op("dve", lambda e, xw=xw, xdt=xdt, E18=E18: e.tensor_tensor(
                        out=xw, in0=xdt, in1=E18[:, 12:18].unsqueeze(2).to_broadcast([128, 6, 64]), op=ALU.mult),
                        [B_xdt, B_E18], [B_xw])
                    for g in range(2):
                        S.op("pe", lambda e, g=g, tok=tok, xw=xw: e.matmul(
                            pSt[:, g * 192:(g + 1) * 192], lhsT=tok[:, 384 + g * 128:384 + (g + 1) * 128],
                            rhs=xw[:, 3 * g:3 * g + 3, :].rearrange("p h d -> p (h d)"), start=True, stop=True),
                            [B_tok, B_xw], [B_pSt])
                    S.op("dve", lambda e, E18=E18: e.tensor_tensor(
                        out=Sf, in0=Sf, in1=E18[:, 6:12].unsqueeze(2).to_broadcast([128, 6, 64]), op=ALU.mult),
                        [B_Sf, B_E18], [B_Sf])
                    S.op("dve", lambda e: e.tensor_tensor(out=Sf.rearrange("p h d -> p (h d)"),
                                                          in0=Sf.rearrange("p h d -> p (h d)"), in1=pSt[:, 0:384],
                                                          op=ALU.add), [B_Sf, B_pSt], [B_Sf])
                    S.op("act", lambda e: e.copy(out=Sb, in_=Sf), [B_Sf], [B_Sb])
                    y1, B_y1 = R_y.next()
                    S.op("dve", lambda e, y1=y1, E18=E18: e.tensor_tensor(
                        out=y1.rearrange("p (h d) -> p h d", h=6), in0=pI[:, 0:384].rearrange("p (h d) -> p h d", h=6),
                        in1=E18[:, 0:6].unsqueeze(2).to_broadcast([128, 6, 64]), op=ALU.mult), [B_pI, B_E18], [B_y1])
                    y2, B_y2 = R_y.next()
                    S.op("dve", lambda e, y1=y1, y2=y2: e.tensor_tensor(out=y2, in0=pY[:, 0:384], in1=y1, op=ALU.add),
                         [B_pY, B_y1], [B_y2])
                    t3, B_t3 = R_y.next()
                    S.op("pool", lambda e, t3=t3, tok=tok: e.tensor_tensor(
                        out=t3, in0=tok[:, 0:384], in1=dsk.rearrange("p h d -> p (h d)"), op=ALU.mult),
                        [B_tok, B_dsk], [B_t3])
                    y3, B_y3 = R_y.next()
                    S.op("pool", lambda e, y3=y3, y2=y2, t3=t3: e.tensor_tensor(out=y3, in0=y2, in1=t3, op=ALU.add),
                         [B_y2, B_t3], [B_y3])
                    y4, B_y4 = R_y.next()
                    S.op("dve", lambda e, y4=y4, y3=y3, szt=szt, ch=ch: e.tensor_tensor(out=y4, in0=y3, in1=szt[:, ch, :],
                                                                                      op=ALU.mult), [B_y3, B_szt], [B_y4])
                    rst, B_rst = R_st.next()
                    for g in range(2):
                        S.op("act", lambda e, g=g, y4=y4, rst=rst: e.activation(
                            out=junk, in_=y4[:, g * 192:(g + 1) * 192], func=AF.Square, accum_out=rst[:, g:g + 1]),
                            [B_y4], [B_junk, B_rst])
                    S.op("dve", lambda e, rst=rst: e.tensor_scalar(out=rst[:, 2:4], in0=rst[:, 0:2], scalar1=1.0 / 192,
                                                                   scalar2=EPS, op0=ALU.mult, op1=ALU.add),
                         [B_rst], [B_rst])
                    S.op("pool", lambda e, rst=rst: e.tensor_tensor(out=rst[:, 4:6], in0=rst[:, 2:4], in1=neghalf,
                                                                    op=ALU.pow), [B_rst, B_nh], [B_rst])
                    yn, B_yn = R_yn.next()
                    for g in range(2):
                        S.op("dve", lambda e, g=g, yn=yn, y4=y4, rst=rst: e.scalar_tensor_tensor(
                            out=yn[:, g * 192:(g + 1) * 192], in0=y4[:, g * 192:(g + 1) * 192],
                            scalar=rst[:, 4 + g:5 + g], in1=ssmn[:, g * 192:(g + 1) * 192], op0=ALU.mult, op1=ALU.mult),
                            [B_y4, B_rst, B_ssmn], [B_yn])
                    for i in range(3):
                        S.op("pe", lambda e, i=i, yn=yn: e.transpose(ptr[:, 640 + i * 128:640 + (i + 1) * 128],
                                                                     yn[:, i * 128:(i + 1) * 128], ident_b),
                             [B_yn, B_idb], [B_ptr])
                    S.op("dve", lambda e, yT=yT, cs=cs: e.tensor_copy(
                        out=yT[:, :, cs], in_=ptr[:, 640:1024].rearrange("p (c t) -> p c t", c=3)), [B_ptr], [B_yT])
                if fz:
                    B_ys, B_yr = fz["B_ysend"], fz["B_yrecv"]
                    S.dma("pool", fz["ysend"][tt // 2, tt % 2].rearrange("(c p) t -> p c t", p=128), yT, [B_yT], [B_ys],
                          B_yT)
                    if tt % 2 == 1:
                        S.coll(fz["ysend"][tt // 2].rearrange("j p t -> (j p) t"),
                               fz["yrecv"][tt // 2].rearrange("r j p t -> (r j p) t"), fz["groups"],
                               [B_ys], [B_yr], B_yr)
                else:
                    S.dma("pool", yssmT_o.rearrange("(c p) t -> p c t", p=128)[:, :, t0:t0 + 512], yT, [B_yT],
                          [B_dram_out], B_yT)
        S.emit(barrier=bool(fz))

    def prephase(self, L):
        S, fz, SEQ = self.S, self.fz, self.SEQ
        qT, kT, vs = L["qT"], L["kT"], L["vs"]
        B_qT, B_kT, B_vs = L["B_qT"], L["B_kT"], L["B_vs"]
        banks, B_bank, B_scr = L["banks"], L["B_bank"], L["B_scr"]
        xbc_d, sz_d, dtda_d = L["xbc_d"], L["sz_d"], L["dtda_d"]
        NTLG = SEQ // 512
        ntl = NTLG // 4
        wscr = self.dscr("w_in_r_bf", [128, 8, 1670], BF16)
        B_wscr = Buf("w_in_r_bf")
        src = L["winr_d"].rearrange("(kc p) n -> kc p n", p=128)
        for kc in range(8):
            S.dma("pool", wscr[:, kc, :], src[kc], [], [B_wscr], B_wscr, max_dma_last_dim=4096)
        wsl = self.sb([128, 8, 1670], BF16, "wsl"); B_wsl = Buf("wsl")
        S.dma("sp", wsl, wscr, [B_wscr], [B_wsl], B_wsl)
        dtb = self.sb([128, 4, 6], F32, "dtb"); B_dtb = Buf("dtb")
        nega = self.sb([128, 4, 6], F32, "nega"); B_nega = Buf("nega")
        for b in range(4):
            S.dma("sp", dtb[:, b, :], L["dtbr_d"].partition_broadcast(128), [], [B_dtb], B_dtb)
            S.dma("sp", nega[:, b, :], L["alogr_d"].partition_broadcast(128), [], [B_nega], B_nega)
        S.op("act", lambda e: e.activation(out=nega, in_=nega, func=AF.Exp), [B_nega], [B_nega])
        S.op("dve", lambda e: e.tensor_scalar(out=nega, in0=nega, scalar1=-1.0, scalar2=None, op0=ALU.mult),
             [B_nega], [B_nega])
        R_h = Ring([(self.sb([128, 8, 512], BF16, f"hTt{i}"), Buf(f"hTt{i}")) for i in range(2)])
        R_xs = Ring([(self.sb([128, 7, 512], BF16, f"xst{i}"), Buf(f"xst{i}")) for i in range(2)])
        R_zs = Ring([(self.sb([128, 4, 384], BF16, f"zst{i}"), Buf(f"zst{i}")) for i in range(2)])
        R_ds = Ring([(self.sb([128, 4, 12], F32, f"dst{i}"), Buf(f"dst{i}")) for i in range(2)])
        dts = self.sb([128, 4, 24], F32, "dts_p"); B_dts = Buf("dts_p")
        R_pf = Ring([(banks[i], B_bank[i]) for i in range(4)])
        pv, B_pv = banks[4], B_bank[4]
        R_pz = Ring([(banks[5], B_bank[5]), (banks[6], B_bank[6])])
        pS, B_pS = banks[7], B_bank[7]
        for g in range(NTLG):
            rank, lt = g // ntl, g % ntl
            hTt, B_h = R_h.next()
            S.dma("sp", hTt, fz["hrecv"][lt][rank], [], [B_h], B_h)
            gs = slice(g * 512, (g + 1) * 512)

            def fmm(c0, hTt=hTt, B_h=B_h):
                p, pb = R_pf.next()
                for kc in range(8):
                    S.op("pe", lambda e, kc=kc, p=p: e.matmul(p, lhsT=wsl[:, kc, c0:c0 + 128], rhs=hTt[:, kc, :],
                                                              start=(kc == 0), stop=(kc == 7)), [B_wsl, B_h], [pb])
                return p, pb
            p, pb = fmm(0)
            S.op("dve", lambda e, p=p, gs=gs: e.tensor_scalar(out=qT[:, gs], in0=p, scalar1=128.0 ** -0.5, scalar2=None,
                                                              op0=ALU.mult), [pb], [B_qT])
            p, pb = fmm(128)
            S.op("act", lambda e, p=p, gs=gs: e.copy(out=kT[:, gs], in_=p), [pb], [B_kT])
            xst, B_xst = R_xs.next()
            for c in range(7):
                p, pb = fmm(256 + c * 128)
                if c % 2 == 0:
                    S.op("dve", lambda e, p=p, c=c, xst=xst: e.tensor_copy(out=xst[:, c, :], in_=p), [pb], [B_xst])
                else:
                    S.op("act", lambda e, p=p, c=c, xst=xst: e.copy(out=xst[:, c, :], in_=p), [pb], [B_xst])
            S.dma("pool", xbc_d.rearrange("(c p) t -> p c t", p=128)[:, :, gs], xst, [B_xst], [B_scr["xbc"][g]], B_xst)
            for b in range(4):
                for kc in range(8):
                    S.op("pe", lambda e, kc=kc, b=b, hTt=hTt: e.matmul(
                        pv[:, b * 128:(b + 1) * 128], lhsT=hTt[:, kc, b * 128:(b + 1) * 128], rhs=wsl[:, kc, 1152:1280],
                        start=(kc == 0), stop=(kc == 7)), [B_wsl, B_h], [B_pv])
            S.op("dve", lambda e, g=g: e.tensor_copy(out=vs[:, g * 4:(g + 1) * 4, :],
                                                     in_=pv.rearrange("p (b d) -> p b d", b=4)), [B_pv], [B_vs])
            zst, B_zst = R_zs.next()
            for b in range(4):
                pz, B_pz = R_pz.next()
                for kc in range(8):
                    S.op("pe", lambda e, kc=kc, b=b, hTt=hTt, pz=pz: e.matmul(
                        pz[:, 0:384], lhsT=hTt[:, kc, b * 128:(b + 1) * 128], rhs=wsl[:, kc, 1280:1664],
                        start=(kc == 0), stop=(kc == 7)), [B_wsl, B_h], [B_pz])
                S.op("act", lambda e, b=b, pz=pz, zst=zst: e.activation(out=zst[:, b, :], in_=pz[:, 0:384], func=AF.Silu),
                     [B_pz], [B_zst])
            S.dma("pool", sz_d[g * 512:(g + 1) * 512, :].rearrange("(b p) n -> p b n", p=128), zst, [B_zst],
                  [B_scr["sz"][g]], B_zst)
            for b in range(4):
                for kc in range(8):
                    S.op("pe", lambda e, kc=kc, b=b, hTt=hTt: e.matmul(
                        pS[:, b * 6:(b + 1) * 6], lhsT=hTt[:, kc, b * 128:(b + 1) * 128], rhs=wsl[:, kc, 1664:1670],
                        start=(kc == 0), stop=(kc == 7)), [B_wsl, B_h], [B_pS])
            dst, B_dst = R_ds.next()
            dtb_f = dtb.rearrange("p b n -> p (b n)")
            S.op("dve", lambda e: e.tensor_tensor(out=dts[:, 0, :], in0=pS[:, 0:24], in1=dtb_f, op=ALU.add),
                 [B_pS, B_dtb], [B_dts])
            S.op("act", lambda e: e.activation(out=dts[:, 1, :], in_=dts[:, 0, :], func=AF.Abs), [B_dts], [B_dts])
            S.op("act", lambda e: e.activation(out=dts[:, 2, :], in_=dts[:, 1, :], func=AF.Exp, scale=-1.0),
                 [B_dts], [B_dts])
            S.op("act", lambda e: e.activation(out=dts[:, 3, :], in_=dts[:, 2, :], func=AF.Ln, bias=1.0),
                 [B_dts], [B_dts])
            S.op("dve", lambda e, dst=dst: e.scalar_tensor_tensor(
                out=dst[:, :, 0:6], in0=dts[:, 0, :].rearrange("p (b n) -> p b n", b=4), scalar=0.0,
                in1=dts[:, 3, :].rearrange("p (b n) -> p b n", b=4), op0=ALU.max, op1=ALU.add), [B_dts], [B_dst])
            S.op("dve", lambda e, dst=dst: e.tensor_tensor(out=dst[:, :, 6:12], in0=dst[:, :, 0:6], in1=nega,
                                                           op=ALU.mult), [B_dst, B_nega], [B_dst])
            S.dma("pool", dtda_d[g * 512:(g + 1) * 512, :].rearrange("(b p) n -> p b n", p=128), dst, [B_dst],
                  [B_scr["dtda"][g]], B_dst)


class FusedProg:
    def __init__(self, NT, SEQ, nph=99):
        nc = self.nc = bass.Bass("TRN2", target_bir_lowering=False)
        ntl = NT // T
        NGRP = SEQ // 512
        groups = [[0, 1, 2, 3], [4, 5, 6, 7]]
        ext = lambda n, s, d: nc.dram_tensor(n, list(s), d, kind="ExternalInput").ap()
        itn = lambda n, s, d: nc.dram_tensor(n, list(s), d, kind="Internal").ap()
        ident = ext("ident", [128, 128], F32)
        c_tri = ext("c_tri", [128, 4, 128], F32)
        c_mb = ext("c_mb", [128, 4, 512], BF16)
        L = []
        for l in range(2):
            L.append(dict(
                xres=itn(f"xres{l}", [NT, D], F32), sg=itn(f"sg{l}", [2 * D, NT], BF16),
                hsend=itn(f"hsend{l}", [ntl, 128, 8, T], BF16), hrecv=itn(f"hrecv{l}", [ntl, 4, 128, 8, T], BF16),
                asend=itn(f"asend{l}", [NGRP, 128, 512], BF16), arecv=itn(f"arecv{l}", [NGRP, 4, 128, 512], BF16),
                ysend=itn(f"ysend{l}", [NGRP // 2, 2, 384, 512], BF16),
                yrecv=itn(f"yrecv{l}", [NGRP // 2, 4, 2, 384, 512], BF16)))
        base = dict(ident=ident, c_tri=c_tri, c_mb=c_mb, groups=groups)
        self.n_ops = 0

        self._ph = 0

        def tok(stage, pfx, **kw):
            self._ph += 1
            if self._ph > nph:
                return
            fz = dict(base); fz.update(kw)
            with nc.cleanup_on_exit():
                p = TokProg(stage, NT, nc=nc, fz=fz, pfx=pfx)
            self.n_ops += len(p.S.ops)

        def mix(l, pa, ps):
            fz = dict(base)
            fz.update(hrecv=L[l]["hrecv"], asend=L[l]["asend"], arecv=L[l]["arecv"], ysend=L[l]["ysend"],
                      yrecv=L[l]["yrecv"], B_asend=Buf("asend"), B_arecv=Buf("arecv"), B_ysend=Buf("ysend"),
                      B_yrecv=Buf("yrecv"))
            self._ph += 1
            if self._ph <= nph:
                with nc.cleanup_on_exit():
                    p = MixProg(SEQ, True, False, nc=nc, fz=fz, pfx=pa)
                self.n_ops += len(p.S.ops)
            self._ph += 1
            if self._ph <= nph:
                with nc.cleanup_on_exit():
                    p = MixProg(SEQ, False, True, nc=nc, fz=fz, pfx=ps)
                self.n_ops += len(p.S.ops)

        tok("A0", "p0_", x_src=None, x_dst=L[0]["xres"], sg_dst=L[0]["sg"], hsend=L[0]["hsend"], hrecv=L[0]["hrecv"])
        mix(0, "p1_", "p2_")
        tok("CA", "p3_", x_src=L[0]["xres"], sg_src=L[0]["sg"], arecv=L[0]["arecv"], yrecv=L[0]["yrecv"],
            x_dst=L[1]["xres"], sg_dst=L[1]["sg"], hsend=L[1]["hsend"], hrecv=L[1]["hrecv"])
        mix(1, "p4_", "p5_")
        tok("C1", "p6_", x_src=L[1]["xres"], sg_src=L[1]["sg"], arecv=L[1]["arecv"], yrecv=L[1]["yrecv"])
        if nph < 7:
            dbg = nc.dram_tensor("dbg", [128, 128], F32, kind="ExternalOutput").ap()
            with nc.semaphore("dbgsem") as dsem, nc.Block() as block:
                @block.sync
                def _(e):
                    e.dma_start(out=dbg, in_=ident).then_inc(dsem, 16)
                    e.wait_ge(dsem, 16)


_PROGS = {}


def _get_prog(kind, *args):
    key = (kind,) + args
    if key not in _PROGS:
        if kind == "tok":
            _PROGS[key] = TokProg(*args)
        else:
            _PROGS[key] = MixProg(*args)
    return _PROGS[key]


def _gain_t(g):
    return np.ascontiguousarray(np.asarray(g, np.float32).reshape(8, 128).T)


def _c(a, dt=np.float32):
    return np.ascontiguousarray(np.asarray(a, dt))


def run_tok(stage, NT, per_core, shared):
    prog = _get_prog("tok", stage, NT)
    ident = np.eye(128, dtype=np.float32)
    in_maps = []
    for c in range(NCORES):
        m = dict(shared)
        m.update(per_core[c])
        m["ident"] = ident
        in_maps.append(m)
    res = run_bass_kernel_spmd(prog.nc, in_maps, core_ids=list(range(NCORES)))
    return res.results


def _mix_consts():
    k = np.arange(128)[:, None]
    l = np.arange(128)[None, :]
    ctri = np.zeros((128, 4, 128), np.float32)
    ctri[:, 0, :] = -1.0 * (k >= l)
    ctri[:, 1, :] = -1.0 * (k < l)
    ctri[:, 2, :] = (k > l)
    ctri[:, 3, :] = (k <= l)
    q = np.arange(512)[None, :]
    cmb = np.zeros((128, 4, 512), np.float32)
    for r in range(4):
        cmb[:, r, :] = np.where(r * 128 + k < q, 0.0, -30000.0)
    return ctri, cmb


def run_mix(SEQ, per_core, do_attn=True, do_ssd=True):
    prog = _get_prog("mix", SEQ, do_attn, do_ssd)
    ctri, cmb = _mix_consts()
    ident = np.eye(128, dtype=np.float32)
    in_maps = []
    for c in range(NCORES):
        m = dict(per_core[c])
        m["ident"] = ident
        m["c_tri"] = ctri
        m["c_mb"] = cmb.astype(NPBF)
        in_maps.append(m)
    res = run_bass_kernel_spmd(prog.nc, in_maps, core_ids=list(range(NCORES)))
    return res.results


def _tok_shared_ffn(tag, p, name, l):
    return {f"f{tag}_gain": _gain_t(p[f"{name}_norm"][l]), f"f{tag}_up": _c(p[f"{name}_w_up"][l]),
            f"f{tag}_dn": _c(p[f"{name}_w_down"][l])}


def _tok_shared_proj(p, l):
    return {"mix_gain": _gain_t(p["mix_norm"][l]), "w_in": _c(p["w_in"][l]),
            "dt_bias": _c(p["dt_bias"][l].reshape(1, NH)), "a_log": _c(p["a_log"][l].reshape(1, NH))}


def _tok_shared_mixout(p, l):
    return {"w_bsb": _c(p["w_branch_sb"][l]), "w_bssm": _c(p["w_branch_ssm"][l]), "w_out": _c(p["w_out"][l])}


def _mix_inputs(tok_res, p, l, B, SEQ):
    cpb = NCORES // B
    per_core = []
    for b in range(B):
        cs = range(b * cpb, (b + 1) * cpb)
        qT = np.concatenate([tok_res[c]["qT"] for c in cs], axis=1)
        kT = np.concatenate([tok_res[c]["kT"] for c in cs], axis=1)
        v = np.concatenate([tok_res[c]["v"] for c in cs], axis=0)
        sz = np.concatenate([tok_res[c]["sz"] for c in cs], axis=0)
        xbcT = np.concatenate([tok_res[c]["xbcT"] for c in cs], axis=1)
        dtda = np.concatenate([tok_res[c]["dtda"] for c in cs], axis=0)
        for r in range(4):
            chs = np.concatenate([np.arange(384 * r, 384 * r + 384), DI + np.arange(256 * r, 256 * r + 256),
                                  DI + 1024 + np.arange(256 * r, 256 * r + 256)])
            cw = np.asarray(p["conv_w"][l], np.float32)[:, chs]
            cb = np.asarray(p["conv_b"][l], np.float32)[chs]
            m = {
                "qT": np.ascontiguousarray(qT[128 * r:128 * (r + 1)]),
                "kT": np.ascontiguousarray(kT[128 * r:128 * (r + 1)]),
                "v": np.ascontiguousarray(v[:, 128 * r:128 * (r + 1)]),
                "xbcT": np.ascontiguousarray(xbcT[chs]),
                "conv_w": _c(cw.T.reshape(7, 128, 4).transpose(1, 0, 2)),
                "conv_b": _c(cb.reshape(7, 128).T),
                "sz": np.ascontiguousarray(sz[:, 384 * r:384 * (r + 1)]),
                "dtda": np.ascontiguousarray(np.concatenate([dtda[:, 6 * r:6 * r + 6],
                                                             dtda[:, NH + 6 * r:NH + 6 * r + 6]], axis=1)),
                "d_skip": _c(p["d_skip"][l][6 * r:6 * r + 6].reshape(1, 6)),
                "ssm_norm": _c(p["ssm_norm"][l][384 * r:384 * r + 384].reshape(1, 384)),
            }
            per_core.append(m)
    return per_core


def _mixout_inputs(mix_res, tok_res, B, NT):
    cpb = NCORES // B
    per_core = []
    for b in range(B):
        attnT = np.concatenate([mix_res[b * 4 + r]["attnT"] for r in range(4)], axis=0)
        yssmT = np.concatenate([mix_res[b * 4 + r]["yssmT"] for r in range(4)], axis=0)
        for i in range(cpb):
            c = b * cpb + i
            per_core.append({
                "x": tok_res[c]["x_out"],
                "attnT": np.ascontiguousarray(attnT[:, i * NT:(i + 1) * NT]),
                "yssmT": np.ascontiguousarray(yssmT[:, i * NT:(i + 1) * NT]),
                "sgT": tok_res[c]["sgT_out"],
            })
    return per_core


def forward(x, p):
    x = np.asarray(x, np.float32)
    B, SEQ, _ = x.shape
    NT = B * SEQ // NCORES
    flat = x.reshape(B * SEQ, D)
    per_core = [{"x": _c(flat[c * NT:(c + 1) * NT])} for c in range(NCORES)]
    shared = {}
    shared.update(_tok_shared_ffn("a", p, "ffn1", 0))
    shared.update(_tok_shared_proj(p, 0))
    tok = run_tok("A0", NT, per_core, shared)
    mix = run_mix(SEQ, _mix_inputs(tok, p, 0, B, SEQ))
    per_core = _mixout_inputs(mix, tok, B, NT)
    shared = {}
    shared.update(_tok_shared_mixout(p, 0))
    shared.update(_tok_shared_ffn("a", p, "ffn2", 0))
    shared.update(_tok_shared_ffn("b", p, "ffn1", 1))
    shared.update(_tok_shared_proj(p, 1))
    tok = run_tok("CA", NT, per_core, shared)
    mix = run_mix(SEQ, _mix_inputs(tok, p, 1, B, SEQ))
    per_core = _mixout_inputs(mix, tok, B, NT)
    shared = {}
    shared.update(_tok_shared_mixout(p, 1))
    shared.update(_tok_shared_ffn("a", p, "ffn2", 1))
    shared["fin_gain"] = _c(np.asarray(p["final_norm"]).reshape(1, D))
    fin = run_tok("C1", NT, per_core, shared)
    y = np.concatenate([np.asarray(fin[c]["y"], np.float32) for c in range(NCORES)], axis=0)
    return y.reshape(B, SEQ, D)


def kernel(x, ffn1_norm, ffn1_w_up, ffn1_w_down, mix_norm, w_in, conv_w, conv_b, dt_bias, a_log, d_skip,
           ssm_norm, w_branch_sb, w_branch_ssm, w_out, ffn2_norm, ffn2_w_up, ffn2_w_down, final_norm):
    p = dict(ffn1_norm=ffn1_norm, ffn1_w_up=ffn1_w_up, ffn1_w_down=ffn1_w_down, mix_norm=mix_norm, w_in=w_in,
             conv_w=conv_w, conv_b=conv_b, dt_bias=dt_bias, a_log=a_log, d_skip=d_skip, ssm_norm=ssm_norm,
             w_branch_sb=w_branch_sb, w_branch_ssm=w_branch_ssm, w_out=w_out, ffn2_norm=ffn2_norm,
             ffn2_w_up=ffn2_w_up, ffn2_w_down=ffn2_w_down, final_norm=final_norm)
    p = {k: np.asarray(v, np.float32) for k, v in p.items()}
    if FUSED:
        return forward_fused(x, p)
    return forward(x, p)


def _w_in_r(w_in, r):
    cols = np.concatenate([
        np.arange(128 * r, 128 * r + 128), 512 + np.arange(128 * r, 128 * r + 128),
        C_XBC + np.arange(384 * r, 384 * r + 384), C_XBC + DI + np.arange(256 * r, 256 * r + 256),
        C_XBC + DI + 1024 + np.arange(256 * r, 256 * r + 256),
        C_V + np.arange(128 * r, 128 * r + 128), C_Z + np.arange(384 * r, 384 * r + 384),
        C_DT + np.arange(6 * r, 6 * r + 6)])
    return np.ascontiguousarray(np.asarray(w_in, np.float32)[:, cols])


def forward_fused(x, p, trace=False):
    x = np.asarray(x, np.float32)
    B, SEQ, _ = x.shape
    NT = B * SEQ // NCORES
    key = ("fused", NT, SEQ)
    if key not in _PROGS:
        _PROGS[key] = FusedProg(NT, SEQ)
    prog = _PROGS[key]
    flat = x.reshape(B * SEQ, D)
    ctri, cmb = _mix_consts()
    sh = {"ident": np.eye(128, dtype=np.float32), "c_tri": ctri, "c_mb": cmb.astype(NPBF)}

    def pf(pfx, d):
        return {pfx + k: v for k, v in d.items()}
    sh.update(pf("p0_", _tok_shared_ffn("a", p, "ffn1", 0)))
    sh.update({"p0_mix_gain": _gain_t(p["mix_norm"][0]), "p0_w_in_g": _c(p["w_in"][0][:, C_GSB:C_GSB + 2 * D])})
    sh.update(pf("p3_", _tok_shared_mixout(p, 0)))
    sh.update(pf("p3_", _tok_shared_ffn("a", p, "ffn2", 0)))
    sh.update(pf("p3_", _tok_shared_ffn("b", p, "ffn1", 1)))
    sh.update({"p3_mix_gain": _gain_t(p["mix_norm"][1]), "p3_w_in_g": _c(p["w_in"][1][:, C_GSB:C_GSB + 2 * D])})
    sh.update(pf("p6_", _tok_shared_mixout(p, 1)))
    sh.update(pf("p6_", _tok_shared_ffn("a", p, "ffn2", 1)))
    sh["p6_fin_gain"] = _c(np.asarray(p["final_norm"]).reshape(1, D))
    per_r = []
    for r in range(4):
        d = {}
        for l, (pa, ps) in enumerate([("p1_", "p2_"), ("p4_", "p5_")]):
            chs = np.concatenate([np.arange(384 * r, 384 * r + 384), DI + np.arange(256 * r, 256 * r + 256),
                                  DI + 1024 + np.arange(256 * r, 256 * r + 256)])
            cw = np.asarray(p["conv_w"][l], np.float32)[:, chs]
            cb = np.asarray(p["conv_b"][l], np.float32)[chs]
            d[pa + "w_in_r"] = _w_in_r(p["w_in"][l], r)
            d[pa + "dt_bias_r"] = _c(p["dt_bias"][l][6 * r:6 * r + 6].reshape(1, 6))
            d[pa + "a_log_r"] = _c(p["a_log"][l][6 * r:6 * r + 6].reshape(1, 6))
            d[ps + "conv_w"] = _c(cw.T.reshape(7, 128, 4).transpose(1, 0, 2))
            d[ps + "conv_b"] = _c(cb.reshape(7, 128).T)
            d[ps + "d_skip"] = _c(p["d_skip"][l][6 * r:6 * r + 6].reshape(1, 6))
            d[ps + "ssm_norm"] = _c(p["ssm_norm"][l][384 * r:384 * r + 384].reshape(1, 384))
        per_r.append(d)
    in_maps = []
    for c in range(NCORES):
        m = dict(sh)
        m.update(per_r[c % 4])
        m["p0_x"] = _c(flat[c * NT:(c + 1) * NT])
        in_maps.append(m)
    res = run_bass_kernel_spmd(prog.nc, in_maps, core_ids=list(range(NCORES)), trace=trace)
    y = np.concatenate([np.asarray(res.results[c]["p6_y"], np.float32) for c in range(NCORES)], axis=0)
    if trace:
        print("exec_time_ns", res.exec_time_ns)
    return y.reshape(B, SEQ, D)
```

```python
import numpy as np
import ml_dtypes
from contextlib import ExitStack
import concourse.bass as bass
import concourse.mybir as mybir
from concourse.bass_utils import run_bass_kernel_spmd

F32 = mybir.dt.float32
BF16 = mybir.dt.bfloat16
AF = mybir.ActivationFunctionType
ALU = mybir.AluOpType
AX = mybir.AxisListType
NPBF = ml_dtypes.bfloat16

D = 1024
FH = 2816
SBW = 512
DI = 1536
CD = 3584
NH = 24
NG = 8
INW = 8728
EPS = 1e-6
NCORES = 8

SEM_LIMIT = 30000
FUSED = True


class Buf:
    __slots__ = ("name", "writers", "readers", "sem", "total")

    def __init__(self, name):
        self.name = name
        self.writers = []
        self.readers = []
        self.sem = None
        self.total = 0


class Op:
    __slots__ = ("eng", "fn", "is_dma", "deps", "signal", "sig_idx", "sem_buf", "sem_val", "inc")

    def __init__(self, eng, fn, is_dma):
        self.eng = eng
        self.fn = fn
        self.is_dma = is_dma
        self.deps = []
        self.signal = False
        self.sig_idx = None
        self.sem_buf = None
        self.sem_val = None
        self.inc = 16


class Sched:
    COMPUTE = ("pe", "act", "dve", "pool")

    def __init__(self, nc, same_engine_sync=True):
        self.nc = nc
        self.ops = []
        self.same_engine_sync = same_engine_sync
        self.dma_bufs = []
        self.n_sems = 0

    def _joins(self, b, op):
        return (b.writers and not b.readers and (op.is_dma or op.eng == "pe") and
                all(w.eng == op.eng and w.is_dma == op.is_dma for w in b.writers))

    def _add(self, op, reads, writes):
        deps = []
        for b in reads:
            deps.extend(b.writers)
        jn = [self._joins(b, op) for b in writes]
        for b, j in zip(writes, jn):
            if not j:
                deps.extend(b.writers)
                deps.extend(b.readers)
        seen = set()
        for d in deps:
            if d is op or id(d) in seen:
                continue
            seen.add(id(d))
            if (not d.is_dma) and (not op.is_dma) and d.eng == op.eng:
                if op.eng == "pe" or not self.same_engine_sync:
                    continue
            op.deps.append(d)
            if not d.is_dma:
                d.signal = True
        for b, j in zip(writes, jn):
            if j:
                b.writers.append(op)
            else:
                b.writers = [op]
                b.readers = []
        for b in reads:
            if b not in writes:
                b.readers.append(op)
        self.ops.append(op)
        return op

    def op(self, eng, fn, reads=(), writes=()):
        return self._add(Op(eng, fn, False), list(reads), list(writes))

    def dma(self, q, out, in_, reads, writes, sem_of, **kw):
        def fn(e, out=out, in_=in_, kw=kw):
            return e.dma_start(out=out, in_=in_, **kw)
        return self.dma_fn(q, fn, reads, writes, sem_of)

    def dma_fn(self, q, fn, reads, writes, sem_of, inc=16):
        o = Op(q, fn, True)
        o.sem_buf = sem_of
        o.inc = inc
        if sem_of.total == 0 and sem_of not in self.dma_bufs:
            self.dma_bufs.append(sem_of)
        sem_of.total += inc
        o.sem_val = sem_of.total
        return self._add(o, list(reads), list(writes))

    def coll(self, ins_ap, outs_ap, groups, reads, writes, sem_of):
        def fn(e):
            return e.collective_compute("AllGather", ALU.bypass, replica_groups=groups, ins=[ins_ap], outs=[outs_ap])
        o = self.dma_fn("pool", fn, reads, writes, sem_of, inc=1)
        prev = getattr(self, "_last_coll", None)
        if prev is not None and prev not in o.deps:
            o.deps.append(prev)
        self._last_coll = o
        return o

    def emit(self, barrier=False):
        nc = self.nc
        engs = {"pe": nc.tensor, "act": nc.scalar, "dve": nc.vector, "pool": nc.gpsimd, "sp": nc.sync}
        cnt = {e: 0 for e in self.COMPUTE}
        for o in self.ops:
            if not o.is_dma and o.signal:
                o.sig_idx = cnt[o.eng]
                cnt[o.eng] += 1
        esem = {}
        for e in self.COMPUTE:
            n = (cnt[e] + SEM_LIMIT - 1) // SEM_LIMIT
            esem[e] = [nc.alloc_semaphore(f"s_{e}{i}_{nc.next_id()}") for i in range(max(n, 1))]
        for b in self.dma_bufs:
            b.sem = nc.alloc_semaphore(f"d_{b.name}_{nc.next_id()}")
        self.n_sems = sum(len(v) for v in esem.values()) + len(self.dma_bufs)
        streams = {e: [] for e in engs}
        for o in self.ops:
            streams[o.eng].append(o)
        waited = {e: {} for e in engs}

        def target(d):
            if d.is_dma:
                return d.sem_buf.sem, d.sem_val
            i = d.sig_idx
            return esem[d.eng][i // SEM_LIMIT], i % SEM_LIMIT + 1

        def emit_stream(ename):
            e = engs[ename]
            w = waited[ename]
            for o in streams[ename]:
                need = {}
                for d in o.deps:
                    sm, v = target(d)
                    k = id(sm)
                    if w.get(k, 0) >= v:
                        continue
                    if k not in need or need[k][1] < v:
                        need[k] = (sm, v)
                for k, (sm, v) in need.items():
                    e.wait_ge(sm, v)
                    w[k] = v
                ins = o.fn(e)
                if o.is_dma:
                    ins.then_inc(o.sem_buf.sem, o.inc)
                elif o.signal:
                    sm, _ = target(o)
                    ins.then_inc(sm, 1)
            if ename == "sp" or barrier:
                for b in self.dma_bufs:
                    e.wait_ge(b.sem, b.total)
            if barrier:
                for en in self.COMPUTE:
                    if cnt[en] > 0:
                        i = cnt[en] - 1
                        e.wait_ge(esem[en][i // SEM_LIMIT], i % SEM_LIMIT + 1)
                e.drain()

        with nc.Block() as block:
            @block.sync
            def _(eng):
                emit_stream("sp")

            @block.tensor
            def _(eng):
                emit_stream("pe")

            @block.scalar
            def _(eng):
                emit_stream("act")

            @block.vector
            def _(eng):
                emit_stream("dve")

            @block.gpsimd
            def _(eng):
                emit_stream("pool")
        if barrier:
            nc.all_engine_barrier()


class Ring:
    def __init__(self, items):
        self.items = items
        self.i = 0

    def next(self):
        it = self.items[self.i % len(self.items)]
        self.i += 1
        return it


T = 512
NB = T // 128

C_Q, C_K, C_V, C_Z, C_XBC, C_DT, C_GSB, C_GSSM = 0, 512, 1024, 1536, 3072, 6656, 6680, 7704


class TokProg:
    def __init__(self, stage, NT, nc=None, fz=None, pfx=""):
        self.stage = stage
        self.NT = NT
        self.fz = fz
        self.pfx = pfx
        self.has_mixout = stage in ("CA", "C1")
        self.ffns = {"A0": ["a"], "CA": ["a", "b"], "C1": ["a"]}[stage]
        self.has_proj = stage in ("A0", "CA")
        self.has_final = stage == "C1"
        self.nc = nc = nc if nc is not None else bass.Bass("TRN2", target_bir_lowering=False)
        self.S = Sched(nc)
        self._n = 0
        self.build()

    def sb(self, shape, dt, name=None):
        self._n += 1
        return self.nc.alloc_sbuf_tensor(f"{self.pfx}{name or 'sb'}_{self._n}", list(shape), dt).ap()

    def ps(self, shape, dt, name=None):
        self._n += 1
        return self.nc.alloc_psum_tensor(f"{self.pfx}{name or 'ps'}_{self._n}", list(shape), dt).ap()

    def din(self, name, shape, dt):
        return self.nc.dram_tensor(self.pfx + name, list(shape), dt, kind="ExternalInput").ap()

    def dout(self, name, shape, dt):
        return self.nc.dram_tensor(self.pfx + name, list(shape), dt, kind="ExternalOutput").ap()

    def dscr(self, name, shape, dt):
        return self.nc.dram_tensor(self.pfx + name, list(shape), dt, kind="Internal").ap()

    def cast_weight(self, name, w, K, c0, ntiles, ncols):
        KC = K // 128
        scr = self.dscr(name + "_bf", [ntiles, 128, KC, ncols], BF16)
        b = Buf(name + "_bf")
        src = w.rearrange("(kc p) n -> kc p n", p=128)
        for kc in range(KC):
            s = src[kc][:, c0:c0 + ntiles * ncols].rearrange("p (nt nn) -> p nt nn", nn=ncols)
            d = scr[:, :, kc, :].rearrange("nt p nn -> p nt nn")
            self.S.dma("pool", d, s, [], [b], b, max_dma_last_dim=4096)
        return scr, b

    def build(self):
        nc, S = self.nc, self.S
        NT = self.NT
        ntiles = NT // T
        fz = self.fz
        x_in = fz["x_src"] if fz and fz.get("x_src") is not None else self.din("x", [NT, D], F32)
        ident_d = fz["ident"] if fz else self.din("ident", [128, 128], F32)
        B_dram_out = Buf("dram_out")
        W = {}
        if self.has_mixout:
            if fz:
                sgT_d = fz["sg_src"]
            else:
                attnT_d = self.din("attnT", [SBW, NT], BF16)
                yssmT_d = self.din("yssmT", [DI, NT], BF16)
                sgT_d = self.din("sgT", [2 * D, NT], BF16)
            w_bsb = self.din("w_bsb", [SBW, D], F32)
            w_bssm = self.din("w_bssm", [DI, D], F32)
            w_out = self.din("w_out", [D, D], F32)
            W["bsb"] = self.cast_weight("w_bsb", w_bsb, SBW, 0, 4, 256)
            W["bssm"] = self.cast_weight("w_bssm", w_bssm, DI, 0, 4, 256)
            W["out"] = self.cast_weight("w_out", w_out, D, 0, 1, 1024)
        for k in self.ffns:
            g = self.din(f"f{k}_gain", [128, 8], F32)
            up = self.din(f"f{k}_up", [D, 2 * FH], F32)
            dn = self.din(f"f{k}_dn", [FH, D], F32)
            W[f"f{k}_gain"] = g
            W[f"f{k}_up"] = self.cast_weight(f"f{k}_up", up, D, 0, 22, 256)
            W[f"f{k}_dn"] = self.cast_weight(f"f{k}_dn", dn, FH, 0, 1, 1024)
        if self.has_proj:
            W["mix_gain"] = self.din("mix_gain", [128, 8], F32)
            if fz:
                w_in_g = self.din("w_in_g", [D, 2 * D], F32)
                W["in_g"] = self.cast_weight("w_in_g", w_in_g, D, 0, 8, 256)
                x_out = fz["x_dst"]
                sgT_o = fz["sg_dst"]
            else:
                w_in = self.din("w_in", [D, INW], F32)
                dt_bias_d = self.din("dt_bias", [1, NH], F32)
                a_log_d = self.din("a_log", [1, NH], F32)
                W["in_g"] = self.cast_weight("w_in_g", w_in, D, C_GSB, 8, 256)
                W["in_qk"] = self.cast_weight("w_in_qk", w_in, D, C_Q, 4, 256)
                W["in_v"] = self.cast_weight("w_in_v", w_in, D, C_V, 1, 512)
                W["in_z"] = self.cast_weight("w_in_z", w_in, D, C_Z, 3, 512)
                W["in_xbc"] = self.cast_weight("w_in_xbc", w_in, D, C_XBC, 14, 256)
                W["in_dt"] = self.cast_weight("w_in_dt", w_in, D, C_DT, 1, NH)
                x_out = self.dout("x_out", [NT, D], F32)
                qT_o = self.dout("qT", [SBW, NT], BF16)
                kT_o = self.dout("kT", [SBW, NT], BF16)
                v_o = self.dout("v", [NT, SBW], BF16)
                sz_o = self.dout("sz", [NT, DI], BF16)
                xbcT_o = self.dout("xbcT", [CD, NT], BF16)
                dtda_o = self.dout("dtda", [NT, 2 * NH], F32)
                sgT_o = self.dout("sgT_out", [2 * D, NT], BF16)
        if self.has_final:
            fin_gain_d = self.din("fin_gain", [1, D], F32)
            y_o = self.dout("y", [NT, D], F32)

        n_xt = 2 if self.stage == "A0" else 1
        R_x = Ring([(self.sb([128, NB, D], F32, f"xt{i}"), [Buf(f"x{i}_{b}") for b in range(NB)]) for i in range(n_xt)])
        self._x = {}
        junk = self.sb([128, D], F32, "junk"); B_junk = Buf("junk")
        hn = [self.sb([128, D], BF16, f"hn{i}") for i in range(2)]
        R_hn = Ring([(hn[i], Buf(f"hn{i}")) for i in range(2)])
        hT = self.sb([128, 8, T], BF16, "hT")
        B_hT = [Buf(f"hT{b}") for b in range(NB)]
        gT = self.sb([128, 22, T], BF16, "gT")
        B_gT = [Buf(f"gT{j}") for j in range(22)]
        wdn = self.sb([128, 22, D], BF16, "wdn"); B_wdn = Buf("wdn")
        NSLOT = 4
        R_w = Ring([(self.sb([128, 4096], BF16, f"wslot{i}"), Buf(f"wslot{i}")) for i in range(NSLOT)])
        R_tmp = Ring([(self.sb([128, T], F32, f"tmp{i}"), Buf(f"tmp{i}")) for i in range(4)])
        stat = self.sb([128, 16], F32, "stat"); B_stat = Buf("stat")
        neghalf = self.sb([128, 4], F32, "neghalf"); B_nh = Buf("neghalf")
        ident_f = self.sb([128, 128], F32, "ident_f"); B_idf = Buf("ident_f")
        ident_b = self.sb([128, 128], BF16, "ident_b"); B_idb = Buf("ident_b")
        gains = {}
        ptr = self.ps([128, 1024], BF16, "ptr"); B_ptr = Buf("ptr")
        R_pA = Ring([(self.ps([128, 512], F32, f"pA{i}"), Buf(f"pA{i}")) for i in range(2)])
        R_pB = Ring([(self.ps([128, 512], F32, f"pB{i}"), Buf(f"pB{i}")) for i in range(2)])
        R_pO = Ring([(self.ps([128, 512], F32, f"pO{i}"), Buf(f"pO{i}")) for i in range(2)])
        pS = self.ps([128, 512], F32, "pS"); B_pS = Buf("pS")

        S.dma("sp", ident_f, ident_d, [], [B_idf], B_idf)
        S.op("dve", lambda e: e.tensor_copy(out=ident_b, in_=ident_f), [B_idf], [B_idb])
        S.op("pool", lambda e: e.memset(neghalf, -0.5), [], [B_nh])
        for key in ([f"f{k}_gain" for k in self.ffns] + (["mix_gain"] if self.has_proj else [])):
            gt = self.sb([128, 8], F32, "g_" + key)
            b = Buf("g_" + key)
            S.dma("sp", gt, W[key], [], [b], b)
            gains[key] = (gt, b)
        if self.has_mixout:
            wout_sb = self.sb([128, 8, D], BF16, "wout_sb"); B_wout = Buf("wout_sb")
            S.dma("sp", wout_sb, W["out"][0][0], [W["out"][1]], [B_wout], B_wout)
            attnT_sb = self.sb([128, 4, T], BF16, "attnT_sb"); B_attn = Buf("attnT_sb")
            yssmT_sb = self.sb([128, 12, T], BF16, "yssmT_sb"); B_yssm = Buf("yssmT_sb")
            R_sg = Ring([(self.sb([128, 4, T], BF16, f"sg_sb{i}"), Buf(f"sg_sb{i}")) for i in range(2)])
            mT = self.sb([128, 8, T], BF16, "mT"); B_mT = [Buf(f"mT{n}") for n in range(8)]
        if self.has_proj:
            R_ost = Ring([(self.sb([128, 4, T], BF16, f"ost{i}"), Buf(f"ost{i}")) for i in range(3)])
        if self.has_proj and fz:
            B_hsend = Buf("hsend")
            B_hrecv = Buf("hrecv")
        if self.has_proj and not fz:
            dtb4 = self.sb([128, NB, NH], F32, "dtb4"); B_dtb = Buf("dtb4")
            nega4 = self.sb([128, NB, NH], F32, "nega4"); B_nega = Buf("nega4")
            for b in range(NB):
                S.dma("sp", dtb4[:, b, :], dt_bias_d.partition_broadcast(128), [], [B_dtb], B_dtb)
                S.dma("sp", nega4[:, b, :], a_log_d.partition_broadcast(128), [], [B_nega], B_nega)
            S.op("act", lambda e: e.activation(out=nega4, in_=nega4, func=AF.Exp), [B_nega], [B_nega])
            S.op("dve", lambda e: e.tensor_scalar(out=nega4, in0=nega4, scalar1=-1.0, scalar2=None, op0=ALU.mult),
                 [B_nega], [B_nega])
            dts = self.sb([128, 4, NB * NH], F32, "dts"); B_dts = Buf("dts")
            dtda_st = self.sb([128, NB, 2 * NH], F32, "dtda_st"); B_dtda = Buf("dtda_st")
        if self.has_final:
            fing = self.sb([128, D], F32, "fing"); B_fing = Buf("fing")
            S.dma("sp", fing, fin_gain_d.partition_broadcast(128), [], [B_fing], B_fing)
            R_fo = Ring([(self.sb([128, D], F32, f"fo{i}"), Buf(f"fo{i}")) for i in range(2)])

        def load_wtile(key, ti, dst_view, slotbuf):
            scr, b = W[key]
            S.dma("sp", dst_view, scr[ti], [b], [slotbuf], slotbuf)

        def rmsnorm_hT(gain_key):
            gt, gb = gains[gain_key]
            for b in range(NB):
                S.op("act", lambda e, b=b, xt=xt: e.activation(out=junk, in_=xt[:, b, :], func=AF.Square,
                                                       accum_out=stat[:, b:b + 1]),
                     [B_x[b]], [B_junk, B_stat])
            S.op("dve", lambda e: e.tensor_scalar(out=stat[:, 4:8], in0=stat[:, 0:4], scalar1=1.0 / D, scalar2=EPS,
                                                  op0=ALU.mult, op1=ALU.add), [B_stat], [B_stat])
            S.op("pool", lambda e: e.tensor_tensor(out=stat[:, 8:12], in0=stat[:, 4:8], in1=neghalf, op=ALU.pow),
                 [B_stat, B_nh], [B_stat])
            for b in range(NB):
                h, hb = R_hn.next()
                S.op("act", lambda e, b=b, h=h, xt=xt: e.activation(out=h, in_=xt[:, b, :], func=AF.Copy,
                                                             scale=stat[:, 8 + b:9 + b]),
                     [B_x[b], B_stat], [hb])
                for kc in range(8):
                    S.op("pe", lambda e, kc=kc, h=h: e.transpose(ptr[:, kc * 128:(kc + 1) * 128],
                                                                 h[:, kc * 128:(kc + 1) * 128], ident_b),
                         [hb, B_idb], [B_ptr])
                S.op("dve", lambda e, b=b: e.tensor_tensor(
                    out=hT[:, :, b * 128:(b + 1) * 128], in0=ptr.rearrange("p (k t) -> p k t", k=8),
                    in1=gt.unsqueeze(2).to_broadcast([128, 8, 128]), op=ALU.mult),
                    [B_ptr, gb], [B_hT[b]])

        def ffn(k):
            rmsnorm_hT(f"f{k}_gain")
            scr, b = W[f"f{k}_dn"]
            S.dma("sp", wdn[:, 0:11, :], scr[0][:, 0:11, :], [b], [B_wdn], B_wdn)
            S.dma("sp", wdn[:, 11:22, :], scr[0][:, 11:22, :], [b], [B_wdn], B_wdn)
            for i in range(11):
                slot, sbuf = R_w.next()
                sv = slot.rearrange("p (two kc n) -> p two kc n", two=2, kc=8)
                load_wtile(f"f{k}_up", i, sv[:, 0], sbuf)
                load_wtile(f"f{k}_up", 11 + i, sv[:, 1], sbuf)
                for jj in range(2):
                    j = 2 * i + jj
                    pg, pgb = R_pA.next()
                    pu, pub = R_pB.next()
                    for kc in range(8):
                        S.op("pe", lambda e, kc=kc, jj=jj, pg=pg, sv=sv: e.matmul(
                            pg, lhsT=sv[:, 0, kc, jj * 128:(jj + 1) * 128], rhs=hT[:, kc, :],
                            start=(kc == 0), stop=(kc == 7)), [sbuf] + B_hT, [pgb])
                    for kc in range(8):
                        S.op("pe", lambda e, kc=kc, jj=jj, pu=pu, sv=sv: e.matmul(
                            pu, lhsT=sv[:, 1, kc, jj * 128:(jj + 1) * 128], rhs=hT[:, kc, :],
                            start=(kc == 0), stop=(kc == 7)), [sbuf] + B_hT, [pub])
                    tmp, tb = R_tmp.next()
                    S.op("act", lambda e, pg=pg, tmp=tmp: e.activation(out=tmp, in_=pg, func=AF.Silu), [pgb], [tb])
                    S.op("dve", lambda e, j=j, pu=pu, tmp=tmp: e.tensor_tensor(out=gT[:, j, :], in0=pu, in1=tmp,
                                                                               op=ALU.mult), [pub, tb], [B_gT[j]])
            for half in range(2):
                for b in range(NB):
                    po, pob = R_pO.next()
                    for j in range(22):
                        S.op("pe", lambda e, j=j, b=b, half=half, po=po: e.matmul(
                            po, lhsT=gT[:, j, b * 128:(b + 1) * 128], rhs=wdn[:, j, half * 512:(half + 1) * 512],
                            start=(j == 0), stop=(j == 21)), [B_gT[j], B_wdn], [pob])
                    S.op("dve", lambda e, b=b, half=half, po=po, xt=xt: e.scalar_tensor_tensor(
                        out=xt[:, b, half * 512:(half + 1) * 512], in0=po, scalar=0.5,
                        in1=xt[:, b, half * 512:(half + 1) * 512], op0=ALU.mult, op1=ALU.add),
                        [pob, B_x[b]], [B_x[b]])

        def mixout(t0):
            if fz:
                ti_ = t0 // T
                if ti_ == 0:
                    arecv, yrecv = fz["arecv"], fz["yrecv"]
                    self.amine = self.dscr("amine", [ntiles * 4, 128 * 512], BF16)
                    self.ymine = self.dscr("ymine", [ntiles * 4, 384 * 512], BF16)
                    self.B_amine, self.B_ymine = Buf("amine"), Buf("ymine")

                    def cp_a(e):
                        self._rank = e.partition_id() % 4
                        src = arecv.rearrange("(r i) h p t -> r (i h) (p t)", i=ntiles)[bass.ds(self._rank, 1)]
                        return e.dma_start(out=self.amine, in_=src.rearrange("o q n -> (o q) n"))

                    def cp_y(e):
                        src = yrecv.rearrange("(r i) h j p t -> r (i h j) (p t)", i=ntiles // 2)[bass.ds(self._rank, 1)]
                        return e.dma_start(out=self.ymine, in_=src.rearrange("o q n -> (o q) n"))
                    S.dma_fn("sp", cp_a, [], [self.B_amine], self.B_amine)
                    S.dma_fn("sp", cp_y, [], [self.B_ymine], self.B_ymine)
                S.dma("sp", attnT_sb, self.amine.rearrange("(i h) (p t) -> i p h t", h=4, t=512)[ti_],
                      [self.B_amine], [B_attn], B_attn)
                yv = self.ymine.rearrange("(i h j) (c p t) -> i j h p c t", h=4, j=2, c=3, t=512)[ti_ // 2, ti_ % 2]
                for h in range(4):
                    S.dma("sp", yssmT_sb[:, 3 * h:3 * h + 3, :], yv[h], [self.B_ymine], [B_yssm], B_yssm)
            else:
                S.dma("sp", attnT_sb, attnT_d.rearrange("(kc p) t -> p kc t", p=128)[:, :, t0:t0 + T],
                      [], [B_attn], B_attn)
                S.dma("sp", yssmT_sb, yssmT_d.rearrange("(kc p) t -> p kc t", p=128)[:, :, t0:t0 + T],
                      [], [B_yssm], B_yssm)
            sgv = sgT_d.rearrange("(kc p) t -> p kc t", p=128)
            for i in range(4):
                sg_sb, B_sg = R_sg.next()
                S.dma("sp", sg_sb[:, 0:2, :], sgv[:, 2 * i:2 * i + 2, t0:t0 + T], [], [B_sg], B_sg)
                S.dma("sp", sg_sb[:, 2:4, :], sgv[:, 8 + 2 * i:8 + 2 * i + 2, t0:t0 + T], [], [B_sg], B_sg)
                slot, sbuf = R_w.next()
                v_sb = slot[:, 0:1024].rearrange("p (kc n) -> p kc n", kc=4)
                v_ss = slot[:, 1024:4096].rearrange("p (kc n) -> p kc n", kc=12)
                load_wtile("bsb", i, v_sb, sbuf)
                load_wtile("bssm", i, v_ss, sbuf)
                for jj in range(2):
                    n = 2 * i + jj
                    p1, p1b = R_pA.next()
                    p2, p2b = R_pB.next()
                    for kc in range(4):
                        S.op("pe", lambda e, kc=kc, jj=jj, p1=p1, v_sb=v_sb: e.matmul(
                            p1, lhsT=v_sb[:, kc, jj * 128:(jj + 1) * 128], rhs=attnT_sb[:, kc, :],
                            start=(kc == 0), stop=(kc == 3)), [sbuf, B_attn], [p1b])
                    for kc in range(12):
                        S.op("pe", lambda e, kc=kc, jj=jj, p2=p2, v_ss=v_ss: e.matmul(
                            p2, lhsT=v_ss[:, kc, jj * 128:(jj + 1) * 128], rhs=yssmT_sb[:, kc, :],
                            start=(kc == 0), stop=(kc == 11)), [sbuf, B_yssm], [p2b])
                    t1, t1b = R_tmp.next()
                    t2, t2b = R_tmp.next()
                    S.op("dve", lambda e, jj=jj, p1=p1, t1=t1, sg_sb=sg_sb: e.tensor_tensor(
                        out=t1, in0=p1, in1=sg_sb[:, jj, :], op=ALU.mult), [p1b, B_sg], [t1b])
                    S.op("dve", lambda e, jj=jj, p2=p2, t2=t2, sg_sb=sg_sb: e.tensor_tensor(
                        out=t2, in0=p2, in1=sg_sb[:, 2 + jj, :], op=ALU.mult), [p2b, B_sg], [t2b])
                    S.op("pool", lambda e, n=n, t1=t1, t2=t2: e.tensor_tensor(out=mT[:, n, :], in0=t1, in1=t2,
                                                                              op=ALU.add), [t1b, t2b], [B_mT[n]])
            for b in range(NB):
                for half in range(2):
                    po, pob = R_pO.next()
                    for kc in range(8):
                        S.op("pe", lambda e, kc=kc, b=b, half=half, po=po: e.matmul(
                            po, lhsT=mT[:, kc, b * 128:(b + 1) * 128], rhs=wout_sb[:, kc, half * 512:(half + 1) * 512],
                            start=(kc == 0), stop=(kc == 7)), [B_mT[kc], B_wout], [pob])
                    S.op("dve", lambda e, b=b, half=half, po=po, xt=xt: e.tensor_tensor(
                        out=xt[:, b, half * 512:(half + 1) * 512], in0=po,
                        in1=xt[:, b, half * 512:(half + 1) * 512], op=ALU.add), [pob, B_x[b]], [B_x[b]])

        def proj(t0):
            for b in range(NB):
                S.dma("pool", x_out[t0 + b * 128:t0 + (b + 1) * 128, :], xt[:, b, :], [B_x[b]], [B_dram_out], B_x[b])
            rmsnorm_hT("mix_gain")
            fm = []
            if fz:
                ti_ = t0 // T
                hs = fz["hsend"][ti_]
                S.dma("pool", hs, hT, B_hT, [B_hsend], B_hsend)
                S.coll(hs.rearrange("p k t -> p (k t)"), fz["hrecv"][ti_].rearrange("r p k t -> (r p) (k t)"),
                       fz["groups"], [B_hsend], [B_hrecv], B_hrecv)
            else:
                fm.append(("in_qk", 0, qT_o[0:512, t0:t0 + T], "q"))
                fm.append(("in_qk", 2, kT_o[0:512, t0:t0 + T], "c"))
                for i in range(7):
                    fm.append(("in_xbc", 2 * i, xbcT_o[i * 512:(i + 1) * 512, t0:t0 + T], "c"))
            for i in range(4):
                fm.append(("in_g", 2 * i, sgT_o[i * 512:(i + 1) * 512, t0:t0 + T], "g"))
            for (wkey, ti, oap, kind) in fm:
                slot, sbuf = R_w.next()
                sv = slot.rearrange("p (two kc n) -> p two kc n", two=2, kc=8)
                load_wtile(wkey, ti, sv[:, 0], sbuf)
                load_wtile(wkey, ti + 1, sv[:, 1], sbuf)
                ost, ostb = R_ost.next()
                for c in range(4):
                    pa, pab = (R_pA if c % 2 == 0 else R_pB).next()
                    for kc in range(8):
                        S.op("pe", lambda e, kc=kc, c=c, pa=pa, sv=sv: e.matmul(
                            pa, lhsT=sv[:, c // 2, kc, (c % 2) * 128:(c % 2 + 1) * 128], rhs=hT[:, kc, :],
                            start=(kc == 0), stop=(kc == 7)), [sbuf] + B_hT, [pab])
                    if kind == "g":
                        S.op("act", lambda e, c=c, pa=pa, ost=ost: e.activation(out=ost[:, c, :], in_=pa,
                                                                                func=AF.Sigmoid), [pab], [ostb])
                    else:
                        sc = (128.0 ** -0.5) if kind == "q" else 1.0
                        S.op("dve", lambda e, c=c, pa=pa, ost=ost, sc=sc: e.tensor_scalar(
                            out=ost[:, c, :], in0=pa, scalar1=sc, scalar2=None, op0=ALU.mult), [pab], [ostb])
                S.dma("pool", oap.rearrange("(c p) t -> p c t", p=128), ost, [ostb], [B_dram_out], ostb)
            if fz:
                return
            tm = [("in_v", 0, v_o, 0, "c")] + [("in_z", i, sz_o, i * 512, "s") for i in range(3)]
            for (wkey, ti, oten, c0, kind) in tm:
                slot, sbuf = R_w.next()
                sv = slot.rearrange("p (kc n) -> p kc n", kc=8)
                load_wtile(wkey, ti, sv, sbuf)
                ost, ostb = R_ost.next()
                for b in range(NB):
                    po, pob = R_pO.next()
                    for kc in range(8):
                        S.op("pe", lambda e, kc=kc, b=b, po=po, sv=sv: e.matmul(
                            po, lhsT=hT[:, kc, b * 128:(b + 1) * 128], rhs=sv[:, kc, :],
                            start=(kc == 0), stop=(kc == 7)), [sbuf, B_hT[b]], [pob])
                    if kind == "s":
                        S.op("act", lambda e, b=b, po=po, ost=ost: e.activation(out=ost[:, b, :], in_=po,
                                                                                func=AF.Silu), [pob], [ostb])
                    else:
                        S.op("dve", lambda e, b=b, po=po, ost=ost: e.tensor_copy(out=ost[:, b, :], in_=po),
                             [pob], [ostb])
                S.dma("pool", oten[t0:t0 + T, c0:c0 + 512].rearrange("(b p) n -> p b n", p=128), ost,
                      [ostb], [B_dram_out], ostb)
            slot, sbuf = R_w.next()
            sv = slot[:, 0:8 * NH].rearrange("p (kc n) -> p kc n", kc=8)
            load_wtile("in_dt", 0, sv, sbuf)
            for b in range(NB):
                for kc in range(8):
                    S.op("pe", lambda e, kc=kc, b=b, sv=sv: e.matmul(
                        pS[:, b * NH:(b + 1) * NH], lhsT=hT[:, kc, b * 128:(b + 1) * 128], rhs=sv[:, kc, :],
                        start=(kc == 0), stop=(kc == 7)), [sbuf, B_hT[b]], [B_pS])
            NN = NB * NH
            dtb_f = dtb4.rearrange("p b n -> p (b n)")
            nega_f = nega4.rearrange("p b n -> p (b n)")
            S.op("dve", lambda e: e.tensor_tensor(out=dts[:, 0, :], in0=pS[:, 0:NN], in1=dtb_f, op=ALU.add),
                 [B_pS, B_dtb], [B_dts])
            S.op("act", lambda e: e.activation(out=dts[:, 1, :], in_=dts[:, 0, :], func=AF.Abs), [B_dts], [B_dts])
            S.op("act", lambda e: e.activation(out=dts[:, 2, :], in_=dts[:, 1, :], func=AF.Exp, scale=-1.0),
                 [B_dts], [B_dts])
            S.op("act", lambda e: e.activation(out=dts[:, 3, :], in_=dts[:, 2, :], func=AF.Ln, bias=1.0),
                 [B_dts], [B_dts])
            S.op("dve", lambda e: e.scalar_tensor_tensor(
                out=dtda_st[:, :, 0:NH], in0=dts[:, 0, :].rearrange("p (b n) -> p b n", b=NB), scalar=0.0,
                in1=dts[:, 3, :].rearrange("p (b n) -> p b n", b=NB), op0=ALU.max, op1=ALU.add),
                [B_dts], [B_dtda])
            S.op("dve", lambda e: e.tensor_tensor(out=dtda_st[:, :, NH:2 * NH], in0=dtda_st[:, :, 0:NH],
                                                  in1=nega4, op=ALU.mult), [B_dtda, B_nega], [B_dtda])
            S.dma("pool", dtda_o[t0:t0 + T, :].rearrange("(b p) n -> p b n", p=128), dtda_st,
                  [B_dtda], [B_dram_out], B_dtda)

        def final(t0):
            for b in range(NB):
                S.op("act", lambda e, b=b, xt=xt: e.activation(out=junk, in_=xt[:, b, :], func=AF.Square,
                                                       accum_out=stat[:, b:b + 1]), [B_x[b]], [B_junk, B_stat])
            S.op("dve", lambda e: e.tensor_scalar(out=stat[:, 4:8], in0=stat[:, 0:4], scalar1=1.0 / D, scalar2=EPS,
                                                  op0=ALU.mult, op1=ALU.add), [B_stat], [B_stat])
            S.op("pool", lambda e: e.tensor_tensor(out=stat[:, 8:12], in0=stat[:, 4:8], in1=neghalf, op=ALU.pow),
                 [B_stat, B_nh], [B_stat])
            for b in range(NB):
                fo, fob = R_fo.next()
                S.op("dve", lambda e, b=b, fo=fo, xt=xt: e.scalar_tensor_tensor(
                    out=fo, in0=xt[:, b, :], scalar=stat[:, 8 + b:9 + b], in1=fing, op0=ALU.mult, op1=ALU.mult),
                    [B_x[b], B_stat, B_fing], [fob])
                S.dma("pool", y_o[t0 + b * 128:t0 + (b + 1) * 128, :], fo, [fob], [B_dram_out], fob)

        xbufs = [R_x.next() for _ in range(ntiles)]

        def load_x(ti):
            xt_, B_x_ = xbufs[ti]
            for b in range(NB):
                S.dma("sp", xt_[:, b, :], x_in[ti * T + b * 128:ti * T + (b + 1) * 128, :], [], [B_x_[b]], B_x_[b])

        if n_xt == 2:
            load_x(0)
        for ti in range(ntiles):
            t0 = ti * T
            xt, B_x = xbufs[ti]
            if n_xt == 1:
                load_x(ti)
            fi = 0
            if self.has_mixout:
                mixout(t0)
                ffn(self.ffns[fi]); fi += 1
            if fi < len(self.ffns):
                ffn(self.ffns[fi]); fi += 1
            if n_xt == 2 and ti + 1 < ntiles:
                load_x(ti + 1)
            if self.has_proj:
                proj(t0)
            if self.has_final:
                final(t0)
        S.emit(barrier=bool(fz))


class MixProg:
    def __init__(self, SEQ, do_attn=True, do_ssd=True, nc=None, fz=None, pfx=""):
        self.SEQ = SEQ
        self.do_attn = do_attn
        self.do_ssd = do_ssd
        self.fz = fz
        self.pfx = pfx
        self.nc = nc if nc is not None else bass.Bass("TRN2", target_bir_lowering=False)
        self.S = Sched(self.nc)
        self._n = 0
        self.build()

    def sb(self, shape, dt, name=None):
        self._n += 1
        return self.nc.alloc_sbuf_tensor(f"{self.pfx}{name or 'sb'}_{self._n}", list(shape), dt).ap()

    def din(self, name, shape, dt):
        return self.nc.dram_tensor(self.pfx + name, list(shape), dt, kind="ExternalInput").ap()

    def dout(self, name, shape, dt):
        return self.nc.dram_tensor(self.pfx + name, list(shape), dt, kind="ExternalOutput").ap()

    def dscr(self, name, shape, dt):
        return self.nc.dram_tensor(self.pfx + name, list(shape), dt, kind="Internal").ap()

    def build(self):
        nc, S = self.nc, self.S
        SEQ = self.SEQ
        NBLK = SEQ // 128
        NGRP = SEQ // 512
        B_dram_out = Buf("dram_out")
        fz = self.fz
        if self.do_ssd:
            convw_d = self.din("conv_w", [128, 7, 4], F32)
            convb_d = self.din("conv_b", [128, 7], F32)
            dskip_d = self.din("d_skip", [1, 6], F32)
            ssmn_d = self.din("ssm_norm", [1, 384], F32)
        NTLG = SEQ // 512
        B_scr = {k: [Buf(f"{k}{g}") for g in range(NTLG)] for k in ("xbc", "sz", "dtda")}
        if fz:
            if self.do_attn:
                xbc_d = self.dscr("xbc_scr", [896, SEQ], BF16)
                sz_d = self.dscr("sz_scr", [SEQ, 384], BF16)
                dtda_d = self.dscr("dtda_scr", [SEQ, 12], F32)
                fz["scr"] = (xbc_d, sz_d, dtda_d)
                winr_d = self.din("w_in_r", [D, 1670], F32)
                dtbr_d = self.din("dt_bias_r", [1, 6], F32)
                alogr_d = self.din("a_log_r", [1, 6], F32)
            else:
                xbc_d, sz_d, dtda_d = fz["scr"]
            ident_d, ctri_d, cmb_d = fz["ident"], fz["c_tri"], fz["c_mb"]
        else:
            qT_d = self.din("qT", [128, SEQ], BF16)
            kT_d = self.din("kT", [128, SEQ], BF16)
            v_d = self.din("v", [SEQ, 128], BF16)
            xbc_d = self.din("xbcT", [896, SEQ], BF16)
            sz_d = self.din("sz", [SEQ, 384], BF16)
            dtda_d = self.din("dtda", [SEQ, 12], F32)
            ident_d = self.din("ident", [128, 128], F32)
            ctri_d = self.din("c_tri", [128, 4, 128], F32)
            cmb_d = self.din("c_mb", [128, 4, 512], BF16)
            attnT_o = self.dout("attnT", [128, SEQ], BF16)
            yssmT_o = self.dout("yssmT", [384, SEQ], BF16)

        ident_f = self.sb([128, 128], F32, "ident_f"); B_idf = Buf("ident_f")
        ident_b = self.sb([128, 128], BF16, "ident_b"); B_idb = Buf("ident_b")
        ctri = self.sb([128, 4, 128], F32, "ctri"); B_ctri = Buf("ctri")
        ctri_b = self.sb([128, 2, 128], BF16, "ctri_b"); B_ctrib = Buf("ctri_b")
        ones_f = self.sb([128, 128], F32, "ones_f"); B_ones = Buf("ones_f")
        S.dma("sp", ident_f, ident_d, [], [B_idf], B_idf)
        S.dma("sp", ctri, ctri_d, [], [B_ctri], B_ctri)
        S.op("dve", lambda e: e.tensor_copy(out=ident_b, in_=ident_f), [B_idf], [B_idb])
        S.op("dve", lambda e: e.tensor_copy(out=ctri_b, in_=ctri[:, 0:2, :]), [B_ctri], [B_ctrib])
        S.op("pool", lambda e: e.memset(ones_f, 1.0), [], [B_ones])
        NTi, NU = ctri_b[:, 0, :], ctri_b[:, 1, :]
        SL, UI = ctri[:, 2, :], ctri[:, 3, :]
        banks = [self.nc.alloc_psum_tensor(f"{self.pfx}bank{i}", [128, 512], F32).ap() for i in range(8)]
        B_bank = [Buf(f"bank{i}") for i in range(8)]

        if self.do_attn:
            mb_b = self.sb([128, 4, 512], BF16, "mb_b"); B_mbb = Buf("mb_b")
            S.dma("sp", mb_b, cmb_d, [], [B_mbb], B_mbb)
            kT = self.sb([128, SEQ], BF16, "kT_sb"); B_kT = Buf("kT_sb")
            qT = self.sb([128, SEQ], BF16, "qT_sb"); B_qT = Buf("qT_sb")
            vs = self.sb([128, NBLK, 128], BF16, "v_sb"); B_vs = Buf("v_sb")
            nch = max(1, SEQ // 4096)
            cw = SEQ // nch
            for i in range(nch if not fz else 0):
                S.dma("sp", kT[:, i * cw:(i + 1) * cw], kT_d[:, i * cw:(i + 1) * cw], [], [B_kT], B_kT)
                S.dma("sp", qT[:, i * cw:(i + 1) * cw], qT_d[:, i * cw:(i + 1) * cw], [], [B_qT], B_qT)
                bw = NBLK // nch
                S.dma("sp", vs[:, i * bw:(i + 1) * bw, :],
                      v_d.rearrange("(j p) d -> p j d", p=128)[:, i * bw:(i + 1) * bw, :], [], [B_vs], B_vs)
            if fz:
                self.prephase(locals())
            NS = 2
            zb = [[(banks[4 * c + i], B_bank[4 * c + i]) for i in range(2)] for c in range(NS)]
            Pb = [(banks[4 * c + 2], B_bank[4 * c + 2]) for c in range(NS)]
            Ob = [(banks[4 * c + 3], B_bank[4 * c + 3]) for c in range(NS)]
            eb = [[(self.sb([128, 512], F32, f"e{c}{i}"), Buf(f"e{c}{i}")) for i in range(2)] for c in range(NS)]
            spb = [[(self.sb([128, 512], BF16, f"sp{c}{i}"), Buf(f"sp{c}{i}")) for i in range(2)] for c in range(NS)]
            E2b = [(self.sb([128, 512], F32, f"E2{c}"), Buf(f"E2{c}")) for c in range(NS)]
            Wb = [[(self.sb([128, 512], BF16, f"W{c}{i}"), Buf(f"W{c}{i}")) for i in range(2)] for c in range(NS)]
            R_ao = Ring([(self.sb([128, 512], BF16, f"ao{i}"), Buf(f"ao{i}")) for i in range(2)])
            groups = list(range(NGRP - 1, -1, -1))
            steps = [[], []]
            for idx, G in enumerate(groups):
                c = idx % NS
                for j in range(4 * G + 3, -1, -1):
                    steps[c].append((G, j, j == 4 * G + 3, j == 0))
            n_iter = max(len(s) for s in steps)

            def S1(c, t):
                G, j, first, last = steps[c][t]
                z, zbuf = zb[c][t % 2]
                diag = j >= 4 * G
                S.op("pe", lambda e: e.matmul(z, lhsT=kT[:, j * 128:(j + 1) * 128], rhs=qT[:, G * 512:(G + 1) * 512],
                                              start=True, stop=not diag), [B_kT, B_qT], [zbuf])
                if diag:
                    r = j - 4 * G
                    S.op("pe", lambda e: e.matmul(z, lhsT=ident_b, rhs=mb_b[:, r, :], start=False, stop=True),
                         [B_idb, B_mbb], [zbuf])

            def S2(c, t):
                z, zbuf = zb[c][t % 2]
                ee, ebuf = eb[c][t % 2]
                S.op("act", lambda e: e.activation(out=ee, in_=z, func=AF.Exp), [zbuf], [ebuf])

            def S3(c, t):
                ee, ebuf = eb[c][t % 2]
                sp, spbuf = spb[c][t % 2]
                S.op("act", lambda e: e.activation(out=sp, in_=ee, func=AF.Ln, bias=1.0), [ebuf], [spbuf])

            def S4(c, t):
                G, j, first, last = steps[c][t]
                P, Pbuf = Pb[c]
                sp, spbuf = spb[c][t % 2]
                if first:
                    S.op("pe", lambda e: e.matmul(P, lhsT=NTi, rhs=sp, start=True, stop=True),
                         [B_ctrib, spbuf], [Pbuf])
                else:
                    spp, sppbuf = spb[c][(t - 1) % 2]
                    S.op("pe", lambda e: e.matmul(P, lhsT=NU, rhs=spp, start=False, stop=False),
                         [B_ctrib, sppbuf], [Pbuf])
                    S.op("pe", lambda e: e.matmul(P, lhsT=NTi, rhs=sp, start=False, stop=True),
                         [B_ctrib, spbuf], [Pbuf])

            def S5(c, t):
                P, Pbuf = Pb[c]
                E2, E2buf = E2b[c]
                S.op("act", lambda e: e.activation(out=E2, in_=P, func=AF.Exp), [Pbuf], [E2buf])

            def S6(c, t):
                ee, ebuf = eb[c][t % 2]
                E2, E2buf = E2b[c]
                W, Wbuf = Wb[c][t % 2]
                S.op("dve", lambda e: e.tensor_tensor(out=W, in0=ee, in1=E2, op=ALU.mult), [ebuf, E2buf], [Wbuf])

            def S7(c, t):
                G, j, first, last = steps[c][t]
                W, Wbuf = Wb[c][t % 2]
                O, Obuf = Ob[c]
                S.op("pe", lambda e: e.matmul(O, lhsT=vs[:, j, :], rhs=W, start=first, stop=last),
                     [B_vs, Wbuf], [Obuf])
                if last:
                    ao, aobuf = R_ao.next()
                    S.op("dve", lambda e: e.tensor_copy(out=ao, in_=O), [Obuf], [aobuf])
                    if fz:
                        B_as, B_ar = fz["B_asend"], fz["B_arecv"]
                        S.dma("pool", fz["asend"][G], ao, [aobuf], [B_as], aobuf)
                        S.coll(fz["asend"][G], fz["arecv"][G].rearrange("r p t -> (r p) t"), fz["groups"],
                               [B_as], [B_ar], B_ar)
                    else:
                        S.dma("pool", attnT_o[:, G * 512:(G + 1) * 512], ao, [aobuf], [B_dram_out], aobuf)

            act = lambda c, t: 0 <= t < len(steps[c])
            for c in range(NS):
                if act(c, 0):
                    S1(c, 0)
            for t in range(n_iter + 1):
                for c in range(NS):
                    if act(c, t + 1):
                        S1(c, t + 1)
                for c in range(NS):
                    if act(c, t):
                        S2(c, t)
                for c in range(NS):
                    if act(c, t):
                        S3(c, t)
                for c in range(NS):
                    if act(c, t - 1):
                        S7(c, t - 1)
                for c in range(NS):
                    if act(c, t):
                        S4(c, t)
                for c in range(NS):
                    if act(c, t):
                        S5(c, t)
                for c in range(NS):
                    if act(c, t):
                        S6(c, t)

        if self.do_ssd:
            NTL = SEQ // 512
            cw_f = self.sb([128, 7, 4], F32, "cw_f"); B_cw = Buf("cw_f")
            cb_f = self.sb([128, 7], F32, "cb_f"); B_cb = Buf("cb_f")
            S.dma("sp", cw_f, convw_d, [], [B_cw], B_cw)
            S.dma("sp", cb_f, convb_d, [], [B_cb], B_cb)
            dg = self.sb([128, 7, 4, 128], BF16, "dg"); B_dg = Buf("dg")
            for c in range(7):
                for k in range(4):
                    S.op("dve", lambda e, c=c, k=k: e.tensor_scalar(out=dg[:, c, k, :], in0=ident_f,
                                                                    scalar1=cw_f[:, c, k:k + 1], scalar2=None,
                                                                    op0=ALU.mult), [B_idf, B_cw], [B_dg])
            dsk6 = self.sb([128, 6], F32, "dsk6"); B_dsk6 = Buf("dsk6")
            dsk = self.sb([128, 6, 64], F32, "dsk"); B_dsk = Buf("dsk")
            ssmn = self.sb([128, 384], F32, "ssmn"); B_ssmn = Buf("ssmn")
            S.dma("sp", dsk6, dskip_d.partition_broadcast(128), [], [B_dsk6], B_dsk6)
            S.dma("sp", ssmn, ssmn_d.partition_broadcast(128), [], [B_ssmn], B_ssmn)
            S.op("dve", lambda e: e.tensor_copy(out=dsk, in_=dsk6.unsqueeze(2).to_broadcast([128, 6, 64])),
                 [B_dsk6], [B_dsk])
            neghalf = self.sb([128, 2], F32, "neghalf"); B_nh = Buf("neghalf")
            S.op("pool", lambda e: e.memset(neghalf, -0.5), [], [B_nh])
            R_xin = Ring([(self.sb([128, 7, 515], BF16, f"xin{i}"), Buf(f"xin{i}")) for i in range(2)])
            R_cT = Ring([(self.sb([128, 7, 512], BF16, f"cT{i}"), Buf(f"cT{i}")) for i in range(2)])
            R_sz = Ring([(self.sb([128, 4, 384], BF16, f"szt{i}"), Buf(f"szt{i}")) for i in range(2)])
            R_dd = Ring([(self.sb([128, 4, 12], F32, f"ddt{i}"), Buf(f"ddt{i}")) for i in range(2)])
            R_yT = Ring([(self.sb([128, 3, 512], BF16, f"yTst{i}"), Buf(f"yTst{i}")) for i in range(2)])
            Sf = self.sb([128, 6, 64], F32, "Sf"); B_Sf = Buf("Sf")
            Sb = self.sb([128, 6, 64], BF16, "Sb"); B_Sb = Buf("Sb")
            S.op("pool", lambda e: e.memset(Sf, 0.0), [], [B_Sf])
            S.op("pool", lambda e: e.memset(Sb, 0.0), [], [B_Sb])
            mk = lambda shape, dt, nm: (self.sb(shape, dt, nm), Buf(nm))
            R_tok = Ring([mk([128, 640], BF16, f"tok{i}") for i in range(2)])
            R_E18 = Ring([mk([128, 18], F32, f"E18{i}") for i in range(2)])
            R_rhsD = Ring([mk([128, 6, 128], F32, f"rhsD{i}") for i in range(1)])
            R_L = Ring([mk([128, 6, 128], F32, f"L{i}") for i in range(1)])
            R_CBm = Ring([mk([128, 2, 128], F32, f"CBm{i}") for i in range(2)])
            R_M = Ring([mk([128, 6, 128], BF16, f"M{i}") for i in range(2)])
            R_xdt = Ring([mk([128, 6, 64], BF16, f"xdt{i}") for i in range(2)])
            R_xw = Ring([mk([128, 6, 64], BF16, f"xw{i}") for i in range(2)])
            R_y = Ring([mk([128, 384], F32, f"yy{i}") for i in range(6)])
            R_st = Ring([mk([128, 8], F32, f"rst{i}") for i in range(2)])
            R_yn = Ring([mk([128, 384], BF16, f"yn{i}") for i in range(2)])
            junk = self.sb([128, 192], F32, "junk_s"); B_junk = Buf("junk_s")
            pc, B_pc = banks[0], B_bank[0]
            ptr, B_ptr = banks[1].bitcast(BF16), B_bank[1]
            pm, B_pm = banks[2], B_bank[2]
            pD = [(banks[3], B_bank[3]), (banks[4], B_bank[4])]
            pY, B_pY = banks[5], B_bank[5]
            pI, B_pI = banks[6], B_bank[6]
            pSt, B_pSt = banks[7], B_bank[7]

            for tt in range(NTL):
                t0 = tt * 512
                xin, B_xin = R_xin.next()
                if tt == 0:
                    S.op("pool", lambda e, xin=xin: e.memset(xin[:, :, 0:3], 0.0), [], [B_xin])
                    S.dma("sp", xin[:, :, 3:515], xbc_d.rearrange("(c p) t -> p c t", p=128)[:, :, 0:512],
                          [B_scr["xbc"][0]], [B_xin], B_xin)
                else:
                    S.dma("sp", xin, xbc_d.rearrange("(c p) t -> p c t", p=128)[:, :, t0 - 3:t0 + 512],
                          [B_scr["xbc"][tt - 1], B_scr["xbc"][tt]], [B_xin], B_xin)
                szt, B_szt = R_sz.next()
                ddt, B_ddt = R_dd.next()
                S.dma("sp", szt, sz_d[t0:t0 + 512, :].rearrange("(b p) n -> p b n", p=128), [B_scr["sz"][tt]],
                      [B_szt], B_szt)
                S.dma("sp", ddt, dtda_d[t0:t0 + 512, :].rearrange("(b p) n -> p b n", p=128), [B_scr["dtda"][tt]],
                      [B_ddt], B_ddt)
                cT, B_cT = R_cT.next()
                for c in range(7):
                    for k in range(4):
                        S.op("pe", lambda e, c=c, k=k, xin=xin: e.matmul(pc, lhsT=dg[:, c, k, :], rhs=xin[:, c, k:k + 512],
                                                                         start=(k == 0), stop=(k == 3)),
                             [B_dg, B_xin], [B_pc])
                    S.op("act", lambda e, c=c, cT=cT: e.activation(out=cT[:, c, :], in_=pc, func=AF.Silu,
                                                                   bias=cb_f[:, c:c + 1]), [B_pc, B_cb], [B_cT])
                yT, B_yT = R_yT.next()
                for ch in range(4):
                    cs = slice(ch * 128, (ch + 1) * 128)
                    dt6 = ddt[:, ch, 0:6]
                    dA6 = ddt[:, ch, 6:12]
                    for i in range(5):
                        S.op("pe", lambda e, i=i, cT=cT, cs=cs: e.transpose(ptr[:, i * 128:(i + 1) * 128], cT[:, i, cs], ident_b),
                             [B_cT, B_idb], [B_ptr])
                    tok, B_tok = R_tok.next()
                    S.op("dve", lambda e, tok=tok: e.tensor_copy(out=tok, in_=ptr[:, 0:640]), [B_ptr], [B_tok])
                    S.op("pe", lambda e, dA6=dA6: e.matmul(pm[:, 256:262], lhsT=UI, rhs=dA6, start=True, stop=True),
                         [B_ctri, B_ddt], [B_pm])
                    S.op("pe", lambda e, dA6=dA6: e.matmul(pm[:, 262:268], lhsT=ones_f, rhs=dA6, start=True, stop=True),
                         [B_ones, B_ddt], [B_pm])
                    S.op("pe", lambda e, dA6=dA6: e.matmul(pm[:, 268:274], lhsT=SL, rhs=dA6, start=True, stop=True),
                         [B_ctri, B_ddt], [B_pm])
                    for g in range(2):
                        S.op("pe", lambda e, g=g, cT=cT, cs=cs: e.matmul(pm[:, g * 128:(g + 1) * 128], lhsT=cT[:, 3 + g, cs],
                                                                         rhs=cT[:, 5 + g, cs], start=True, stop=True),
                             [B_cT], [B_pm])
                    E18, B_E18 = R_E18.next()
                    S.op("act", lambda e, E18=E18: e.activation(out=E18, in_=pm[:, 256:274], func=AF.Exp), [B_pm], [B_E18])
                    CBm, B_CBm = R_CBm.next()
                    S.op("dve", lambda e, CBm=CBm: e.tensor_tensor(
                        out=CBm, in0=pm[:, 0:256].rearrange("p (g q) -> p g q", g=2),
                        in1=UI.unsqueeze(1).to_broadcast([128, 2, 128]), op=ALU.mult), [B_pm, B_ctri], [B_CBm])
                    rhsD, B_rhsD = R_rhsD.next()
                    S.op("dve", lambda e, rhsD=rhsD, dA6=dA6: e.tensor_tensor(
                        out=rhsD, in0=UI.unsqueeze(1).to_broadcast([128, 6, 128]),
                        in1=dA6.unsqueeze(2).to_broadcast([128, 6, 128]), op=ALU.mult), [B_ctri, B_ddt], [B_rhsD])
                    L, B_L = R_L.next()
                    for g in range(2):
                        pDg, B_pDg = pD[g]
                        S.op("pe", lambda e, g=g, pDg=pDg, rhsD=rhsD: e.matmul(
                            pDg[:, 0:384], lhsT=SL, rhs=rhsD[:, 3 * g:3 * g + 3, :].rearrange("p h q -> p (h q)"),
                            start=True, stop=True), [B_ctri, B_rhsD], [B_pDg])
                        S.op("act", lambda e, g=g, pDg=pDg, L=L: e.activation(
                            out=L[:, 3 * g:3 * g + 3, :].rearrange("p h q -> p (h q)"), in_=pDg[:, 0:384], func=AF.Exp),
                            [B_pDg], [B_L])
                    M, B_M = R_M.next()
                    for g in range(2):
                        S.op("dve", lambda e, g=g, M=M, L=L, CBm=CBm: e.tensor_tensor(
                            out=M[:, 3 * g:3 * g + 3, :], in0=L[:, 3 * g:3 * g + 3, :],
                            in1=CBm[:, g, :].unsqueeze(1).to_broadcast([128, 3, 128]), op=ALU.mult),
                            [B_L, B_CBm], [B_M])
                    xdt, B_xdt = R_xdt.next()
                    S.op("dve", lambda e, xdt=xdt, tok=tok, dt6=dt6: e.tensor_tensor(
                        out=xdt, in0=tok[:, 0:384].rearrange("p (h d) -> p h d", h=6),
                        in1=dt6.unsqueeze(2).to_broadcast([128, 6, 64]), op=ALU.mult), [B_tok, B_ddt], [B_xdt])
                    for hh in range(6):
                        S.op("pe", lambda e, hh=hh, M=M, xdt=xdt: e.matmul(pY[:, hh * 64:(hh + 1) * 64], lhsT=M[:, hh, :],
                                                                           rhs=xdt[:, hh, :], start=True, stop=True),
                             [B_M, B_xdt], [B_pY])
                    for g in range(2):
                        S.op("pe", lambda e, g=g, cT=cT, cs=cs: e.matmul(
                            pI[:, g * 192:(g + 1) * 192], lhsT=cT[:, 5 + g, cs],
                            rhs=Sb[:, 3 * g:3 * g + 3, :].rearrange("p h d -> p (h d)"), start=True, stop=True),
                            [B_cT, B_Sb], [B_pI])
                    y1, B_y1 = R_y.next()
                    S.op("dve", lambda e, y1=y1, E18=E18: e.tensor_tensor(
                        out=y1.rearrange("p (h d) -> p h d", h=6), in0=pI[:, 0:384].rearrange("p (h d) -> p h d", h=6),
                        in1=E18[:, 0:6].unsqueeze(2).to_broadcast([128, 6, 64]), op=ALU.mult), [B_pI, B_E18], [B_y1])
                    y2, B_y2 = R_y.next()
                    S.op("dve", lambda e, y1=y1, y2=y2: e.tensor_tensor(out=y2, in0=pY[:, 0:384], in1=y1, op=ALU.add),
                         [B_pY, B_y1], [B_y2])
                    t3, B_t3 = R_y.next()
                    S.op("pool", lambda e, t3=t3, tok=tok: e.tensor_tensor(
                        out=t3, in0=tok[:, 0:384], in1=dsk.rearrange("p h d -> p (h d)"), op=ALU.mult),
                        [B_tok, B_dsk], [B_t3])
                    y3, B_y3 = R_y.next()
                    S.op("pool", lambda e, y3=y3, y2=y2, t3=t3: e.tensor_tensor(out=y3, in0=y2, in1=t3, op=ALU.add),
                         [B_y2, B_t3], [B_y3])
                    y4, B_y4 = R_y.next()
                    S.op("dve", lambda e, y4=y4, y3=y3, szt=szt, ch=ch: e.tensor_tensor(out=y4, in0=y3, in1=szt[:, ch, :],
                                                                                      op=ALU.mult), [B_y3, B_szt], [B_y4])
                    rst, B_rst = R_st.next()
                    for g in range(2):
                        S.op("act", lambda e, g=g, y4=y4, rst=rst: e.activation(
                            out=junk, in_=y4[:, g * 192:(g + 1) * 192], func=AF.Square, accum_out=rst[:, g:g + 1]),
                            [B_y4], [B_junk, B_rst])
                    S.op("dve", lambda e, rst=rst: e.tensor_scalar(out=rst[:, 2:4], in0=rst[:, 0:2], scalar1=1.0 / 192,
                                                                   scalar2=EPS, op0=ALU.mult, op1=ALU.add),
                         [B_rst], [B_rst])
                    S.op("pool", lambda e, rst=rst: e.tensor_tensor(out=rst[:, 4:6], in0=rst[:, 2:4], in1=neghalf,
                                                                    op=ALU.pow), [B_rst, B_nh], [B_rst])
                    yn, B_yn = R_yn.next()
                    for g in range(2):
                        S.op("dve", lambda e, g=g, yn=yn, y4=y4, rst=rst: e.scalar_tensor_tensor(
                            out=yn[:, g * 192:(g + 1) * 192], in0=y4[:, g * 192:(g + 1) * 192],
                            scalar=rst[:, 4 + g:5 + g], in1=ssmn[:, g * 192:(g + 1) * 192], op0=ALU.mult, op1=ALU.mult),
                            [B_y4, B_rst, B_ssmn], [B_yn])
                    for i in range(3):
                        S.op("pe", lambda e, i=i, yn=yn: e.transpose(ptr[:, 640 + i * 128:640 + (i + 1) * 128],
                                                                     yn[:, i * 128:(i + 1) * 128], ident_b),
                             [B_yn, B_idb], [B_ptr])
                    S.op("dve", lambda e, yT=yT, cs=cs: e.tensor_copy(
                        out=yT[:, :, cs], in_=ptr[:, 640:1024].rearrange("p (c t) -> p c t", c=3)), [B_ptr], [B_yT])
                    xw, B_xw = R_xw.next()
                    S.op("dve", lambda e, xw=xw, xdt=xdt, E18=E18: e.tensor_tensor(
                        out=xw, in0=xdt, in1=E18[:, 12:18].unsqueeze(2).to_broadcast([128, 6, 64]), op=ALU.mult),
                        [B_xdt, B_E18], [B_xw])
                    for g in range(2):
                        S.op("pe", lambda e, g=g, tok=tok, xw=xw: e.matmul(
                            pSt[:, g * 192:(g + 1) * 192], lhsT=tok[:, 384 + g * 128:384 + (g + 1) * 128],
                            rhs=xw[:, 3 * g:3 * g + 3, :].rearrange("p h d -> p (h d)"), start=True, stop=True),
                            [B_tok, B_xw], [B_pSt])
                    S.op("dve", lambda e, E18=E18: e.tensor_tensor(
                        out=Sf, in0=Sf, in1=E18[:, 6:12].unsqueeze(2).to_broadcast([128, 6, 64]), op=ALU.mult),
                        [B_Sf, B_E18], [B_Sf])
                    S.op("dve", lambda e: e.tensor_tensor(out=Sf.rearrange("p h d -> p (h d)"),
                                                          in0=Sf.rearrange("p h d -> p (h d)"), in1=pSt[:, 0:384],
                                                          op=ALU.add), [B_Sf, B_pSt], [B_Sf])
                    S.op("act", lambda e: e.copy(out=Sb, in_=Sf), [B_Sf], [B_Sb])
                if fz:
                    B_ys, B_yr = fz["B_ysend"], fz["B_yrecv"]
                    S.dma("pool", fz["ysend"][tt // 2, tt % 2].rearrange("(c p) t -> p c t", p=128), yT, [B_yT], [B_ys],
                          B_yT)
                    if tt % 2 == 1:
                        S.coll(fz["ysend"][tt // 2].rearrange("j p t -> (j p) t"),
                               fz["yrecv"][tt // 2].rearrange("r j p t -> (r j p) t"), fz["groups"],
                               [B_ys], [B_yr], B_yr)
                else:
                    S.dma("pool", yssmT_o.rearrange("(c p) t -> p c t", p=128)[:, :, t0:t0 + 512], yT, [B_yT],
                          [B_dram_out], B_yT)
        S.emit(barrier=bool(fz))

    def prephase(self, L):
        S, fz, SEQ = self.S, self.fz, self.SEQ
        qT, kT, vs = L["qT"], L["kT"], L["vs"]
        B_qT, B_kT, B_vs = L["B_qT"], L["B_kT"], L["B_vs"]
        banks, B_bank, B_scr = L["banks"], L["B_bank"], L["B_scr"]
        xbc_d, sz_d, dtda_d = L["xbc_d"], L["sz_d"], L["dtda_d"]
        NTLG = SEQ // 512
        ntl = NTLG // 4
        wscr = self.dscr("w_in_r_bf", [128, 8, 1670], BF16)
        B_wscr = Buf("w_in_r_bf")
        src = L["winr_d"].rearrange("(kc p) n -> kc p n", p=128)
        for kc in range(8):
            S.dma("pool", wscr[:, kc, :], src[kc], [], [B_wscr], B_wscr, max_dma_last_dim=4096)
        wsl = self.sb([128, 8, 1670], BF16, "wsl"); B_wsl = Buf("wsl")
        S.dma("sp", wsl, wscr, [B_wscr], [B_wsl], B_wsl)
        dtb = self.sb([128, 4, 6], F32, "dtb"); B_dtb = Buf("dtb")
        nega = self.sb([128, 4, 6], F32, "nega"); B_nega = Buf("nega")
        for b in range(4):
            S.dma("sp", dtb[:, b, :], L["dtbr_d"].partition_broadcast(128), [], [B_dtb], B_dtb)
            S.dma("sp", nega[:, b, :], L["alogr_d"].partition_broadcast(128), [], [B_nega], B_nega)
        S.op("act", lambda e: e.activation(out=nega, in_=nega, func=AF.Exp), [B_nega], [B_nega])
        S.op("dve", lambda e: e.tensor_scalar(out=nega, in0=nega, scalar1=-1.0, scalar2=None, op0=ALU.mult),
             [B_nega], [B_nega])
        R_h = Ring([(self.sb([128, 8, 512], BF16, f"hTt{i}"), Buf(f"hTt{i}")) for i in range(2)])
        R_xs = Ring([(self.sb([128, 7, 512], BF16, f"xst{i}"), Buf(f"xst{i}")) for i in range(2)])
        R_zs = Ring([(self.sb([128, 4, 384], BF16, f"zst{i}"), Buf(f"zst{i}")) for i in range(2)])
        R_ds = Ring([(self.sb([128, 4, 12], F32, f"dst{i}"), Buf(f"dst{i}")) for i in range(2)])
        dts = self.sb([128, 4, 24], F32, "dts_p"); B_dts = Buf("dts_p")
        R_pf = Ring([(banks[i], B_bank[i]) for i in range(4)])
        pv, B_pv = banks[4], B_bank[4]
        R_pz = Ring([(banks[5], B_bank[5]), (banks[6], B_bank[6])])
        pS, B_pS = banks[7], B_bank[7]
        for g in range(NTLG):
            rank, lt = g // ntl, g % ntl
            hTt, B_h = R_h.next()
            S.dma("sp", hTt, fz["hrecv"][lt][rank], [], [B_h], B_h)
            gs = slice(g * 512, (g + 1) * 512)

            def fmm(c0, hTt=hTt, B_h=B_h):
                p, pb = R_pf.next()
                for kc in range(8):
                    S.op("pe", lambda e, kc=kc, p=p: e.matmul(p, lhsT=wsl[:, kc, c0:c0 + 128], rhs=hTt[:, kc, :],
                                                              start=(kc == 0), stop=(kc == 7)), [B_wsl, B_h], [pb])
                return p, pb
            p, pb = fmm(0)
            S.op("dve", lambda e, p=p, gs=gs: e.tensor_scalar(out=qT[:, gs], in0=p, scalar1=128.0 ** -0.5, scalar2=None,
                                                              op0=ALU.mult), [pb], [B_qT])
            p, pb = fmm(128)
            S.op("act", lambda e, p=p, gs=gs: e.copy(out=kT[:, gs], in_=p), [pb], [B_kT])
            xst, B_xst = R_xs.next()
            for c in range(7):
                p, pb = fmm(256 + c * 128)
                if c % 2 == 0:
                    S.op("dve", lambda e, p=p, c=c, xst=xst: e.tensor_copy(out=xst[:, c, :], in_=p), [pb], [B_xst])
                else:
                    S.op("act", lambda e, p=p, c=c, xst=xst: e.copy(out=xst[:, c, :], in_=p), [pb], [B_xst])
            S.dma("pool", xbc_d.rearrange("(c p) t -> p c t", p=128)[:, :, gs], xst, [B_xst], [B_scr["xbc"][g]], B_xst)
            for b in range(4):
                for kc in range(8):
                    S.op("pe", lambda e, kc=kc, b=b, hTt=hTt: e.matmul(
                        pv[:, b * 128:(b + 1) * 128], lhsT=hTt[:, kc, b * 128:(b + 1) * 128], rhs=wsl[:, kc, 1152:1280],
                        start=(kc == 0), stop=(kc == 7)), [B_wsl, B_h], [B_pv])
            S.op("dve", lambda e, g=g: e.tensor_copy(out=vs[:, g * 4:(g + 1) * 4, :],
                                                     in_=pv.rearrange("p (b d) -> p b d", b=4)), [B_pv], [B_vs])
            zst, B_zst = R_zs.next()
            for b in range(4):
                pz, B_pz = R_pz.next()
                for kc in range(8):
                    S.op("pe", lambda e, kc=kc, b=b, hTt=hTt, pz=pz: e.matmul(
                        pz[:, 0:384], lhsT=hTt[:, kc, b * 128:(b + 1) * 128], rhs=wsl[:, kc, 1280:1664],
                        start=(kc == 0), stop=(kc == 7)), [B_wsl, B_h], [B_pz])
                S.op("act", lambda e, b=b, pz=pz, zst=zst: e.activation(out=zst[:, b, :], in_=pz[:, 0:384], func=AF.Silu),
                     [B_pz], [B_zst])
            S.dma("pool", sz_d[g * 512:(g + 1) * 512, :].rearrange("(b p) n -> p b n", p=128), zst, [B_zst],
                  [B_scr["sz"][g]], B_zst)
            for b in range(4):
                for kc in range(8):
                    S.op("pe", lambda e, kc=kc, b=b, hTt=hTt: e.matmul(
                        pS[:, b * 6:(b + 1) * 6], lhsT=hTt[:, kc, b * 128:(b + 1) * 128], rhs=wsl[:, kc, 1664:1670],
                        start=(kc == 0), stop=(kc == 7)), [B_wsl, B_h], [B_pS])
            dst, B_dst = R_ds.next()
            dtb_f = dtb.rearrange("p b n -> p (b n)")
            S.op("dve", lambda e: e.tensor_tensor(out=dts[:, 0, :], in0=pS[:, 0:24], in1=dtb_f, op=ALU.add),
                 [B_pS, B_dtb], [B_dts])
            S.op("act", lambda e: e.activation(out=dts[:, 1, :], in_=dts[:, 0, :], func=AF.Abs), [B_dts], [B_dts])
            S.op("act", lambda e: e.activation(out=dts[:, 2, :], in_=dts[:, 1, :], func=AF.Exp, scale=-1.0),
                 [B_dts], [B_dts])
            S.op("act", lambda e: e.activation(out=dts[:, 3, :], in_=dts[:, 2, :], func=AF.Ln, bias=1.0),
                 [B_dts], [B_dts])
            S.op("dve", lambda e, dst=dst: e.scalar_tensor_tensor(
                out=dst[:, :, 0:6], in0=dts[:, 0, :].rearrange("p (b n) -> p b n", b=4), scalar=0.0,
                in1=dts[:, 3, :].rearrange("p (b n) -> p b n", b=4), op0=ALU.max, op1=ALU.add), [B_dts], [B_dst])
            S.op("dve", lambda e, dst=dst: e.tensor_tensor(out=dst[:, :, 6:12], in0=dst[:, :, 0:6], in1=nega,
                                                           op=ALU.mult), [B_dst, B_nega], [B_dst])
            S.dma("pool", dtda_d[g * 512:(g + 1) * 512, :].rearrange("(b p) n -> p b n", p=128), dst, [B_dst],
                  [B_scr["dtda"][g]], B_dst)


class FusedProg:
    def __init__(self, NT, SEQ, nph=99):
        nc = self.nc = bass.Bass("TRN2", target_bir_lowering=False)
        ntl = NT // T
        NGRP = SEQ // 512
        groups = [[0, 1, 2, 3], [4, 5, 6, 7]]
        ext = lambda n, s, d: nc.dram_tensor(n, list(s), d, kind="ExternalInput").ap()
        itn = lambda n, s, d: nc.dram_tensor(n, list(s), d, kind="Internal").ap()
        ident = ext("ident", [128, 128], F32)
        c_tri = ext("c_tri", [128, 4, 128], F32)
        c_mb = ext("c_mb", [128, 4, 512], BF16)
        L = []
        for l in range(2):
            L.append(dict(
                xres=itn(f"xres{l}", [NT, D], F32), sg=itn(f"sg{l}", [2 * D, NT], BF16),
                hsend=itn(f"hsend{l}", [ntl, 128, 8, T], BF16), hrecv=itn(f"hrecv{l}", [ntl, 4, 128, 8, T], BF16),
                asend=itn(f"asend{l}", [NGRP, 128, 512], BF16), arecv=itn(f"arecv{l}", [NGRP, 4, 128, 512], BF16),
                ysend=itn(f"ysend{l}", [NGRP // 2, 2, 384, 512], BF16),
                yrecv=itn(f"yrecv{l}", [NGRP // 2, 4, 2, 384, 512], BF16)))
        base = dict(ident=ident, c_tri=c_tri, c_mb=c_mb, groups=groups)
        self.n_ops = 0

        self._ph = 0

        def tok(stage, pfx, **kw):
            self._ph += 1
            if self._ph > nph:
                return
            fz = dict(base); fz.update(kw)
            with nc.cleanup_on_exit():
                p = TokProg(stage, NT, nc=nc, fz=fz, pfx=pfx)
            self.n_ops += len(p.S.ops)

        def mix(l, pa, ps):
            fz = dict(base)
            fz.update(hrecv=L[l]["hrecv"], asend=L[l]["asend"], arecv=L[l]["arecv"], ysend=L[l]["ysend"],
                      yrecv=L[l]["yrecv"], B_asend=Buf("asend"), B_arecv=Buf("arecv"), B_ysend=Buf("ysend"),
                      B_yrecv=Buf("yrecv"))
            self._ph += 1
            if self._ph <= nph:
                with nc.cleanup_on_exit():
                    p = MixProg(SEQ, True, False, nc=nc, fz=fz, pfx=pa)
                self.n_ops += len(p.S.ops)
            self._ph += 1
            if self._ph <= nph:
                with nc.cleanup_on_exit():
                    p = MixProg(SEQ, False, True, nc=nc, fz=fz, pfx=ps)
                self.n_ops += len(p.S.ops)

        tok("A0", "p0_", x_src=None, x_dst=L[0]["xres"], sg_dst=L[0]["sg"], hsend=L[0]["hsend"], hrecv=L[0]["hrecv"])
        mix(0, "p1_", "p2_")
        tok("CA", "p3_", x_src=L[0]["xres"], sg_src=L[0]["sg"], arecv=L[0]["arecv"], yrecv=L[0]["yrecv"],
            x_dst=L[1]["xres"], sg_dst=L[1]["sg"], hsend=L[1]["hsend"], hrecv=L[1]["hrecv"])
        mix(1, "p4_", "p5_")
        tok("C1", "p6_", x_src=L[1]["xres"], sg_src=L[1]["sg"], arecv=L[1]["arecv"], yrecv=L[1]["yrecv"])
        if nph < 7:
            dbg = nc.dram_tensor("dbg", [128, 128], F32, kind="ExternalOutput").ap()
            with nc.semaphore("dbgsem") as dsem, nc.Block() as block:
                @block.sync
                def _(e):
                    e.dma_start(out=dbg, in_=ident).then_inc(dsem, 16)
                    e.wait_ge(dsem, 16)


_PROGS = {}


def _get_prog(kind, *args):
    key = (kind,) + args
    if key not in _PROGS:
        if kind == "tok":
            _PROGS[key] = TokProg(*args)
        else:
            _PROGS[key] = MixProg(*args)
    return _PROGS[key]


def _gain_t(g):
    return np.ascontiguousarray(np.asarray(g, np.float32).reshape(8, 128).T)


def _c(a, dt=np.float32):
    return np.ascontiguousarray(np.asarray(a, dt))


def run_tok(stage, NT, per_core, shared):
    prog = _get_prog("tok", stage, NT)
    ident = np.eye(128, dtype=np.float32)
    in_maps = []
    for c in range(NCORES):
        m = dict(shared)
        m.update(per_core[c])
        m["ident"] = ident
        in_maps.append(m)
    res = run_bass_kernel_spmd(prog.nc, in_maps, core_ids=list(range(NCORES)))
    return res.results


def _mix_consts():
    k = np.arange(128)[:, None]
    l = np.arange(128)[None, :]
    ctri = np.zeros((128, 4, 128), np.float32)
    ctri[:, 0, :] = -1.0 * (k >= l)
    ctri[:, 1, :] = -1.0 * (k < l)
    ctri[:, 2, :] = (k > l)
    ctri[:, 3, :] = (k <= l)
    q = np.arange(512)[None, :]
    cmb = np.zeros((128, 4, 512), np.float32)
    for r in range(4):
        cmb[:, r, :] = np.where(r * 128 + k < q, 0.0, -30000.0)
    return ctri, cmb


def run_mix(SEQ, per_core, do_attn=True, do_ssd=True):
    prog = _get_prog("mix", SEQ, do_attn, do_ssd)
    ctri, cmb = _mix_consts()
    ident = np.eye(128, dtype=np.float32)
    in_maps = []
    for c in range(NCORES):
        m = dict(per_core[c])
        m["ident"] = ident
        m["c_tri"] = ctri
        m["c_mb"] = cmb.astype(NPBF)
        in_maps.append(m)
    res = run_bass_kernel_spmd(prog.nc, in_maps, core_ids=list(range(NCORES)))
    return res.results


def _tok_shared_ffn(tag, p, name, l):
    return {f"f{tag}_gain": _gain_t(p[f"{name}_norm"][l]), f"f{tag}_up": _c(p[f"{name}_w_up"][l]),
            f"f{tag}_dn": _c(p[f"{name}_w_down"][l])}


def _tok_shared_proj(p, l):
    return {"mix_gain": _gain_t(p["mix_norm"][l]), "w_in": _c(p["w_in"][l]),
            "dt_bias": _c(p["dt_bias"][l].reshape(1, NH)), "a_log": _c(p["a_log"][l].reshape(1, NH))}


def _tok_shared_mixout(p, l):
    return {"w_bsb": _c(p["w_branch_sb"][l]), "w_bssm": _c(p["w_branch_ssm"][l]), "w_out": _c(p["w_out"][l])}


def _mix_inputs(tok_res, p, l, B, SEQ):
    cpb = NCORES // B
    per_core = []
    for b in range(B):
        cs = range(b * cpb, (b + 1) * cpb)
        qT = np.concatenate([tok_res[c]["qT"] for c in cs], axis=1)
        kT = np.concatenate([tok_res[c]["kT"] for c in cs], axis=1)
        v = np.concatenate([tok_res[c]["v"] for c in cs], axis=0)
        sz = np.concatenate([tok_res[c]["sz"] for c in cs], axis=0)
        xbcT = np.concatenate([tok_res[c]["xbcT"] for c in cs], axis=1)
        dtda = np.concatenate([tok_res[c]["dtda"] for c in cs], axis=0)
        for r in range(4):
            chs = np.concatenate([np.arange(384 * r, 384 * r + 384), DI + np.arange(256 * r, 256 * r + 256),
                                  DI + 1024 + np.arange(256 * r, 256 * r + 256)])
            cw = np.asarray(p["conv_w"][l], np.float32)[:, chs]
            cb = np.asarray(p["conv_b"][l], np.float32)[chs]
            m = {
                "qT": np.ascontiguousarray(qT[128 * r:128 * (r + 1)]),
                "kT": np.ascontiguousarray(kT[128 * r:128 * (r + 1)]),
                "v": np.ascontiguousarray(v[:, 128 * r:128 * (r + 1)]),
                "xbcT": np.ascontiguousarray(xbcT[chs]),
                "conv_w": _c(cw.T.reshape(7, 128, 4).transpose(1, 0, 2)),
                "conv_b": _c(cb.reshape(7, 128).T),
                "sz": np.ascontiguousarray(sz[:, 384 * r:384 * (r + 1)]),
                "dtda": np.ascontiguousarray(np.concatenate([dtda[:, 6 * r:6 * r + 6],
                                                             dtda[:, NH + 6 * r:NH + 6 * r + 6]], axis=1)),
                "d_skip": _c(p["d_skip"][l][6 * r:6 * r + 6].reshape(1, 6)),
                "ssm_norm": _c(p["ssm_norm"][l][384 * r:384 * r + 384].reshape(1, 384)),
            }
            per_core.append(m)
    return per_core


def _mixout_inputs(mix_res, tok_res, B, NT):
    cpb = NCORES // B
    per_core = []
    for b in range(B):
        attnT = np.concatenate([mix_res[b * 4 + r]["attnT"] for r in range(4)], axis=0)
        yssmT = np.concatenate([mix_res[b * 4 + r]["yssmT"] for r in range(4)], axis=0)
        for i in range(cpb):
            c = b * cpb + i
            per_core.append({
                "x": tok_res[c]["x_out"],
                "attnT": np.ascontiguousarray(attnT[:, i * NT:(i + 1) * NT]),
                "yssmT": np.ascontiguousarray(yssmT[:, i * NT:(i + 1) * NT]),
                "sgT": tok_res[c]["sgT_out"],
            })
    return per_core


def forward(x, p):
    x = np.asarray(x, np.float32)
    B, SEQ, _ = x.shape
    NT = B * SEQ // NCORES
    flat = x.reshape(B * SEQ, D)
    per_core = [{"x": _c(flat[c * NT:(c + 1) * NT])} for c in range(NCORES)]
    shared = {}
    shared.update(_tok_shared_ffn("a", p, "ffn1", 0))
    shared.update(_tok_shared_proj(p, 0))
    tok = run_tok("A0", NT, per_core, shared)
    mix = run_mix(SEQ, _mix_inputs(tok, p, 0, B, SEQ))
    per_core = _mixout_inputs(mix, tok, B, NT)
    shared = {}
    shared.update(_tok_shared_mixout(p, 0))
    shared.update(_tok_shared_ffn("a", p, "ffn2", 0))
    shared.update(_tok_shared_ffn("b", p, "ffn1", 1))
    shared.update(_tok_shared_proj(p, 1))
    tok = run_tok("CA", NT, per_core, shared)
    mix = run_mix(SEQ, _mix_inputs(tok, p, 1, B, SEQ))
    per_core = _mixout_inputs(mix, tok, B, NT)
    shared = {}
    shared.update(_tok_shared_mixout(p, 1))
    shared.update(_tok_shared_ffn("a", p, "ffn2", 1))
    shared["fin_gain"] = _c(np.asarray(p["final_norm"]).reshape(1, D))
    fin = run_tok("C1", NT, per_core, shared)
    y = np.concatenate([np.asarray(fin[c]["y"], np.float32) for c in range(NCORES)], axis=0)
    return y.reshape(B, SEQ, D)


def kernel(x, ffn1_norm, ffn1_w_up, ffn1_w_down, mix_norm, w_in, conv_w, conv_b, dt_bias, a_log, d_skip,
           ssm_norm, w_branch_sb, w_branch_ssm, w_out, ffn2_norm, ffn2_w_up, ffn2_w_down, final_norm):
    p = dict(ffn1_norm=ffn1_norm, ffn1_w_up=ffn1_w_up, ffn1_w_down=ffn1_w_down, mix_norm=mix_norm, w_in=w_in,
             conv_w=conv_w, conv_b=conv_b, dt_bias=dt_bias, a_log=a_log, d_skip=d_skip, ssm_norm=ssm_norm,
             w_branch_sb=w_branch_sb, w_branch_ssm=w_branch_ssm, w_out=w_out, ffn2_norm=ffn2_norm,
             ffn2_w_up=ffn2_w_up, ffn2_w_down=ffn2_w_down, final_norm=final_norm)
    p = {k: np.asarray(v, np.float32) for k, v in p.items()}
    if FUSED:
        return forward_fused(x, p)
    return forward(x, p)


def _w_in_r(w_in, r):
    cols = np.concatenate([
        np.arange(128 * r, 128 * r + 128), 512 + np.arange(128 * r, 128 * r + 128),
        C_XBC + np.arange(384 * r, 384 * r + 384), C_XBC + DI + np.arange(256 * r, 256 * r + 256),
        C_XBC + DI + 1024 + np.arange(256 * r, 256 * r + 256),
        C_V + np.arange(128 * r, 128 * r + 128), C_Z + np.arange(384 * r, 384 * r + 384),
        C_DT + np.arange(6 * r, 6 * r + 6)])
    return np.ascontiguousarray(np.asarray(w_in, np.float32)[:, cols])


def forward_fused(x, p, trace=False):
    x = np.asarray(x, np.float32)
    B, SEQ, _ = x.shape
    NT = B * SEQ // NCORES
    key = ("fused", NT, SEQ)
    if key not in _PROGS:
        _PROGS[key] = FusedProg(NT, SEQ)
    prog = _PROGS[key]
    flat = x.reshape(B * SEQ, D)
    ctri, cmb = _mix_consts()
    sh = {"ident": np.eye(128, dtype=np.float32), "c_tri": ctri, "c_mb": cmb.astype(NPBF)}

    def pf(pfx, d):
        return {pfx + k: v for k, v in d.items()}
    sh.update(pf("p0_", _tok_shared_ffn("a", p, "ffn1", 0)))
    sh.update({"p0_mix_gain": _gain_t(p["mix_norm"][0]), "p0_w_in_g": _c(p["w_in"][0][:, C_GSB:C_GSB + 2 * D])})
    sh.update(pf("p3_", _tok_shared_mixout(p, 0)))
    sh.update(pf("p3_", _tok_shared_ffn("a", p, "ffn2", 0)))
    sh.update(pf("p3_", _tok_shared_ffn("b", p, "ffn1", 1)))
    sh.update({"p3_mix_gain": _gain_t(p["mix_norm"][1]), "p3_w_in_g": _c(p["w_in"][1][:, C_GSB:C_GSB + 2 * D])})
    sh.update(pf("p6_", _tok_shared_mixout(p, 1)))
    sh.update(pf("p6_", _tok_shared_ffn("a", p, "ffn2", 1)))
    sh["p6_fin_gain"] = _c(np.asarray(p["final_norm"]).reshape(1, D))
    per_r = []
    for r in range(4):
        d = {}
        for l, (pa, ps) in enumerate([("p1_", "p2_"), ("p4_", "p5_")]):
            chs = np.concatenate([np.arange(384 * r, 384 * r + 384), DI + np.arange(256 * r, 256 * r + 256),
                                  DI + 1024 + np.arange(256 * r, 256 * r + 256)])
            cw = np.asarray(p["conv_w"][l], np.float32)[:, chs]
            cb = np.asarray(p["conv_b"][l], np.float32)[chs]
            d[pa + "w_in_r"] = _w_in_r(p["w_in"][l], r)
            d[pa + "dt_bias_r"] = _c(p["dt_bias"][l][6 * r:6 * r + 6].reshape(1, 6))
            d[pa + "a_log_r"] = _c(p["a_log"][l][6 * r:6 * r + 6].reshape(1, 6))
            d[ps + "conv_w"] = _c(cw.T.reshape(7, 128, 4).transpose(1, 0, 2))
            d[ps + "conv_b"] = _c(cb.reshape(7, 128).T)
            d[ps + "d_skip"] = _c(p["d_skip"][l][6 * r:6 * r + 6].reshape(1, 6))
            d[ps + "ssm_norm"] = _c(p["ssm_norm"][l][384 * r:384 * r + 384].reshape(1, 384))
        per_r.append(d)
    in_maps = []
    for c in range(NCORES):
        m = dict(sh)
        m.update(per_r[c % 4])
        m["p0_x"] = _c(flat[c * NT:(c + 1) * NT])
        in_maps.append(m)
    res = run_bass_kernel_spmd(prog.nc, in_maps, core_ids=list(range(NCORES)), trace=trace)
    y = np.concatenate([np.asarray(res.results[c]["p6_y"], np.float32) for c in range(NCORES)], axis=0)
    if trace:
        print("exec_time_ns", res.exec_time_ns)
    return y.reshape(B, SEQ, D)
```

```python
import numpy as np
import ml_dtypes
from contextlib import ExitStack
import concourse.bass as bass
import concourse.mybir as mybir
from concourse.bass_utils import run_bass_kernel_spmd

F32 = mybir.dt.float32
BF16 = mybir.dt.bfloat16
AF = mybir.ActivationFunctionType
ALU = mybir.AluOpType
AX = mybir.AxisListType
NPBF = ml_dtypes.bfloat16

D = 1024
FH = 2816
SBW = 512
DI = 1536
CD = 3584
NH = 24
NG = 8
INW = 8728
EPS = 1e-6
NCORES = 8

SEM_LIMIT = 30000
FUSED = True


class Buf:
    __slots__ = ("name", "writers", "readers", "sem", "total")

    def __init__(self, name):
        self.name = name
        self.writers = []
        self.readers = []
        self.sem = None
        self.total = 0


class Op:
    __slots__ = ("eng", "fn", "is_dma", "deps", "signal", "sig_idx", "sem_buf", "sem_val", "inc")

    def __init__(self, eng, fn, is_dma):
        self.eng = eng
        self.fn = fn
        self.is_dma = is_dma
        self.deps = []
        self.signal = False
        self.sig_idx = None
        self.sem_buf = None
        self.sem_val = None
        self.inc = 16


class Sched:
    COMPUTE = ("pe", "act", "dve", "pool")

    def __init__(self, nc, same_engine_sync=True):
        self.nc = nc
        self.ops = []
        self.same_engine_sync = same_engine_sync
        self.dma_bufs = []
        self.n_sems = 0

    def _joins(self, b, op):
        return (b.writers and not b.readers and (op.is_dma or op.eng == "pe") and
                all(w.eng == op.eng and w.is_dma == op.is_dma for w in b.writers))

    def _add(self, op, reads, writes):
        deps = []
        for b in reads:
            deps.extend(b.writers)
        jn = [self._joins(b, op) for b in writes]
        for b, j in zip(writes, jn):
            if not j:
                deps.extend(b.writers)
                deps.extend(b.readers)
        seen = set()
        for d in deps:
            if d is op or id(d) in seen:
                continue
            seen.add(id(d))
            if (not d.is_dma) and (not op.is_dma) and d.eng == op.eng:
                if op.eng == "pe" or not self.same_engine_sync:
                    continue
            op.deps.append(d)
            if not d.is_dma:
                d.signal = True
        for b, j in zip(writes, jn):
            if j:
                b.writers.append(op)
            else:
                b.writers = [op]
                b.readers = []
        for b in reads:
            if b not in writes:
                b.readers.append(op)
        self.ops.append(op)
        return op

    def op(self, eng, fn, reads=(), writes=()):
        return self._add(Op(eng, fn, False), list(reads), list(writes))

    def dma(self, q, out, in_, reads, writes, sem_of, **kw):
        def fn(e, out=out, in_=in_, kw=kw):
            return e.dma_start(out=out, in_=in_, **kw)
        return self.dma_fn(q, fn, reads, writes, sem_of)

    def dma_fn(self, q, fn, reads, writes, sem_of, inc=16):
        o = Op(q, fn, True)
        o.sem_buf = sem_of
        o.inc = inc
        if sem_of.total == 0 and sem_of not in self.dma_bufs:
            self.dma_bufs.append(sem_of)
        sem_of.total += inc
        o.sem_val = sem_of.total
        return self._add(o, list(reads), list(writes))

    def coll(self, ins_ap, outs_ap, groups, reads, writes, sem_of):
        def fn(e):
            return e.collective_compute("AllGather", ALU.bypass, replica_groups=groups, ins=[ins_ap], outs=[outs_ap])
        o = self.dma_fn("pool", fn, reads, writes, sem_of, inc=1)
        prev = getattr(self, "_last_coll", None)
        if prev is not None and prev not in o.deps:
            o.deps.append(prev)
        self._last_coll = o
        return o

    def emit(self, barrier=False):
        nc = self.nc
        engs = {"pe": nc.tensor, "act": nc.scalar, "dve": nc.vector, "pool": nc.gpsimd, "sp": nc.sync}
        cnt = {e: 0 for e in self.COMPUTE}
        for o in self.ops:
            if not o.is_dma and o.signal:
                o.sig_idx = cnt[o.eng]
                cnt[o.eng] += 1
        esem = {}
        for e in self.COMPUTE:
            n = (cnt[e] + SEM_LIMIT - 1) // SEM_LIMIT
            esem[e] = [nc.alloc_semaphore(f"s_{e}{i}_{nc.next_id()}") for i in range(max(n, 1))]
        for b in self.dma_bufs:
            b.sem = nc.alloc_semaphore(f"d_{b.name}_{nc.next_id()}")
        self.n_sems = sum(len(v) for v in esem.values()) + len(self.dma_bufs)
        streams = {e: [] for e in engs}
        for o in self.ops:
            streams[o.eng].append(o)
        waited = {e: {} for e in engs}

        def target(d):
            if d.is_dma:
                return d.sem_buf.sem, d.sem_val
            i = d.sig_idx
            return esem[d.eng][i // SEM_LIMIT], i % SEM_LIMIT + 1

        def emit_stream(ename):
            e = engs[ename]
            w = waited[ename]
            for o in streams[ename]:
                need = {}
                for d in o.deps:
                    sm, v = target(d)
                    k = id(sm)
                    if w.get(k, 0) >= v:
                        continue
                    if k not in need or need[k][1] < v:
                        need[k] = (sm, v)
                for k, (sm, v) in need.items():
                    e.wait_ge(sm, v)
                    w[k] = v
                ins = o.fn(e)
                if o.is_dma:
                    ins.then_inc(o.sem_buf.sem, o.inc)
                elif o.signal:
                    sm, _ = target(o)
                    ins.then_inc(sm, 1)
            if ename == "sp" or barrier:
                for b in self.dma_bufs:
                    e.wait_ge(b.sem, b.total)
            if barrier:
                for en in self.COMPUTE:
                    if cnt[en] > 0:
                        i = cnt[en] - 1
                        e.wait_ge(esem[en][i // SEM_LIMIT], i % SEM_LIMIT + 1)
                e.drain()

        with nc.Block() as block:
            @block.sync
            def _(eng):
                emit_stream("sp")

            @block.tensor
            def _(eng):
                emit_stream("pe")

            @block.scalar
            def _(eng):
                emit_stream("act")

            @block.vector
            def _(eng):
                emit_stream("dve")

            @block.gpsimd
            def _(eng):
                emit_stream("pool")
        if barrier:
            nc.all_engine_barrier()


class Ring:
    def __init__(self, items):
        self.items = items
        self.i = 0

    def next(self):
        it = self.items[self.i % len(self.items)]
        self.i += 1
        return it


T = 512
NB = T // 128

C_Q, C_K, C_V, C_Z, C_XBC, C_DT, C_GSB, C_GSSM = 0, 512, 1024, 1536, 3072, 6656, 6680, 7704


class TokProg:
    def __init__(self, stage, NT, nc=None, fz=None, pfx=""):
        self.stage = stage
        self.NT = NT
        self.fz = fz
        self.pfx = pfx
        self.has_mixout = stage in ("CA", "C1")
        self.ffns = {"A0": ["a"], "CA": ["a", "b"], "C1": ["a"]}[stage]
        self.has_proj = stage in ("A0", "CA")
        self.has_final = stage == "C1"
        self.nc = nc = nc if nc is not None else bass.Bass("TRN2", target_bir_lowering=False)
        self.S = Sched(nc)
        self._n = 0
        self.build()

    def sb(self, shape, dt, name=None):
        self._n += 1
        return self.nc.alloc_sbuf_tensor(f"{self.pfx}{name or 'sb'}_{self._n}", list(shape), dt).ap()

    def ps(self, shape, dt, name=None):
        self._n += 1
        return self.nc.alloc_psum_tensor(f"{self.pfx}{name or 'ps'}_{self._n}", list(shape), dt).ap()

    def din(self, name, shape, dt):
        return self.nc.dram_tensor(self.pfx + name, list(shape), dt, kind="ExternalInput").ap()

    def dout(self, name, shape, dt):
        return self.nc.dram_tensor(self.pfx + name, list(shape), dt, kind="ExternalOutput").ap()

    def dscr(self, name, shape, dt):
        return self.nc.dram_tensor(self.pfx + name, list(shape), dt, kind="Internal").ap()

    def cast_weight(self, name, w, K, c0, ntiles, ncols):
        KC = K // 128
        scr = self.dscr(name + "_bf", [ntiles, 128, KC, ncols], BF16)
        b = Buf(name + "_bf")
        src = w.rearrange("(kc p) n -> kc p n", p=128)
        for kc in range(KC):
            s = src[kc][:, c0:c0 + ntiles * ncols].rearrange("p (nt nn) -> p nt nn", nn=ncols)
            d = scr[:, :, kc, :].rearrange("nt p nn -> p nt nn")
            self.S.dma("pool", d, s, [], [b], b, max_dma_last_dim=4096)
        return scr, b

    def build(self):
        nc, S = self.nc, self.S
        NT = self.NT
        ntiles = NT // T
        fz = self.fz
        x_in = fz["x_src"] if fz and fz.get("x_src") is not None else self.din("x", [NT, D], F32)
        ident_d = fz["ident"] if fz else self.din("ident", [128, 128], F32)
        B_dram_out = Buf("dram_out")
        W = {}
        if self.has_mixout:
            if fz:
                sgT_d = fz["sg_src"]
            else:
                attnT_d = self.din("attnT", [SBW, NT], BF16)
                yssmT_d = self.din("yssmT", [DI, NT], BF16)
                sgT_d = self.din("sgT", [2 * D, NT], BF16)
            w_bsb = self.din("w_bsb", [SBW, D], F32)
            w_bssm = self.din("w_bssm", [DI, D], F32)
            w_out = self.din("w_out", [D, D], F32)
            W["bsb"] = self.cast_weight("w_bsb", w_bsb, SBW, 0, 4, 256)
            W["bssm"] = self.cast_weight("w_bssm", w_bssm, DI, 0, 4, 256)
            W["out"] = self.cast_weight("w_out", w_out, D, 0, 1, 1024)
        for k in self.ffns:
            g = self.din(f"f{k}_gain", [128, 8], F32)
            up = self.din(f"f{k}_up", [D, 2 * FH], F32)
            dn = self.din(f"f{k}_dn", [FH, D], F32)
            W[f"f{k}_gain"] = g
            W[f"f{k}_up"] = self.cast_weight(f"f{k}_up", up, D, 0, 22, 256)
            W[f"f{k}_dn"] = self.cast_weight(f"f{k}_dn", dn, FH, 0, 1, 1024)
        if self.has_proj:
            W["mix_gain"] = self.din("mix_gain", [128, 8], F32)
            if fz:
                w_in_g = self.din("w_in_g", [D, 2 * D], F32)
                W["in_g"] = self.cast_weight("w_in_g", w_in_g, D, 0, 8, 256)
                x_out = fz["x_dst"]
                sgT_o = fz["sg_dst"]
            else:
                w_in = self.din("w_in", [D, INW], F32)
                dt_bias_d = self.din("dt_bias", [1, NH], F32)
                a_log_d = self.din("a_log", [1, NH], F32)
                W["in_g"] = self.cast_weight("w_in_g", w_in, D, C_GSB, 8, 256)
                W["in_qk"] = self.cast_weight("w_in_qk", w_in, D, C_Q, 4, 256)
                W["in_v"] = self.cast_weight("w_in_v", w_in, D, C_V, 1, 512)
                W["in_z"] = self.cast_weight("w_in_z", w_in, D, C_Z, 3, 512)
                W["in_xbc"] = self.cast_weight("w_in_xbc", w_in, D, C_XBC, 14, 256)
                W["in_dt"] = self.cast_weight("w_in_dt", w_in, D, C_DT, 1, NH)
                x_out = self.dout("x_out", [NT, D], F32)
                qT_o = self.dout("qT", [SBW, NT], BF16)
                kT_o = self.dout("kT", [SBW, NT], BF16)
                v_o = self.dout("v", [NT, SBW], BF16)
                sz_o = self.dout("sz", [NT, DI], BF16)
                xbcT_o = self.dout("xbcT", [CD, NT], BF16)
                dtda_o = self.dout("dtda", [NT, 2 * NH], F32)
                sgT_o = self.dout("sgT_out", [2 * D, NT], BF16)
        if self.has_final:
            fin_gain_d = self.din("fin_gain", [1, D], F32)
            y_o = self.dout("y", [NT, D], F32)

        xt = self.sb([128, NB, D], F32, "xt")
        B_x = [Buf(f"x{b}") for b in range(NB)]
        junk = self.sb([128, D], F32, "junk"); B_junk = Buf("junk")
        hn = [self.sb([128, D], BF16, f"hn{i}") for i in range(2)]
        R_hn = Ring([(hn[i], Buf(f"hn{i}")) for i in range(2)])
        hT = self.sb([128, 8, T], BF16, "hT")
        B_hT = [Buf(f"hT{b}") for b in range(NB)]
        gT = self.sb([128, 22, T], BF16, "gT")
        B_gT = [Buf(f"gT{j}") for j in range(22)]
        wdn = self.sb([128, 22, D], BF16, "wdn"); B_wdn = Buf("wdn")
        NSLOT = 4
        R_w = Ring([(self.sb([128, 4096], BF16, f"wslot{i}"), Buf(f"wslot{i}")) for i in range(NSLOT)])
        R_tmp = Ring([(self.sb([128, T], F32, f"tmp{i}"), Buf(f"tmp{i}")) for i in range(4)])
        stat = self.sb([128, 16], F32, "stat"); B_stat = Buf("stat")
        neghalf = self.sb([128, 4], F32, "neghalf"); B_nh = Buf("neghalf")
        ident_f = self.sb([128, 128], F32, "ident_f"); B_idf = Buf("ident_f")
        ident_b = self.sb([128, 128], BF16, "ident_b"); B_idb = Buf("ident_b")
        gains = {}
        ptr = self.ps([128, 1024], BF16, "ptr"); B_ptr = Buf("ptr")
        R_pA = Ring([(self.ps([128, 512], F32, f"pA{i}"), Buf(f"pA{i}")) for i in range(2)])
        R_pB = Ring([(self.ps([128, 512], F32, f"pB{i}"), Buf(f"pB{i}")) for i in range(2)])
        R_pO = Ring([(self.ps([128, 512], F32, f"pO{i}"), Buf(f"pO{i}")) for i in range(2)])
        pS = self.ps([128, 512], F32, "pS"); B_pS = Buf("pS")

        S.dma("sp", ident_f, ident_d, [], [B_idf], B_idf)
        S.op("dve", lambda e: e.tensor_copy(out=ident_b, in_=ident_f), [B_idf], [B_idb])
        S.op("pool", lambda e: e.memset(neghalf, -0.5), [], [B_nh])
        for key in ([f"f{k}_gain" for k in self.ffns] + (["mix_gain"] if self.has_proj else [])):
            gt = self.sb([128, 8], F32, "g_" + key)
            b = Buf("g_" + key)
            S.dma("sp", gt, W[key], [], [b], b)
            gains[key] = (gt, b)
        if self.has_mixout:
            wout_sb = self.sb([128, 8, D], BF16, "wout_sb"); B_wout = Buf("wout_sb")
            S.dma("sp", wout_sb, W["out"][0][0], [W["out"][1]], [B_wout], B_wout)
            attnT_sb = self.sb([128, 4, T], BF16, "attnT_sb"); B_attn = Buf("attnT_sb")
            yssmT_sb = self.sb([128, 12, T], BF16, "yssmT_sb"); B_yssm = Buf("yssmT_sb")
            R_sg = Ring([(self.sb([128, 4, T], BF16, f"sg_sb{i}"), Buf(f"sg_sb{i}")) for i in range(2)])
            mT = self.sb([128, 8, T], BF16, "mT"); B_mT = [Buf(f"mT{n}") for n in range(8)]
        if self.has_proj:
            R_ost = Ring([(self.sb([128, 4, T], BF16, f"ost{i}"), Buf(f"ost{i}")) for i in range(3)])
        if self.has_proj and fz:
            B_hsend = Buf("hsend")
            B_hrecv = Buf("hrecv")
        if self.has_proj and not fz:
            dtb4 = self.sb([128, NB, NH], F32, "dtb4"); B_dtb = Buf("dtb4")
            nega4 = self.sb([128, NB, NH], F32, "nega4"); B_nega = Buf("nega4")
            for b in range(NB):
                S.dma("sp", dtb4[:, b, :], dt_bias_d.partition_broadcast(128), [], [B_dtb], B_dtb)
                S.dma("sp", nega4[:, b, :], a_log_d.partition_broadcast(128), [], [B_nega], B_nega)
            S.op("act", lambda e: e.activation(out=nega4, in_=nega4, func=AF.Exp), [B_nega], [B_nega])
            S.op("dve", lambda e: e.tensor_scalar(out=nega4, in0=nega4, scalar1=-1.0, scalar2=None, op0=ALU.mult),
                 [B_nega], [B_nega])
            dts = self.sb([128, 4, NB * NH], F32, "dts"); B_dts = Buf("dts")
            dtda_st = self.sb([128, NB, 2 * NH], F32, "dtda_st"); B_dtda = Buf("dtda_st")
        if self.has_final:
            fing = self.sb([128, D], F32, "fing"); B_fing = Buf("fing")
            S.dma("sp", fing, fin_gain_d.partition_broadcast(128), [], [B_fing], B_fing)
            R_fo = Ring([(self.sb([128, D], F32, f"fo{i}"), Buf(f"fo{i}")) for i in range(2)])

        def load_wtile(key, ti, dst_view, slotbuf):
            scr, b = W[key]
            S.dma("sp", dst_view, scr[ti], [b], [slotbuf], slotbuf)

        def rmsnorm_hT(gain_key):
            gt, gb = gains[gain_key]
            for b in range(NB):
                S.op("act", lambda e, b=b: e.activation(out=junk, in_=xt[:, b, :], func=AF.Square,
                                                       accum_out=stat[:, b:b + 1]),
                     [B_x[b]], [B_junk, B_stat])
            S.op("dve", lambda e: e.tensor_scalar(out=stat[:, 4:8], in0=stat[:, 0:4], scalar1=1.0 / D, scalar2=EPS,
                                                  op0=ALU.mult, op1=ALU.add), [B_stat], [B_stat])
            S.op("pool", lambda e: e.tensor_tensor(out=stat[:, 8:12], in0=stat[:, 4:8], in1=neghalf, op=ALU.pow),
                 [B_stat, B_nh], [B_stat])
            for b in range(NB):
                h, hb = R_hn.next()
                S.op("act", lambda e, b=b, h=h: e.activation(out=h, in_=xt[:, b, :], func=AF.Copy,
                                                             scale=stat[:, 8 + b:9 + b]),
                     [B_x[b], B_stat], [hb])
                for kc in range(8):
                    S.op("pe", lambda e, kc=kc, h=h: e.transpose(ptr[:, kc * 128:(kc + 1) * 128],
                                                                 h[:, kc * 128:(kc + 1) * 128], ident_b),
                         [hb, B_idb], [B_ptr])
                S.op("dve", lambda e, b=b: e.tensor_tensor(
                    out=hT[:, :, b * 128:(b + 1) * 128], in0=ptr.rearrange("p (k t) -> p k t", k=8),
                    in1=gt.unsqueeze(2).to_broadcast([128, 8, 128]), op=ALU.mult),
                    [B_ptr, gb], [B_hT[b]])

        def ffn(k):
            rmsnorm_hT(f"f{k}_gain")
            scr_dn, b_dn = W[f"f{k}_dn"]
            for i in range(11):
                if i == 3:
                    S.dma("sp", wdn[:, 0:11, :], scr_dn[0][:, 0:11, :], [b_dn], [B_wdn], B_wdn)
                    S.dma("sp", wdn[:, 11:22, :], scr_dn[0][:, 11:22, :], [b_dn], [B_wdn], B_wdn)
                slot, sbuf = R_w.next()
                sv = slot.rearrange("p (two kc n) -> p two kc n", two=2, kc=8)
                load_wtile(f"f{k}_up", i, sv[:, 0], sbuf)
                load_wtile(f"f{k}_up", 11 + i, sv[:, 1], sbuf)
                for jj in range(2):
                    j = 2 * i + jj
                    pg, pgb = R_pA.next()
                    pu, pub = R_pB.next()
                    for kc in range(8):
                        S.op("pe", lambda e, kc=kc, jj=jj, pg=pg, sv=sv: e.matmul(
                            pg, lhsT=sv[:, 0, kc, jj * 128:(jj + 1) * 128], rhs=hT[:, kc, :],
                            start=(kc == 0), stop=(kc == 7)), [sbuf] + B_hT, [pgb])
                    for kc in range(8):
                        S.op("pe", lambda e, kc=kc, jj=jj, pu=pu, sv=sv: e.matmul(
                            pu, lhsT=sv[:, 1, kc, jj * 128:(jj + 1) * 128], rhs=hT[:, kc, :],
                            start=(kc == 0), stop=(kc == 7)), [sbuf] + B_hT, [pub])
                    tmp, tb = R_tmp.next()
                    S.op("act", lambda e, pg=pg, tmp=tmp: e.activation(out=tmp, in_=pg, func=AF.Silu), [pgb], [tb])
                    S.op("dve", lambda e, j=j, pu=pu, tmp=tmp: e.tensor_tensor(out=gT[:, j, :], in0=pu, in1=tmp,
                                                                               op=ALU.mult), [pub, tb], [B_gT[j]])
            for half in range(2):
                for b in range(NB):
                    po, pob = R_pO.next()
                    for j in range(22):
                        S.op("pe", lambda e, j=j, b=b, half=half, po=po: e.matmul(
                            po, lhsT=gT[:, j, b * 128:(b + 1) * 128], rhs=wdn[:, j, half * 512:(half + 1) * 512],
                            start=(j == 0), stop=(j == 21)), [B_gT[j], B_wdn], [pob])
                    S.op("dve", lambda e, b=b, half=half, po=po: e.scalar_tensor_tensor(
                        out=xt[:, b, half * 512:(half + 1) * 512], in0=po, scalar=0.5,
                        in1=xt[:, b, half * 512:(half + 1) * 512], op0=ALU.mult, op1=ALU.add),
                        [pob, B_x[b]], [B_x[b]])

        def mixout(t0, after_first_loads=None):
            if fz:
                ti_ = t0 // T
                if ti_ == 0:
                    arecv, yrecv = fz["arecv"], fz["yrecv"]
                    self.amine = self.dscr("amine", [ntiles * 4, 128 * 512], BF16)
                    self.ymine = self.dscr("ymine", [ntiles * 4, 384 * 512], BF16)
                    self.B_amine, self.B_ymine = Buf("amine"), Buf("ymine")

                    def cp_a(e):
                        self._rank = e.partition_id() % 4
                        src = arecv.rearrange("(r i) h p t -> r (i h) (p t)", i=ntiles)[bass.ds(self._rank, 1)]
                        return e.dma_start(out=self.amine, in_=src.rearrange("o q n -> (o q) n"))

                    def cp_y(e):
                        src = yrecv.rearrange("(r i) h j p t -> r (i h j) (p t)", i=ntiles // 2)[bass.ds(self._rank, 1)]
                        return e.dma_start(out=self.ymine, in_=src.rearrange("o q n -> (o q) n"))
                    S.dma_fn("sp", cp_a, [], [self.B_amine], self.B_amine)
                    S.dma_fn("sp", cp_y, [], [self.B_ymine], self.B_ymine)
                S.dma("sp", attnT_sb, self.amine.rearrange("(i h) (p t) -> i p h t", h=4, t=512)[ti_],
                      [self.B_amine], [B_attn], B_attn)
                yv = self.ymine.rearrange("(i h j) (c p t) -> i j h p c t", h=4, j=2, c=3, t=512)[ti_ // 2, ti_ % 2]
                for h in range(4):
                    S.dma("sp", yssmT_sb[:, 3 * h:3 * h + 3, :], yv[h], [self.B_ymine], [B_yssm], B_yssm)
            else:
                S.dma("sp", attnT_sb, attnT_d.rearrange("(kc p) t -> p kc t", p=128)[:, :, t0:t0 + T],
                      [], [B_attn], B_attn)
                S.dma("sp", yssmT_sb, yssmT_d.rearrange("(kc p) t -> p kc t", p=128)[:, :, t0:t0 + T],
                      [], [B_yssm], B_yssm)
            sgv = sgT_d.rearrange("(kc p) t -> p kc t", p=128)
            for i in range(4):
                sg_sb, B_sg = R_sg.next()
                S.dma("sp", sg_sb[:, 0:2, :], sgv[:, 2 * i:2 * i + 2, t0:t0 + T], [], [B_sg], B_sg)
                S.dma("sp", sg_sb[:, 2:4, :], sgv[:, 8 + 2 * i:8 + 2 * i + 2, t0:t0 + T], [], [B_sg], B_sg)
                slot, sbuf = R_w.next()
                v_sb = slot[:, 0:1024].rearrange("p (kc n) -> p kc n", kc=4)
                v_ss = slot[:, 1024:4096].rearrange("p (kc n) -> p kc n", kc=12)
                load_wtile("bsb", i, v_sb, sbuf)
                load_wtile("bssm", i, v_ss, sbuf)
                if i == 0 and after_first_loads is not None:
                    after_first_loads()
                for jj in range(2):
                    n = 2 * i + jj
                    p1, p1b = R_pA.next()
                    p2, p2b = R_pB.next()
                    for kc in range(4):
                        S.op("pe", lambda e, kc=kc, jj=jj, p1=p1, v_sb=v_sb: e.matmul(
                            p1, lhsT=v_sb[:, kc, jj * 128:(jj + 1) * 128], rhs=attnT_sb[:, kc, :],
                            start=(kc == 0), stop=(kc == 3)), [sbuf, B_attn], [p1b])
                    for kc in range(12):
                        S.op("pe", lambda e, kc=kc, jj=jj, p2=p2, v_ss=v_ss: e.matmul(
                            p2, lhsT=v_ss[:, kc, jj * 128:(jj + 1) * 128], rhs=yssmT_sb[:, kc, :],
                            start=(kc == 0), stop=(kc == 11)), [sbuf, B_yssm], [p2b])
                    t1, t1b = R_tmp.next()
                    t2, t2b = R_tmp.next()
                    S.op("dve", lambda e, jj=jj, p1=p1, t1=t1, sg_sb=sg_sb: e.tensor_tensor(
                        out=t1, in0=p1, in1=sg_sb[:, jj, :], op=ALU.mult), [p1b, B_sg], [t1b])
                    S.op("dve", lambda e, jj=jj, p2=p2, t2=t2, sg_sb=sg_sb: e.tensor_tensor(
                        out=t2, in0=p2, in1=sg_sb[:, 2 + jj, :], op=ALU.mult), [p2b, B_sg], [t2b])
                    S.op("pool", lambda e, n=n, t1=t1, t2=t2: e.tensor_tensor(out=mT[:, n, :], in0=t1, in1=t2,
                                                                              op=ALU.add), [t1b, t2b], [B_mT[n]])
            for b in range(NB):
                for half in range(2):
                    po, pob = R_pO.next()
                    for kc in range(8):
                        S.op("pe", lambda e, kc=kc, b=b, half=half, po=po: e.matmul(
                            po, lhsT=mT[:, kc, b * 128:(b + 1) * 128], rhs=wout_sb[:, kc, half * 512:(half + 1) * 512],
                            start=(kc == 0), stop=(kc == 7)), [B_mT[kc], B_wout], [pob])
                    S.op("dve", lambda e, b=b, half=half, po=po: e.tensor_tensor(
                        out=xt[:, b, half * 512:(half + 1) * 512], in0=po,
                        in1=xt[:, b, half * 512:(half + 1) * 512], op=ALU.add), [pob, B_x[b]], [B_x[b]])

        def proj(t0):
            for b in range(NB):
                S.dma("pool", x_out[t0 + b * 128:t0 + (b + 1) * 128, :], xt[:, b, :], [B_x[b]], [B_dram_out], B_x[b])
            rmsnorm_hT("mix_gain")
            fm = []
            if fz:
                ti_ = t0 // T
                hs = fz["hsend"][ti_]
                S.dma("pool", hs, hT, B_hT, [B_hsend], B_hsend)
                S.coll(hs.rearrange("p k t -> p (k t)"), fz["hrecv"][ti_].rearrange("r p k t -> (r p) (k t)"),
                       fz["groups"], [B_hsend], [B_hrecv], B_hrecv)
            else:
                fm.append(("in_qk", 0, qT_o[0:512, t0:t0 + T], "q"))
                fm.append(("in_qk", 2, kT_o[0:512, t0:t0 + T], "c"))
                for i in range(7):
                    fm.append(("in_xbc", 2 * i, xbcT_o[i * 512:(i + 1) * 512, t0:t0 + T], "c"))
            for i in range(4):
                fm.append(("in_g", 2 * i, sgT_o[i * 512:(i + 1) * 512, t0:t0 + T], "g"))
            for (wkey, ti, oap, kind) in fm:
                slot, sbuf = R_w.next()
                sv = slot.rearrange("p (two kc n) -> p two kc n", two=2, kc=8)
                load_wtile(wkey, ti, sv[:, 0], sbuf)
                load_wtile(wkey, ti + 1, sv[:, 1], sbuf)
                ost, ostb = R_ost.next()
                for c in range(4):
                    pa, pab = (R_pA if c % 2 == 0 else R_pB).next()
                    for kc in range(8):
                        S.op("pe", lambda e, kc=kc, c=c, pa=pa, sv=sv: e.matmul(
                            pa, lhsT=sv[:, c // 2, kc, (c % 2) * 128:(c % 2 + 1) * 128], rhs=hT[:, kc, :],
                            start=(kc == 0), stop=(kc == 7)), [sbuf] + B_hT, [pab])
                    if kind == "g":
                        S.op("act", lambda e, c=c, pa=pa, ost=ost: e.activation(out=ost[:, c, :], in_=pa,
                                                                                func=AF.Sigmoid), [pab], [ostb])
                    else:
                        sc = (128.0 ** -0.5) if kind == "q" else 1.0
                        S.op("dve", lambda e, c=c, pa=pa, ost=ost, sc=sc: e.tensor_scalar(
                            out=ost[:, c, :], in0=pa, scalar1=sc, scalar2=None, op0=ALU.mult), [pab], [ostb])
                S.dma("pool", oap.rearrange("(c p) t -> p c t", p=128), ost, [ostb], [B_dram_out], ostb)
            if fz:
                return
            tm = [("in_v", 0, v_o, 0, "c")] + [("in_z", i, sz_o, i * 512, "s") for i in range(3)]
            for (wkey, ti, oten, c0, kind) in tm:
                slot, sbuf = R_w.next()
                sv = slot.rearrange("p (kc n) -> p kc n", kc=8)
                load_wtile(wkey, ti, sv, sbuf)
                ost, ostb = R_ost.next()
                for b in range(NB):
                    po, pob = R_pO.next()
                    for kc in range(8):
                        S.op("pe", lambda e, kc=kc, b=b, po=po, sv=sv: e.matmul(
                            po, lhsT=hT[:, kc, b * 128:(b + 1) * 128], rhs=sv[:, kc, :],
                            start=(kc == 0), stop=(kc == 7)), [sbuf, B_hT[b]], [pob])
                    if kind == "s":
                        S.op("act", lambda e, b=b, po=po, ost=ost: e.activation(out=ost[:, b, :], in_=po,
                                                                                func=AF.Silu), [pob], [ostb])
                    else:
                        S.op("dve", lambda e, b=b, po=po, ost=ost: e.tensor_copy(out=ost[:, b, :], in_=po),
                             [pob], [ostb])
                S.dma("pool", oten[t0:t0 + T, c0:c0 + 512].rearrange("(b p) n -> p b n", p=128), ost,
                      [ostb], [B_dram_out], ostb)
            slot, sbuf = R_w.next()
            sv = slot[:, 0:8 * NH].rearrange("p (kc n) -> p kc n", kc=8)
            load_wtile("in_dt", 0, sv, sbuf)
            for b in range(NB):
                for kc in range(8):
                    S.op("pe", lambda e, kc=kc, b=b, sv=sv: e.matmul(
                        pS[:, b * NH:(b + 1) * NH], lhsT=hT[:, kc, b * 128:(b + 1) * 128], rhs=sv[:, kc, :],
                        start=(kc == 0), stop=(kc == 7)), [sbuf, B_hT[b]], [B_pS])
            NN = NB * NH
            dtb_f = dtb4.rearrange("p b n -> p (b n)")
            nega_f = nega4.rearrange("p b n -> p (b n)")
            S.op("dve", lambda e: e.tensor_tensor(out=dts[:, 0, :], in0=pS[:, 0:NN], in1=dtb_f, op=ALU.add),
                 [B_pS, B_dtb], [B_dts])
            S.op("act", lambda e: e.activation(out=dts[:, 1, :], in_=dts[:, 0, :], func=AF.Abs), [B_dts], [B_dts])
            S.op("act", lambda e: e.activation(out=dts[:, 2, :], in_=dts[:, 1, :], func=AF.Exp, scale=-1.0),
                 [B_dts], [B_dts])
            S.op("act", lambda e: e.activation(out=dts[:, 3, :], in_=dts[:, 2, :], func=AF.Ln, bias=1.0),
                 [B_dts], [B_dts])
            S.op("dve", lambda e: e.scalar_tensor_tensor(
                out=dtda_st[:, :, 0:NH], in0=dts[:, 0, :].rearrange("p (b n) -> p b n", b=NB), scalar=0.0,
                in1=dts[:, 3, :].rearrange("p (b n) -> p b n", b=NB), op0=ALU.max, op1=ALU.add),
                [B_dts], [B_dtda])
            S.op("dve", lambda e: e.tensor_tensor(out=dtda_st[:, :, NH:2 * NH], in0=dtda_st[:, :, 0:NH],
                                                  in1=nega4, op=ALU.mult), [B_dtda, B_nega], [B_dtda])
            S.dma("pool", dtda_o[t0:t0 + T, :].rearrange("(b p) n -> p b n", p=128), dtda_st,
                  [B_dtda], [B_dram_out], B_dtda)

        def final(t0):
            for b in range(NB):
                S.op("act", lambda e, b=b: e.activation(out=junk, in_=xt[:, b, :], func=AF.Square,
                                                       accum_out=stat[:, b:b + 1]), [B_x[b]], [B_junk, B_stat])
            S.op("dve", lambda e: e.tensor_scalar(out=stat[:, 4:8], in0=stat[:, 0:4], scalar1=1.0 / D, scalar2=EPS,
                                                  op0=ALU.mult, op1=ALU.add), [B_stat], [B_stat])
            S.op("pool", lambda e: e.tensor_tensor(out=stat[:, 8:12], in0=stat[:, 4:8], in1=neghalf, op=ALU.pow),
                 [B_stat, B_nh], [B_stat])
            for b in range(NB):
                fo, fob = R_fo.next()
                S.op("dve", lambda e, b=b, fo=fo: e.scalar_tensor_tensor(
                    out=fo, in0=xt[:, b, :], scalar=stat[:, 8 + b:9 + b], in1=fing, op0=ALU.mult, op1=ALU.mult),
                    [B_x[b], B_stat, B_fing], [fob])
                S.dma("pool", y_o[t0 + b * 128:t0 + (b + 1) * 128, :], fo, [fob], [B_dram_out], fob)

        for ti in range(ntiles):
            t0 = ti * T
            def load_x(t0=t0):
                for b in range(NB):
                    S.dma("sp", xt[:, b, :], x_in[t0 + b * 128:t0 + (b + 1) * 128, :], [], [B_x[b]], B_x[b])
            fi = 0
            if not self.has_mixout:
                load_x()
            if self.has_mixout:
                mixout(t0, load_x)
                ffn(self.ffns[fi]); fi += 1
            if fi < len(self.ffns):
                ffn(self.ffns[fi]); fi += 1
            if self.has_proj:
                proj(t0)
            if self.has_final:
                final(t0)
        S.emit(barrier=bool(fz))


class MixProg:
    def __init__(self, SEQ, do_attn=True, do_ssd=True, nc=None, fz=None, pfx=""):
        self.SEQ = SEQ
        self.do_attn = do_attn
        self.do_ssd = do_ssd
        self.fz = fz
        self.pfx = pfx
        self.nc = nc if nc is not None else bass.Bass("TRN2", target_bir_lowering=False)
        self.S = Sched(self.nc)
        self._n = 0
        self.build()

    def sb(self, shape, dt, name=None):
        self._n += 1
        return self.nc.alloc_sbuf_tensor(f"{self.pfx}{name or 'sb'}_{self._n}", list(shape), dt).ap()

    def din(self, name, shape, dt):
        return self.nc.dram_tensor(self.pfx + name, list(shape), dt, kind="ExternalInput").ap()

    def dout(self, name, shape, dt):
        return self.nc.dram_tensor(self.pfx + name, list(shape), dt, kind="ExternalOutput").ap()

    def dscr(self, name, shape, dt):
        return self.nc.dram_tensor(self.pfx + name, list(shape), dt, kind="Internal").ap()

    def build(self):
        nc, S = self.nc, self.S
        SEQ = self.SEQ
        NBLK = SEQ // 128
        NGRP = SEQ // 512
        B_dram_out = Buf("dram_out")
        fz = self.fz
        if self.do_ssd:
            convw_d = self.din("conv_w", [128, 7, 4], F32)
            convb_d = self.din("conv_b", [128, 7], F32)
            dskip_d = self.din("d_skip", [1, 6], F32)
            ssmn_d = self.din("ssm_norm", [1, 384], F32)
        NTLG = SEQ // 512
        B_scr = {k: [Buf(f"{k}{g}") for g in range(NTLG)] for k in ("xbc", "sz", "dtda")}
        if fz:
            if self.do_attn:
                xbc_d = self.dscr("xbc_scr", [896, SEQ], BF16)
                sz_d = self.dscr("sz_scr", [SEQ, 384], BF16)
                dtda_d = self.dscr("dtda_scr", [SEQ, 12], F32)
                fz["scr"] = (xbc_d, sz_d, dtda_d)
                winr_d = self.din("w_in_r", [D, 1670], F32)
                dtbr_d = self.din("dt_bias_r", [1, 6], F32)
                alogr_d = self.din("a_log_r", [1, 6], F32)
            else:
                xbc_d, sz_d, dtda_d = fz["scr"]
            ident_d, ctri_d, cmb_d = fz["ident"], fz["c_tri"], fz["c_mb"]
        else:
            qT_d = self.din("qT", [128, SEQ], BF16)
            kT_d = self.din("kT", [128, SEQ], BF16)
            v_d = self.din("v", [SEQ, 128], BF16)
            xbc_d = self.din("xbcT", [896, SEQ], BF16)
            sz_d = self.din("sz", [SEQ, 384], BF16)
            dtda_d = self.din("dtda", [SEQ, 12], F32)
            ident_d = self.din("ident", [128, 128], F32)
            ctri_d = self.din("c_tri", [128, 4, 128], F32)
            cmb_d = self.din("c_mb", [128, 4, 512], BF16)
            attnT_o = self.dout("attnT", [128, SEQ], BF16)
            yssmT_o = self.dout("yssmT", [384, SEQ], BF16)

        ident_f = self.sb([128, 128], F32, "ident_f"); B_idf = Buf("ident_f")
        ident_b = self.sb([128, 128], BF16, "ident_b"); B_idb = Buf("ident_b")
        ctri = self.sb([128, 4, 128], F32, "ctri"); B_ctri = Buf("ctri")
        ctri_b = self.sb([128, 2, 128], BF16, "ctri_b"); B_ctrib = Buf("ctri_b")
        ones_f = self.sb([128, 128], F32, "ones_f"); B_ones = Buf("ones_f")
        S.dma("sp", ident_f, ident_d, [], [B_idf], B_idf)
        S.dma("sp", ctri, ctri_d, [], [B_ctri], B_ctri)
        S.op("dve", lambda e: e.tensor_copy(out=ident_b, in_=ident_f), [B_idf], [B_idb])
        S.op("dve", lambda e: e.tensor_copy(out=ctri_b, in_=ctri[:, 0:2, :]), [B_ctri], [B_ctrib])
        S.op("pool", lambda e: e.memset(ones_f, 1.0), [], [B_ones])
        NTi, NU = ctri_b[:, 0, :], ctri_b[:, 1, :]
        SL, UI = ctri[:, 2, :], ctri[:, 3, :]
        banks = [self.nc.alloc_psum_tensor(f"{self.pfx}bank{i}", [128, 512], F32).ap() for i in range(8)]
        B_bank = [Buf(f"bank{i}") for i in range(8)]

        if self.do_attn:
            mb_b = self.sb([128, 4, 512], BF16, "mb_b"); B_mbb = Buf("mb_b")
            S.dma("sp", mb_b, cmb_d, [], [B_mbb], B_mbb)
            kT = self.sb([128, SEQ], BF16, "kT_sb"); B_kT = Buf("kT_sb")
            qT = self.sb([128, SEQ], BF16, "qT_sb"); B_qT = Buf("qT_sb")
            vs = self.sb([128, NBLK, 128], BF16, "v_sb"); B_vs = Buf("v_sb")
            nch = max(1, SEQ // 4096)
            cw = SEQ // nch
            for i in range(nch if not fz else 0):
                S.dma("sp", kT[:, i * cw:(i + 1) * cw], kT_d[:, i * cw:(i + 1) * cw], [], [B_kT], B_kT)
                S.dma("sp", qT[:, i * cw:(i + 1) * cw], qT_d[:, i * cw:(i + 1) * cw], [], [B_qT], B_qT)
                bw = NBLK // nch
                S.dma("sp", vs[:, i * bw:(i + 1) * bw, :],
                      v_d.rearrange("(j p) d -> p j d", p=128)[:, i * bw:(i + 1) * bw, :], [], [B_vs], B_vs)
            if fz:
                self.prephase(locals())
            NS = 2
            zb = [[(banks[4 * c + i], B_bank[4 * c + i]) for i in range(2)] for c in range(NS)]
            Pb = [(banks[4 * c + 2], B_bank[4 * c + 2]) for c in range(NS)]
            Ob = [(banks[4 * c + 3], B_bank[4 * c + 3]) for c in range(NS)]
            eb = [[(self.sb([128, 512], F32, f"e{c}{i}"), Buf(f"e{c}{i}")) for i in range(2)] for c in range(NS)]
            spb = [[(self.sb([128, 512], BF16, f"sp{c}{i}"), Buf(f"sp{c}{i}")) for i in range(2)] for c in range(NS)]
            E2b = [(self.sb([128, 512], F32, f"E2{c}"), Buf(f"E2{c}")) for c in range(NS)]
            Wb = [[(self.sb([128, 512], BF16, f"W{c}{i}"), Buf(f"W{c}{i}")) for i in range(2)] for c in range(NS)]
            R_ao = Ring([(self.sb([128, 512], BF16, f"ao{i}"), Buf(f"ao{i}")) for i in range(2)])
            groups = list(range(NGRP - 1, -1, -1))
            steps = [[], []]
            tot = [0] * NS
            for idx, G in enumerate(groups):
                c = tot.index(min(tot))
                tot[c] += 4 * G + 4
                for j in range(4 * G + 3, -1, -1):
                    steps[c].append((G, j, j == 4 * G + 3, j == 0))
            n_iter = max(len(s) for s in steps)

            def S1(c, t):
                G, j, first, last = steps[c][t]
                z, zbuf = zb[c][t % 2]
                diag = j >= 4 * G
                S.op("pe", lambda e: e.matmul(z, lhsT=kT[:, j * 128:(j + 1) * 128], rhs=qT[:, G * 512:(G + 1) * 512],
                                              start=True, stop=not diag), [B_kT, B_qT], [zbuf])
                if diag:
                    r = j - 4 * G
                    S.op("pe", lambda e: e.matmul(z, lhsT=ident_b, rhs=mb_b[:, r, :], start=False, stop=True),
                         [B_idb, B_mbb], [zbuf])

            def S2(c, t):
                z, zbuf = zb[c][t % 2]
                ee, ebuf = eb[c][t % 2]
                S.op("act", lambda e: e.activation(out=ee, in_=z, func=AF.Exp), [zbuf], [ebuf])

            def S3(c, t):
                ee, ebuf = eb[c][t % 2]
                sp, spbuf = spb[c][t % 2]
                S.op("act", lambda e: e.activation(out=sp, in_=ee, func=AF.Ln, bias=1.0), [ebuf], [spbuf])

            def S4(c, t):
                G, j, first, last = steps[c][t]
                P, Pbuf = Pb[c]
                sp, spbuf = spb[c][t % 2]
                if first:
                    S.op("pe", lambda e: e.matmul(P, lhsT=NTi, rhs=sp, start=True, stop=True),
                         [B_ctrib, spbuf], [Pbuf])
                else:
                    spp, sppbuf = spb[c][(t - 1) % 2]
                    S.op("pe", lambda e: e.matmul(P, lhsT=NU, rhs=spp, start=False, stop=False),
                         [B_ctrib, sppbuf], [Pbuf])
                    S.op("pe", lambda e: e.matmul(P, lhsT=NTi, rhs=sp, start=False, stop=True),
                         [B_ctrib, spbuf], [Pbuf])

            def S5(c, t):
                P, Pbuf = Pb[c]
                E2, E2buf = E2b[c]
                S.op("act", lambda e: e.activation(out=E2, in_=P, func=AF.Exp), [Pbuf], [E2buf])

            def S6(c, t):
                ee, ebuf = eb[c][t % 2]
                E2, E2buf = E2b[c]
                W, Wbuf = Wb[c][t % 2]
                S.op("dve", lambda e: e.tensor_tensor(out=W, in0=ee, in1=E2, op=ALU.mult), [ebuf, E2buf], [Wbuf])

            def S7(c, t):
                G, j, first, last = steps[c][t]
                W, Wbuf = Wb[c][t % 2]
                O, Obuf = Ob[c]
                S.op("pe", lambda e: e.matmul(O, lhsT=vs[:, j, :], rhs=W, start=first, stop=last),
                     [B_vs, Wbuf], [Obuf])
                if last:
                    ao, aobuf = R_ao.next()
                    S.op("dve", lambda e: e.tensor_copy(out=ao, in_=O), [Obuf], [aobuf])
                    if fz:
                        B_as, B_ar = fz["B_asend"], fz["B_arecv"]
                        S.dma("pool", fz["asend"][G], ao, [aobuf], [B_as], aobuf)
                        S.coll(fz["asend"][G], fz["arecv"][G].rearrange("r p t -> (r p) t"), fz["groups"],
                               [B_as], [B_ar], B_ar)
                    else:
                        S.dma("pool", attnT_o[:, G * 512:(G + 1) * 512], ao, [aobuf], [B_dram_out], aobuf)

            act = lambda c, t: 0 <= t < len(steps[c])
            for c in range(NS):
                if act(c, 0):
                    S1(c, 0)
            for t in range(n_iter + 1):
                for c in range(NS):
                    if act(c, t + 1):
                        S1(c, t + 1)
                for c in range(NS):
                    if act(c, t):
                        S2(c, t)
                for c in range(NS):
                    if act(c, t):
                        S3(c, t)
                for c in range(NS):
                    if act(c, t - 1):
                        S7(c, t - 1)
                for c in range(NS):
                    if act(c, t):
                        S4(c, t)
                for c in range(NS):
                    if act(c, t):
                        S5(c, t)
                for c in range(NS):
                    if act(c, t):
                        S6(c, t)

        if self.do_ssd:
            NTL = SEQ // 512
            cw_f = self.sb([128, 7, 4], F32, "cw_f"); B_cw = Buf("cw_f")
            cb_f = self.sb([128, 7], F32, "cb_f"); B_cb = Buf("cb_f")
            S.dma("sp", cw_f, convw_d, [], [B_cw], B_cw)
            S.dma("sp", cb_f, convb_d, [], [B_cb], B_cb)
            dg = self.sb([128, 7, 4, 128], BF16, "dg"); B_dg = Buf("dg")
            for c in range(7):
                for k in range(4):
                    S.op("dve", lambda e, c=c, k=k: e.tensor_scalar(out=dg[:, c, k, :], in0=ident_f,
                                                                    scalar1=cw_f[:, c, k:k + 1], scalar2=None,
                                                                    op0=ALU.mult), [B_idf, B_cw], [B_dg])
            dsk6 = self.sb([128, 6], F32, "dsk6"); B_dsk6 = Buf("dsk6")
            dsk = self.sb([128, 6, 64], F32, "dsk"); B_dsk = Buf("dsk")
            ssmn = self.sb([128, 384], F32, "ssmn"); B_ssmn = Buf("ssmn")
            S.dma("sp", dsk6, dskip_d.partition_broadcast(128), [], [B_dsk6], B_dsk6)
            S.dma("sp", ssmn, ssmn_d.partition_broadcast(128), [], [B_ssmn], B_ssmn)
            S.op("dve", lambda e: e.tensor_copy(out=dsk, in_=dsk6.unsqueeze(2).to_broadcast([128, 6, 64])),
                 [B_dsk6], [B_dsk])
            neghalf = self.sb([128, 2], F32, "neghalf"); B_nh = Buf("neghalf")
            S.op("pool", lambda e: e.memset(neghalf, -0.5), [], [B_nh])
            R_xin = Ring([(self.sb([128, 7, 515], BF16, f"xin{i}"), Buf(f"xin{i}")) for i in range(2)])
            R_cT = Ring([(self.sb([128, 7, 512], BF16, f"cT{i}"), Buf(f"cT{i}")) for i in range(2)])
            R_sz = Ring([(self.sb([128, 4, 384], BF16, f"szt{i}"), Buf(f"szt{i}")) for i in range(2)])
            R_dd = Ring([(self.sb([128, 4, 12], F32, f"ddt{i}"), Buf(f"ddt{i}")) for i in range(2)])
            R_yT = Ring([(self.sb([128, 3, 512], BF16, f"yTst{i}"), Buf(f"yTst{i}")) for i in range(2)])
            Sf = self.sb([128, 6, 64], F32, "Sf"); B_Sf = Buf("Sf")
            Sb = self.sb([128, 6, 64], BF16, "Sb"); B_Sb = Buf("Sb")
            S.op("pool", lambda e: e.memset(Sf, 0.0), [], [B_Sf])
            S.op("pool", lambda e: e.memset(Sb, 0.0), [], [B_Sb])
            mk = lambda shape, dt, nm: (self.sb(shape, dt, nm), Buf(nm))
            R_tok = Ring([mk([128, 640], BF16, f"tok{i}") for i in range(2)])
            R_E18 = Ring([mk([128, 18], F32, f"E18{i}") for i in range(2)])
            R_rhsD = Ring([mk([128, 6, 128], F32, f"rhsD{i}") for i in range(1)])
            R_L = Ring([mk([128, 6, 128], F32, f"L{i}") for i in range(1)])
            R_CBm = Ring([mk([128, 2, 128], F32, f"CBm{i}") for i in range(2)])
            R_M = Ring([mk([128, 6, 128], BF16, f"M{i}") for i in range(2)])
            R_xdt = Ring([mk([128, 6, 64], BF16, f"xdt{i}") for i in range(2)])
            R_xw = Ring([mk([128, 6, 64], BF16, f"xw{i}") for i in range(2)])
            R_y = Ring([mk([128, 384], F32, f"yy{i}") for i in range(6)])
            R_st = Ring([mk([128, 8], F32, f"rst{i}") for i in range(2)])
            R_yn = Ring([mk([128, 384], BF16, f"yn{i}") for i in range(2)])
            junk = self.sb([128, 192], F32, "junk_s"); B_junk = Buf("junk_s")
            pc, B_pc = banks[0], B_bank[0]
            ptr, B_ptr = banks[1].bitcast(BF16), B_bank[1]
            pm, B_pm = banks[2], B_bank[2]
            pD = [(banks[3], B_bank[3]), (banks[4], B_bank[4])]
            pY, B_pY = banks[5], B_bank[5]
            pI, B_pI = banks[6], B_bank[6]
            pSt, B_pSt = banks[7], B_bank[7]

            for tt in range(NTL):
                t0 = tt * 512
                xin, B_xin = R_xin.next()
                if tt == 0:
                    S.op("pool", lambda e, xin=xin: e.memset(xin[:, :, 0:3], 0.0), [], [B_xin])
                    S.dma("sp", xin[:, :, 3:515], xbc_d.rearrange("(c p) t -> p c t", p=128)[:, :, 0:512],
                          [B_scr["xbc"][0]], [B_xin], B_xin)
                else:
                    S.dma("sp", xin, xbc_d.rearrange("(c p) t -> p c t", p=128)[:, :, t0 - 3:t0 + 512],
                          [B_scr["xbc"][tt - 1], B_scr["xbc"][tt]], [B_xin], B_xin)
                szt, B_szt = R_sz.next()
                ddt, B_ddt = R_dd.next()
                S.dma("sp", szt, sz_d[t0:t0 + 512, :].rearrange("(b p) n -> p b n", p=128), [B_scr["sz"][tt]],
                      [B_szt], B_szt)
                S.dma("sp", ddt, dtda_d[t0:t0 + 512, :].rearrange("(b p) n -> p b n", p=128), [B_scr["dtda"][tt]],
                      [B_ddt], B_ddt)
                cT, B_cT = R_cT.next()
                for c in range(7):
                    for k in range(4):
                        S.op("pe", lambda e, c=c, k=k, xin=xin: e.matmul(pc, lhsT=dg[:, c, k, :], rhs=xin[:, c, k:k + 512],
                                                                         start=(k == 0), stop=(k == 3)),
                             [B_dg, B_xin], [B_pc])
                    S.op("act", lambda e, c=c, cT=cT: e.activation(out=cT[:, c, :], in_=pc, func=AF.Silu,
                                                                   bias=cb_f[:, c:c + 1]), [B_pc, B_cb], [B_cT])
                yT, B_yT = R_yT.next()
                for ch in range(4):
                    cs = slice(ch * 128, (ch + 1) * 128)
                    dt6 = ddt[:, ch, 0:6]
                    dA6 = ddt[:, ch, 6:12]
                    for i in range(5):
                        S.op("pe", lambda e, i=i, cT=cT, cs=cs: e.transpose(ptr[:, i * 128:(i + 1) * 128], cT[:, i, cs], ident_b),
                             [B_cT, B_idb], [B_ptr])
                    tok, B_tok = R_tok.next()
                    S.op("dve", lambda e, tok=tok: e.tensor_copy(out=tok, in_=ptr[:, 0:640]), [B_ptr], [B_tok])
                    S.op("pe", lambda e, dA6=dA6: e.matmul(pm[:, 256:262], lhsT=UI, rhs=dA6, start=True, stop=True),
                         [B_ctri, B_ddt], [B_pm])
                    S.op("pe", lambda e, dA6=dA6: e.matmul(pm[:, 262:268], lhsT=ones_f, rhs=dA6, start=True, stop=True),
                         [B_ones, B_ddt], [B_pm])
                    S.op("pe", lambda e, dA6=dA6: e.matmul(pm[:, 268:274], lhsT=SL, rhs=dA6, start=True, stop=True),
                         [B_ctri, B_ddt], [B_pm])
                    for g in range(2):
                        S.op("pe", lambda e, g=g, cT=cT, cs=cs: e.matmul(pm[:, g * 128:(g + 1) * 128], lhsT=cT[:, 3 + g, cs],
                                                                         rhs=cT[:, 5 + g, cs], start=True, stop=True),
                             [B_cT], [B_pm])
                    E18, B_E18 = R_E18.next()
                    S.op("act", lambda e, E18=E18: e.activation(out=E18, in_=pm[:, 256:274], func=AF.Exp), [B_pm], [B_E18])
                    CBm, B_CBm = R_CBm.next()
                    S.op("dve", lambda e, CBm=CBm: e.tensor_tensor(
                        out=CBm, in0=pm[:, 0:256].rearrange("p (g q) -> p g q", g=2),
                        in1=UI.unsqueeze(1).to_broadcast([128, 2, 128]), op=ALU.mult), [B_pm, B_ctri], [B_CBm])
                    rhsD, B_rhsD = R_rhsD.next()
                    S.op("dve", lambda e, rhsD=rhsD, dA6=dA6: e.tensor_tensor(
                        out=rhsD, in0=UI.unsqueeze(1).to_broadcast([128, 6, 128]),
                        in1=dA6.unsqueeze(2).to_broadcast([128, 6, 128]), op=ALU.mult), [B_ctri, B_ddt], [B_rhsD])
                    L, B_L = R_L.next()
                    for g in range(2):
                        pDg, B_pDg = pD[g]
                        S.op("pe", lambda e, g=g, pDg=pDg, rhsD=rhsD: e.matmul(
                            pDg[:, 0:384], lhsT=SL, rhs=rhsD[:, 3 * g:3 * g + 3, :].rearrange("p h q -> p (h q)"),
                            start=True, stop=True), [B_ctri, B_rhsD], [B_pDg])
                        S.op("act", lambda e, g=g, pDg=pDg, L=L: e.activation(
                            out=L[:, 3 * g:3 * g + 3, :].rearrange("p h q -> p (h q)"), in_=pDg[:, 0:384], func=AF.Exp),
                            [B_pDg], [B_L])
                    M, B_M = R_M.next()
                    for g in range(2):
                        S.op("dve", lambda e, g=g, M=M, L=L, CBm=CBm: e.tensor_tensor(
                            out=M[:, 3 * g:3 * g + 3, :], in0=L[:, 3 * g:3 * g + 3, :],
                            in1=CBm[:, g, :].unsqueeze(1).to_broadcast([128, 3, 128]), op=ALU.mult),
                            [B_L, B_CBm], [B_M])
                    xdt, B_xdt = R_xdt.next()
                    S.op("dve", lambda e, xdt=xdt, tok=tok, dt6=dt6: e.tensor_tensor(
                        out=xdt, in0=tok[:, 0:384].rearrange("p (h d) -> p h d", h=6),
                        in1=dt6.unsqueeze(2).to_broadcast([128, 6, 64]), op=ALU.mult), [B_tok, B_ddt], [B_xdt])
                    for hh in range(6):
                        S.op("pe", lambda e, hh=hh, M=M, xdt=xdt: e.matmul(pY[:, hh * 64:(hh + 1) * 64], lhsT=M[:, hh, :],
                                                                           rhs=xdt[:, hh, :], start=True, stop=True),
                             [B_M, B_xdt], [B_pY])
                    for g in range(2):
                        S.op("pe", lambda e, g=g, cT=cT, cs=cs: e.matmul(
                            pI[:, g * 192:(g + 1) * 192], lhsT=cT[:, 5 + g, cs],
                            rhs=Sb[:, 3 * g:3 * g + 3, :].rearrange("p h d -> p (h d)"), start=True, stop=True),
                            [B_cT, B_Sb], [B_pI])
                    y1, B_y1 = R_y.next()
                    S.op("dve", lambda e, y1=y1, E18=E18: e.tensor_tensor(
                        out=y1.rearrange("p (h d) -> p h d", h=6), in0=pI[:, 0:384].rearrange("p (h d) -> p h d", h=6),
                        in1=E18[:, 0:6].unsqueeze(2).to_broadcast([128, 6, 64]), op=ALU.mult), [B_pI, B_E18], [B_y1])
                    y2, B_y2 = R_y.next()
                    S.op("dve", lambda e, y1=y1, y2=y2: e.tensor_tensor(out=y2, in0=pY[:, 0:384], in1=y1, op=ALU.add),
                         [B_pY, B_y1], [B_y2])
                    t3, B_t3 = R_y.next()
                    S.op("pool", lambda e, t3=t3, tok=tok: e.tensor_tensor(
                        out=t3, in0=tok[:, 0:384], in1=dsk.rearrange("p h d -> p (h d)"), op=ALU.mult),
                        [B_tok, B_dsk], [B_t3])
                    y3, B_y3 = R_y.next()
                    S.op("pool", lambda e, y3=y3, y2=y2, t3=t3: e.tensor_tensor(out=y3, in0=y2, in1=t3, op=ALU.add),
                         [B_y2, B_t3], [B_y3])
                    y4, B_y4 = R_y.next()
                    S.op("dve", lambda e, y4=y4, y3=y3, szt=szt, ch=ch: e.tensor_tensor(out=y4, in0=y3, in1=szt[:, ch, :],
                                                                                      op=ALU.mult), [B_y3, B_szt], [B_y4])
                    rst, B_rst = R_st.next()
                    for g in range(2):
                        S.op("act", lambda e, g=g, y4=y4, rst=rst: e.activation(
                            out=junk, in_=y4[:, g * 192:(g + 1) * 192], func=AF.Square, accum_out=rst[:, g:g + 1]),
                            [B_y4], [B_junk, B_rst])
                    S.op("dve", lambda e, rst=rst: e.tensor_scalar(out=rst[:, 2:4], in0=rst[:, 0:2], scalar1=1.0 / 192,
                                                                   scalar2=EPS, op0=ALU.mult, op1=ALU.add),
                         [B_rst], [B_rst])
                    S.op("pool", lambda e, rst=rst: e.tensor_tensor(out=rst[:, 4:6], in0=rst[:, 2:4], in1=neghalf,
                                                                    op=ALU.pow), [B_rst, B_nh], [B_rst])
                    yn, B_yn = R_yn.next()
                    for g in range(2):
                        S.op("dve", lambda e, g=g, yn=yn, y4=y4, rst=rst: e.scalar_tensor_tensor(
                            out=yn[:, g * 192:(g + 1) * 192], in0=y4[:, g * 192:(g + 1) * 192],
                            scalar=rst[:, 4 + g:5 + g], in1=ssmn[:, g * 192:(g + 1) * 192], op0=ALU.mult, op1=ALU.mult),
                            [B_y4, B_rst, B_ssmn], [B_yn])
                    for i in range(3):
                        S.op("pe", lambda e, i=i, yn=yn: e.transpose(ptr[:, 640 + i * 128:640 + (i + 1) * 128],
                                                                     yn[:, i * 128:(i + 1) * 128], ident_b),
                             [B_yn, B_idb], [B_ptr])
                    S.op("dve", lambda e, yT=yT, cs=cs: e.tensor_copy(
                        out=yT[:, :, cs], in_=ptr[:, 640:1024].rearrange("p (c t) -> p c t", c=3)), [B_ptr], [B_yT])
                    xw, B_xw = R_xw.next()
                    S.op("dve", lambda e, xw=xw, xdt=xdt, E18=E18: e.tensor_tensor(
                        out=xw, in0=xdt, in1=E18[:, 12:18].unsqueeze(2).to_broadcast([128, 6, 64]), op=ALU.mult),
                        [B_xdt, B_E18], [B_xw])
                    for g in range(2):
                        S.op("pe", lambda e, g=g, tok=tok, xw=xw: e.matmul(
                            pSt[:, g * 192:(g + 1) * 192], lhsT=tok[:, 384 + g * 128:384 + (g + 1) * 128],
                            rhs=xw[:, 3 * g:3 * g + 3, :].rearrange("p h d -> p (h d)"), start=True, stop=True),
                            [B_tok, B_xw], [B_pSt])
                    S.op("dve", lambda e, E18=E18: e.tensor_tensor(
                        out=Sf, in0=Sf, in1=E18[:, 6:12].unsqueeze(2).to_broadcast([128, 6, 64]), op=ALU.mult),
                        [B_Sf, B_E18], [B_Sf])
                    S.op("dve", lambda e: e.tensor_tensor(out=Sf.rearrange("p h d -> p (h d)"),
                                                          in0=Sf.rearrange("p h d -> p (h d)"), in1=pSt[:, 0:384],
                                                          op=ALU.add), [B_Sf, B_pSt], [B_Sf])
                    S.op("act", lambda e: e.copy(out=Sb, in_=Sf), [B_Sf], [B_Sb])
                if fz:
                    B_ys, B_yr = fz["B_ysend"], fz["B_yrecv"]
                    S.dma("pool", fz["ysend"][tt // 2, tt % 2].rearrange("(c p) t -> p c t", p=128), yT, [B_yT], [B_ys],
                          B_yT)
                    if tt % 2 == 1:
                        S.coll(fz["ysend"][tt // 2].rearrange("j p t -> (j p) t"),
                               fz["yrecv"][tt // 2].rearrange("r j p t -> (r j p) t"), fz["groups"],
                               [B_ys], [B_yr], B_yr)
                else:
                    S.dma("pool", yssmT_o.rearrange("(c p) t -> p c t", p=128)[:, :, t0:t0 + 512], yT, [B_yT],
                          [B_dram_out], B_yT)
        S.emit(barrier=bool(fz))

    def prephase(self, L):
        S, fz, SEQ = self.S, self.fz, self.SEQ
        qT, kT, vs = L["qT"], L["kT"], L["vs"]
        B_qT, B_kT, B_vs = L["B_qT"], L["B_kT"], L["B_vs"]
        banks, B_bank, B_scr = L["banks"], L["B_bank"], L["B_scr"]
        xbc_d, sz_d, dtda_d = L["xbc_d"], L["sz_d"], L["dtda_d"]
        NTLG = SEQ // 512
        ntl = NTLG // 4
        wscr = self.dscr("w_in_r_bf", [128, 8, 1670], BF16)
        B_wscr = Buf("w_in_r_bf")
        src = L["winr_d"].rearrange("(kc p) n -> kc p n", p=128)
        for kc in range(8):
            S.dma("pool", wscr[:, kc, :], src[kc], [], [B_wscr], B_wscr, max_dma_last_dim=4096)
        wsl = self.sb([128, 8, 1670], BF16, "wsl"); B_wsl = Buf("wsl")
        S.dma("sp", wsl, wscr, [B_wscr], [B_wsl], B_wsl)
        dtb = self.sb([128, 4, 6], F32, "dtb"); B_dtb = Buf("dtb")
        nega = self.sb([128, 4, 6], F32, "nega"); B_nega = Buf("nega")
        for b in range(4):
            S.dma("sp", dtb[:, b, :], L["dtbr_d"].partition_broadcast(128), [], [B_dtb], B_dtb)
            S.dma("sp", nega[:, b, :], L["alogr_d"].partition_broadcast(128), [], [B_nega], B_nega)
        S.op("act", lambda e: e.activation(out=nega, in_=nega, func=AF.Exp), [B_nega], [B_nega])
        S.op("dve", lambda e: e.tensor_scalar(out=nega, in0=nega, scalar1=-1.0, scalar2=None, op0=ALU.mult),
             [B_nega], [B_nega])
        R_h = Ring([(self.sb([128, 8, 512], BF16, f"hTt{i}"), Buf(f"hTt{i}")) for i in range(2)])
        R_xs = Ring([(self.sb([128, 7, 512], BF16, f"xst{i}"), Buf(f"xst{i}")) for i in range(2)])
        R_zs = Ring([(self.sb([128, 4, 384], BF16, f"zst{i}"), Buf(f"zst{i}")) for i in range(2)])
        R_ds = Ring([(self.sb([128, 4, 12], F32, f"dst{i}"), Buf(f"dst{i}")) for i in range(2)])
        dts = self.sb([128, 4, 24], F32, "dts_p"); B_dts = Buf("dts_p")
        R_pf = Ring([(banks[i], B_bank[i]) for i in range(4)])
        pv, B_pv = banks[4], B_bank[4]
        R_pz = Ring([(banks[5], B_bank[5]), (banks[6], B_bank[6])])
        pS, B_pS = banks[7], B_bank[7]
        for g in range(NTLG):
            rank, lt = g // ntl, g % ntl
            hTt, B_h = R_h.next()
            S.dma("sp", hTt, fz["hrecv"][lt][rank], [], [B_h], B_h)
            gs = slice(g * 512, (g + 1) * 512)

            def fmm(c0, hTt=hTt, B_h=B_h):
                p, pb = R_pf.next()
                for kc in range(8):
                    S.op("pe", lambda e, kc=kc, p=p: e.matmul(p, lhsT=wsl[:, kc, c0:c0 + 128], rhs=hTt[:, kc, :],
                                                              start=(kc == 0), stop=(kc == 7)), [B_wsl, B_h], [pb])
                return p, pb
            p, pb = fmm(0)
            S.op("dve", lambda e, p=p, gs=gs: e.tensor_scalar(out=qT[:, gs], in0=p, scalar1=128.0 ** -0.5, scalar2=None,
                                                              op0=ALU.mult), [pb], [B_qT])
            p, pb = fmm(128)
            S.op("act", lambda e, p=p, gs=gs: e.copy(out=kT[:, gs], in_=p), [pb], [B_kT])
            xst, B_xst = R_xs.next()
            for c in range(7):
                p, pb = fmm(256 + c * 128)
                if c % 2 == 0:
                    S.op("dve", lambda e, p=p, c=c, xst=xst: e.tensor_copy(out=xst[:, c, :], in_=p), [pb], [B_xst])
                else:
                    S.op("act", lambda e, p=p, c=c, xst=xst: e.copy(out=xst[:, c, :], in_=p), [pb], [B_xst])
            S.dma("pool", xbc_d.rearrange("(c p) t -> p c t", p=128)[:, :, gs], xst, [B_xst], [B_scr["xbc"][g]], B_xst)
            for b in range(4):
                for kc in range(8):
                    S.op("pe", lambda e, kc=kc, b=b, hTt=hTt: e.matmul(
                        pv[:, b * 128:(b + 1) * 128], lhsT=hTt[:, kc, b * 128:(b + 1) * 128], rhs=wsl[:, kc, 1152:1280],
                        start=(kc == 0), stop=(kc == 7)), [B_wsl, B_h], [B_pv])
            S.op("dve", lambda e, g=g: e.tensor_copy(out=vs[:, g * 4:(g + 1) * 4, :],
                                                     in_=pv.rearrange("p (b d) -> p b d", b=4)), [B_pv], [B_vs])
            zst, B_zst = R_zs.next()
            for b in range(4):
                pz, B_pz = R_pz.next()
                for kc in range(8):
                    S.op("pe", lambda e, kc=kc, b=b, hTt=hTt, pz=pz: e.matmul(
                        pz[:, 0:384], lhsT=hTt[:, kc, b * 128:(b + 1) * 128], rhs=wsl[:, kc, 1280:1664],
                        start=(kc == 0), stop=(kc == 7)), [B_wsl, B_h], [B_pz])
                S.op("act", lambda e, b=b, pz=pz, zst=zst: e.activation(out=zst[:, b, :], in_=pz[:, 0:384], func=AF.Silu),
                     [B_pz], [B_zst])
            S.dma("pool", sz_d[g * 512:(g + 1) * 512, :].rearrange("(b p) n -> p b n", p=128), zst, [B_zst],
                  [B_scr["sz"][g]], B_zst)
            for b in range(4):
                for kc in range(8):
                    S.op("pe", lambda e, kc=kc, b=b, hTt=hTt: e.matmul(
                        pS[:, b * 6:(b + 1) * 6], lhsT=hTt[:, kc, b * 128:(b + 1) * 128], rhs=wsl[:, kc, 1664:1670],
                        start=(kc == 0), stop=(kc == 7)), [B_wsl, B_h], [B_pS])
            dst, B_dst = R_ds.next()
            dtb_f = dtb.rearrange("p b n -> p (b n)")
            S.op("dve", lambda e: e.tensor_tensor(out=dts[:, 0, :], in0=pS[:, 0:24], in1=dtb_f, op=ALU.add),
                 [B_pS, B_dtb], [B_dts])
            S.op("act", lambda e: e.activation(out=dts[:, 1, :], in_=dts[:, 0, :], func=AF.Abs), [B_dts], [B_dts])
            S.op("act", lambda e: e.activation(out=dts[:, 2, :], in_=dts[:, 1, :], func=AF.Exp, scale=-1.0),
                 [B_dts], [B_dts])
            S.op("act", lambda e: e.activation(out=dts[:, 3, :], in_=dts[:, 2, :], func=AF.Ln, bias=1.0),
                 [B_dts], [B_dts])
            S.op("dve", lambda e, dst=dst: e.scalar_tensor_tensor(
                out=dst[:, :, 0:6], in0=dts[:, 0, :].rearrange("p (b n) -> p b n", b=4), scalar=0.0,
                in1=dts[:, 3, :].rearrange("p (b n) -> p b n", b=4), op0=ALU.max, op1=ALU.add), [B_dts], [B_dst])
            S.op("dve", lambda e, dst=dst: e.tensor_tensor(out=dst[:, :, 6:12], in0=dst[:, :, 0:6], in1=nega,
                                                           op=ALU.mult), [B_dst, B_nega], [B_dst])
            S.dma("pool", dtda_d[g * 512:(g + 1) * 512, :].rearrange("(b p) n -> p b n", p=128), dst, [B_dst],
                  [B_scr["dtda"][g]], B_dst)


class FusedProg:
    def __init__(self, NT, SEQ, nph=99):
        nc = self.nc = bass.Bass("TRN2", target_bir_lowering=False)
        ntl = NT // T
        NGRP = SEQ // 512
        groups = [[0, 1, 2, 3], [4, 5, 6, 7]]
        ext = lambda n, s, d: nc.dram_tensor(n, list(s), d, kind="ExternalInput").ap()
        itn = lambda n, s, d: nc.dram_tensor(n, list(s), d, kind="Internal").ap()
        ident = ext("ident", [128, 128], F32)
        c_tri = ext("c_tri", [128, 4, 128], F32)
        c_mb = ext("c_mb", [128, 4, 512], BF16)
        L = []
        for l in range(2):
            L.append(dict(
                xres=itn(f"xres{l}", [NT, D], F32), sg=itn(f"sg{l}", [2 * D, NT], BF16),
                hsend=itn(f"hsend{l}", [ntl, 128, 8, T], BF16), hrecv=itn(f"hrecv{l}", [ntl, 4, 128, 8, T], BF16),
                asend=itn(f"asend{l}", [NGRP, 128, 512], BF16), arecv=itn(f"arecv{l}", [NGRP, 4, 128, 512], BF16),
                ysend=itn(f"ysend{l}", [NGRP // 2, 2, 384, 512], BF16),
                yrecv=itn(f"yrecv{l}", [NGRP // 2, 4, 2, 384, 512], BF16)))
        base = dict(ident=ident, c_tri=c_tri, c_mb=c_mb, groups=groups)
        self.n_ops = 0

        self._ph = 0

        def tok(stage, pfx, **kw):
            self._ph += 1
            if self._ph > nph:
                return
            fz = dict(base); fz.update(kw)
            with nc.cleanup_on_exit():
                p = TokProg(stage, NT, nc=nc, fz=fz, pfx=pfx)
            self.n_ops += len(p.S.ops)

        def mix(l, pa, ps):
            fz = dict(base)
            fz.update(hrecv=L[l]["hrecv"], asend=L[l]["asend"], arecv=L[l]["arecv"], ysend=L[l]["ysend"],
                      yrecv=L[l]["yrecv"], B_asend=Buf("asend"), B_arecv=Buf("arecv"), B_ysend=Buf("ysend"),
                      B_yrecv=Buf("yrecv"))
            self._ph += 1
            if self._ph <= nph:
                with nc.cleanup_on_exit():
                    p = MixProg(SEQ, True, False, nc=nc, fz=fz, pfx=pa)
                self.n_ops += len(p.S.ops)
            self._ph += 1
            if self._ph <= nph:
                with nc.cleanup_on_exit():
                    p = MixProg(SEQ, False, True, nc=nc, fz=fz, pfx=ps)
                self.n_ops += len(p.S.ops)

        tok("A0", "p0_", x_src=None, x_dst=L[0]["xres"], sg_dst=L[0]["sg"], hsend=L[0]["hsend"], hrecv=L[0]["hrecv"])
        mix(0, "p1_", "p2_")
        tok("CA", "p3_", x_src=L[0]["xres"], sg_src=L[0]["sg"], arecv=L[0]["arecv"], yrecv=L[0]["yrecv"],
            x_dst=L[1]["xres"], sg_dst=L[1]["sg"], hsend=L[1]["hsend"], hrecv=L[1]["hrecv"])
        mix(1, "p4_", "p5_")
        tok("C1", "p6_", x_src=L[1]["xres"], sg_src=L[1]["sg"], arecv=L[1]["arecv"], yrecv=L[1]["yrecv"])
        if nph < 7:
            dbg = nc.dram_tensor("dbg", [128, 128], F32, kind="ExternalOutput").ap()
            with nc.semaphore("dbgsem") as dsem, nc.Block() as block:
                @block.sync
                def _(e):
                    e.dma_start(out=dbg, in_=ident).then_inc(dsem, 16)
                    e.wait_ge(dsem, 16)


_PROGS = {}


def _get_prog(kind, *args):
    key = (kind,) + args
    if key not in _PROGS:
        if kind == "tok":
            _PROGS[key] = TokProg(*args)
        else:
            _PROGS[key] = MixProg(*args)
    return _PROGS[key]


def _gain_t(g):
    return np.ascontiguousarray(np.asarray(g, np.float32).reshape(8, 128).T)


def _c(a, dt=np.float32):
    return np.ascontiguousarray(np.asarray(a, dt))


def run_tok(stage, NT, per_core, shared):
    prog = _get_prog("tok", stage, NT)
    ident = np.eye(128, dtype=np.float32)
    in_maps = []
    for c in range(NCORES):
        m = dict(shared)
        m.update(per_core[c])
        m["ident"] = ident
        in_maps.append(m)
    res = run_bass_kernel_spmd(prog.nc, in_maps, core_ids=list(range(NCORES)))
    return res.results


def _mix_consts():
    k = np.arange(128)[:, None]
    l = np.arange(128)[None, :]
    ctri = np.zeros((128, 4, 128), np.float32)
    ctri[:, 0, :] = -1.0 * (k >= l)
    ctri[:, 1, :] = -1.0 * (k < l)
    ctri[:, 2, :] = (k > l)
    ctri[:, 3, :] = (k <= l)
    q = np.arange(512)[None, :]
    cmb = np.zeros((128, 4, 512), np.float32)
    for r in range(4):
        cmb[:, r, :] = np.where(r * 128 + k < q, 0.0, -30000.0)
    return ctri, cmb


def run_mix(SEQ, per_core, do_attn=True, do_ssd=True):
    prog = _get_prog("mix", SEQ, do_attn, do_ssd)
    ctri, cmb = _mix_consts()
    ident = np.eye(128, dtype=np.float32)
    in_maps = []
    for c in range(NCORES):
        m = dict(per_core[c])
        m["ident"] = ident
        m["c_tri"] = ctri
        m["c_mb"] = cmb.astype(NPBF)
        in_maps.append(m)
    res = run_bass_kernel_spmd(prog.nc, in_maps, core_ids=list(range(NCORES)))
    return res.results


def _tok_shared_ffn(tag, p, name, l):
    return {f"f{tag}_gain": _gain_t(p[f"{name}_norm"][l]), f"f{tag}_up": _c(p[f"{name}_w_up"][l]),
            f"f{tag}_dn": _c(p[f"{name}_w_down"][l])}


def _tok_shared_proj(p, l):
    return {"mix_gain": _gain_t(p["mix_norm"][l]), "w_in": _c(p["w_in"][l]),
            "dt_bias": _c(p["dt_bias"][l].reshape(1, NH)), "a_log": _c(p["a_log"][l].reshape(1, NH))}


def _tok_shared_mixout(p, l):
    return {"w_bsb": _c(p["w_branch_sb"][l]), "w_bssm": _c(p["w_branch_ssm"][l]), "w_out": _c(p["w_out"][l])}


def _mix_inputs(tok_res, p, l, B, SEQ):
    cpb = NCORES // B
    per_core = []
    for b in range(B):
        cs = range(b * cpb, (b + 1) * cpb)
        qT = np.concatenate([tok_res[c]["qT"] for c in cs], axis=1)
        kT = np.concatenate([tok_res[c]["kT"] for c in cs], axis=1)
        v = np.concatenate([tok_res[c]["v"] for c in cs], axis=0)
        sz = np.concatenate([tok_res[c]["sz"] for c in cs], axis=0)
        xbcT = np.concatenate([tok_res[c]["xbcT"] for c in cs], axis=1)
        dtda = np.concatenate([tok_res[c]["dtda"] for c in cs], axis=0)
        for r in range(4):
            chs = np.concatenate([np.arange(384 * r, 384 * r + 384), DI + np.arange(256 * r, 256 * r + 256),
                                  DI + 1024 + np.arange(256 * r, 256 * r + 256)])
            cw = np.asarray(p["conv_w"][l], np.float32)[:, chs]
            cb = np.asarray(p["conv_b"][l], np.float32)[chs]
            m = {
                "qT": np.ascontiguousarray(qT[128 * r:128 * (r + 1)]),
                "kT": np.ascontiguousarray(kT[128 * r:128 * (r + 1)]),
                "v": np.ascontiguousarray(v[:, 128 * r:128 * (r + 1)]),
                "xbcT": np.ascontiguousarray(xbcT[chs]),
                "conv_w": _c(cw.T.reshape(7, 128, 4).transpose(1, 0, 2)),
                "conv_b": _c(cb.reshape(7, 128).T),
                "sz": np.ascontiguousarray(sz[:, 384 * r:384 * (r + 1)]),
                "dtda": np.ascontiguousarray(np.concatenate([dtda[:, 6 * r:6 * r + 6],
                                                             dtda[:, NH + 6 * r:NH + 6 * r + 6]], axis=1)),
                "d_skip": _c(p["d_skip"][l][6 * r:6 * r + 6].reshape(1, 6)),
                "ssm_norm": _c(p["ssm_norm"][l][384 * r:384 * r + 384].reshape(1, 384)),
            }
            per_core.append(m)
    return per_core


def _mixout_inputs(mix_res, tok_res, B, NT):
    cpb = NCORES // B
    per_core = []
    for b in range(B):
        attnT = np.concatenate([mix_res[b * 4 + r]["attnT"] for r in range(4)], axis=0)
        yssmT = np.concatenate([mix_res[b * 4 + r]["yssmT"] for r in range(4)], axis=0)
        for i in range(cpb):
            c = b * cpb + i
            per_core.append({
                "x": tok_res[c]["x_out"],
                "attnT": np.ascontiguousarray(attnT[:, i * NT:(i + 1) * NT]),
                "yssmT": np.ascontiguousarray(yssmT[:, i * NT:(i + 1) * NT]),
                "sgT": tok_res[c]["sgT_out"],
            })
    return per_core


def forward(x, p):
    x = np.asarray(x, np.float32)
    B, SEQ, _ = x.shape
    NT = B * SEQ // NCORES
    flat = x.reshape(B * SEQ, D)
    per_core = [{"x": _c(flat[c * NT:(c + 1) * NT])} for c in range(NCORES)]
    shared = {}
    shared.update(_tok_shared_ffn("a", p, "ffn1", 0))
    shared.update(_tok_shared_proj(p, 0))
    tok = run_tok("A0", NT, per_core, shared)
    mix = run_mix(SEQ, _mix_inputs(tok, p, 0, B, SEQ))
    per_core = _mixout_inputs(mix, tok, B, NT)
    shared = {}
    shared.update(_tok_shared_mixout(p, 0))
    shared.update(_tok_shared_ffn("a", p, "ffn2", 0))
    shared.update(_tok_shared_ffn("b", p, "ffn1", 1))
    shared.update(_tok_shared_proj(p, 1))
    tok = run_tok("CA", NT, per_core, shared)
    mix = run_mix(SEQ, _mix_inputs(tok, p, 1, B, SEQ))
    per_core = _mixout_inputs(mix, tok, B, NT)
    shared = {}
    shared.update(_tok_shared_mixout(p, 1))
    shared.update(_tok_shared_ffn("a", p, "ffn2", 1))
    shared["fin_gain"] = _c(np.asarray(p["final_norm"]).reshape(1, D))
    fin = run_tok("C1", NT, per_core, shared)
    y = np.concatenate([np.asarray(fin[c]["y"], np.float32) for c in range(NCORES)], axis=0)
    return y.reshape(B, SEQ, D)


def kernel(x, ffn1_norm, ffn1_w_up, ffn1_w_down, mix_norm, w_in, conv_w, conv_b, dt_bias, a_log, d_skip,
           ssm_norm, w_branch_sb, w_branch_ssm, w_out, ffn2_norm, ffn2_w_up, ffn2_w_down, final_norm):
    p = dict(ffn1_norm=ffn1_norm, ffn1_w_up=ffn1_w_up, ffn1_w_down=ffn1_w_down, mix_norm=mix_norm, w_in=w_in,
             conv_w=conv_w, conv_b=conv_b, dt_bias=dt_bias, a_log=a_log, d_skip=d_skip, ssm_norm=ssm_norm,
             w_branch_sb=w_branch_sb, w_branch_ssm=w_branch_ssm, w_out=w_out, ffn2_norm=ffn2_norm,
             ffn2_w_up=ffn2_w_up, ffn2_w_down=ffn2_w_down, final_norm=final_norm)
    p = {k: np.asarray(v, np.float32) for k, v in p.items()}
    if FUSED:
        return forward_fused(x, p)
    return forward(x, p)


def _w_in_r(w_in, r):
    cols = np.concatenate([
        np.arange(128 * r, 128 * r + 128), 512 + np.arange(128 * r, 128 * r + 128),
        C_XBC + np.arange(384 * r, 384 * r + 384), C_XBC + DI + np.arange(256 * r, 256 * r + 256),
        C_XBC + DI + 1024 + np.arange(256 * r, 256 * r + 256),
        C_V + np.arange(128 * r, 128 * r + 128), C_Z + np.arange(384 * r, 384 * r + 384),
        C_DT + np.arange(6 * r, 6 * r + 6)])
    return np.ascontiguousarray(np.asarray(w_in, np.float32)[:, cols])


def forward_fused(x, p, trace=False):
    x = np.asarray(x, np.float32)
    B, SEQ, _ = x.shape
    NT = B * SEQ // NCORES
    key = ("fused", NT, SEQ)
    if key not in _PROGS:
        _PROGS[key] = FusedProg(NT, SEQ)
    prog = _PROGS[key]
    flat = x.reshape(B * SEQ, D)
    ctri, cmb = _mix_consts()
    sh = {"ident": np.eye(128, dtype=np.float32), "c_tri": ctri, "c_mb": cmb.astype(NPBF)}

    def pf(pfx, d):
        return {pfx + k: v for k, v in d.items()}
    sh.update(pf("p0_", _tok_shared_ffn("a", p, "ffn1", 0)))
    sh.update({"p0_mix_gain": _gain_t(p["mix_norm"][0]), "p0_w_in_g": _c(p["w_in"][0][:, C_GSB:C_GSB + 2 * D])})
    sh.update(pf("p3_", _tok_shared_mixout(p, 0)))
    sh.update(pf("p3_", _tok_shared_ffn("a", p, "ffn2", 0)))
    sh.update(pf("p3_", _tok_shared_ffn("b", p, "ffn1", 1)))
    sh.update({"p3_mix_gain": _gain_t(p["mix_norm"][1]), "p3_w_in_g": _c(p["w_in"][1][:, C_GSB:C_GSB + 2 * D])})
    sh.update(pf("p6_", _tok_shared_mixout(p, 1)))
    sh.update(pf("p6_", _tok_shared_ffn("a", p, "ffn2", 1)))
    sh["p6_fin_gain"] = _c(np.asarray(p["final_norm"]).reshape(1, D))
    per_r = []
    for r in range(4):
        d = {}
        for l, (pa, ps) in enumerate([("p1_", "p2_"), ("p4_", "p5_")]):
            chs = np.concatenate([np.arange(384 * r, 384 * r + 384), DI + np.arange(256 * r, 256 * r + 256),
                                  DI + 1024 + np.arange(256 * r, 256 * r + 256)])
            cw = np.asarray(p["conv_w"][l], np.float32)[:, chs]
            cb = np.asarray(p["conv_b"][l], np.float32)[chs]
            d[pa + "w_in_r"] = _w_in_r(p["w_in"][l], r)
            d[pa + "dt_bias_r"] = _c(p["dt_bias"][l][6 * r:6 * r + 6].reshape(1, 6))
            d[pa + "a_log_r"] = _c(p["a_log"][l][6 * r:6 * r + 6].reshape(1, 6))
            d[ps + "conv_w"] = _c(cw.T.reshape(7, 128, 4).transpose(1, 0, 2))
            d[ps + "conv_b"] = _c(cb.reshape(7, 128).T)
            d[ps + "d_skip"] = _c(p["d_skip"][l][6 * r:6 * r + 6].reshape(1, 6))
            d[ps + "ssm_norm"] = _c(p["ssm_norm"][l][384 * r:384 * r + 384].reshape(1, 384))
        per_r.append(d)
    in_maps = []
    for c in range(NCORES):
        m = dict(sh)
        m.update(per_r[c % 4])
        m["p0_x"] = _c(flat[c * NT:(c + 1) * NT])
        in_maps.append(m)
    res = run_bass_kernel_spmd(prog.nc, in_maps, core_ids=list(range(NCORES)), trace=trace)
    y = np.concatenate([np.asarray(res.results[c]["p6_y"], np.float32) for c in range(NCORES)], axis=0)
    if trace:
        print("exec_time_ns", res.exec_time_ns)
    return y.reshape(B, SEQ, D)
```
